# Optimizing a Trainium2 kernel written in Bass

```python
import math
import jax
import jax.numpy as jnp
from jax import lax
import numpy as np

D_MODEL = 2048
BATCH = 4
SEQ = 4096
DEPTH = 2
DEC_BATCH = 4
DEC_SEQ = 2048
PAST_LEN = 128

HEAD_DIM = 128
D_MIX = D_MODEL
GDN_HEADS = D_MIX // (2 * HEAD_DIM)
GDN_W = GDN_HEADS * HEAD_DIM
ATT_HEADS = (D_MIX - GDN_W) // HEAD_DIM
ATT_KV_HEADS = 2
ATT_GROUPS = ATT_HEADS // ATT_KV_HEADS
ATT_W = ATT_HEADS * HEAD_DIM
KV_W = ATT_KV_HEADS * HEAD_DIM
CONV_K = 5
CHUNK = 64
Q_BLOCK = 128
GRID_W = 64
ROPE_THETA = 10000.0
ROT_HALF = HEAD_DIM // 2
D_FF = 4 * D_MODEL
N_MOD = 6
GDN_IN = 4 * GDN_W + 4 * GDN_HEADS
IN_COLS = GDN_IN + ATT_W + 2 * KV_W
EPS = 1e-6

kernel_name = 'hybrid_gdn_gqa_axial_adaln_encoder'


def rmsnorm(x, w):
    xf = x.astype(jnp.float32)
    y = xf * lax.rsqrt(jnp.mean(xf * xf, axis=-1, keepdims=True) + EPS)
    return y * w.astype(jnp.float32)


def l2norm(x):
    return x * lax.rsqrt(jnp.sum(x * x, axis=-1, keepdims=True) + EPS)


def short_conv(x, w):
    C = x.shape[-1]
    return lax.conv_general_dilated(
        x, w[:, None, :].astype(x.dtype), window_strides=(1,),
        padding=[(CONV_K // 2, CONV_K // 2)],
        dimension_numbers=('NWC', 'WIO', 'NWC'), feature_group_count=C)


def gated_delta_chunked(q, k, v, g, beta):
    B, H, T, DK = q.shape
    DV = v.shape[-1]
    N = T // CHUNK
    q = q * DK ** -0.5

    def chunks(t):
        return t.reshape((B, H, N, CHUNK) + t.shape[3:])

    q, k, v, g, beta = chunks(q), chunks(k), chunks(v), chunks(g), chunks(beta)
    gc = jnp.cumsum(g, axis=-1)
    incl = jnp.tril(jnp.ones((CHUNK, CHUNK), dtype=bool))
    strict = jnp.tril(jnp.ones((CHUNK, CHUNK), dtype=bool), -1)
    decay = jnp.where(incl, jnp.exp(jnp.where(incl, gc[..., :, None] - gc[..., None, :], 0.0)), 0.0)
    k_beta = k * beta[..., None]
    v_beta = v * beta[..., None]
    L = jnp.where(strict, jnp.einsum('bhncd,bhnsd->bhncs', k_beta, k) * decay, 0.0)
    eye = jnp.eye(CHUNK, dtype=q.dtype)
    t_inv = lax.linalg.triangular_solve(L + eye, jnp.broadcast_to(eye, L.shape),
                                        left_side=True, lower=True, unit_diagonal=True)
    u = jnp.einsum('bhncs,bhnse->bhnce', t_inv, v_beta)
    w = jnp.einsum('bhncs,bhnsd->bhncd', t_inv, k_beta * jnp.exp(gc)[..., None])
    a_intra = jnp.where(incl, jnp.einsum('bhncd,bhnsd->bhncs', q, k) * decay, 0.0)
    q_dec = q * jnp.exp(gc)[..., None]
    k_dec = k * jnp.exp(gc[..., -1:] - gc)[..., None]
    g_last = jnp.exp(gc[..., -1])
    xs = (jnp.moveaxis(q_dec, 2, 0), jnp.moveaxis(k_dec, 2, 0), jnp.moveaxis(u, 2, 0),
          jnp.moveaxis(w, 2, 0), jnp.moveaxis(a_intra, 2, 0), jnp.moveaxis(g_last, 2, 0))

    def step(S, inp):
        qd, kd, ui, wi, ai, gl = inp
        v_new = ui - jnp.einsum('bhcd,bhde->bhce', wi, S)
        o = jnp.einsum('bhcd,bhde->bhce', qd, S) + jnp.einsum('bhcs,bhse->bhce', ai, v_new)
        S = S * gl[..., None, None] + jnp.einsum('bhcd,bhce->bhde', kd, v_new)
        return S, o

    S0 = jnp.zeros((B, H, DK, DV), q.dtype)
    _, o = lax.scan(step, S0, xs)
    return jnp.moveaxis(o, 0, 2).reshape(B, H, T, DV)


def gdn_group(p, conv_w, a_log, dt_bias, norm_w):
    B, T, _ = p.shape
    qkv = jax.nn.silu(short_conv(p[..., :3 * GDN_W], conv_w)).astype(jnp.float32)
    q, k, v = jnp.split(qkv, 3, axis=-1)

    def heads(t):
        return t.reshape(B, T, GDN_HEADS, HEAD_DIM).transpose(0, 2, 1, 3)

    q = l2norm(heads(q))
    k = l2norm(heads(k))
    v = heads(v)
    z = p[..., 3 * GDN_W:4 * GDN_W].astype(jnp.float32).reshape(B, T, GDN_HEADS, HEAD_DIM)
    off = 4 * GDN_W
    b = p[..., off:off + 2 * GDN_HEADS].astype(jnp.float32).reshape(B, T, 2, GDN_HEADS)
    a = p[..., off + 2 * GDN_HEADS:GDN_IN].astype(jnp.float32).reshape(B, T, 2, GDN_HEADS)
    beta = jnp.transpose(jax.nn.sigmoid(b), (2, 0, 3, 1))
    g = jnp.transpose(-jnp.exp(a_log.astype(jnp.float32))
                      * jax.nn.softplus(a + dt_bias.astype(jnp.float32)), (2, 0, 3, 1))

    def flip(t):
        return jnp.flip(t, axis=2)

    o_fwd = gated_delta_chunked(q, k, v, g[0], beta[0])
    o_bwd = flip(gated_delta_chunked(flip(q), flip(k), flip(v), flip(g[1]), flip(beta[1])))
    o = (o_fwd + o_bwd).transpose(0, 2, 1, 3)
    y = rmsnorm(o, norm_w) * jax.nn.silu(z)
    return y.reshape(B, T, GDN_W)


def axial_rope_tables(T):
    rows = T // GRID_W
    row = jnp.repeat(jnp.arange(rows, dtype=jnp.float32), GRID_W)
    col = jnp.tile(jnp.arange(GRID_W, dtype=jnp.float32), rows)
    inv_freq = ROPE_THETA ** (-jnp.arange(0, ROT_HALF, 2, dtype=jnp.float32) / ROT_HALF)
    ang_r = row[:, None] * inv_freq[None, :]
    ang_c = col[:, None] * inv_freq[None, :]
    ang = jnp.concatenate([ang_r, ang_r, ang_c, ang_c], axis=-1)[:, None, :]
    return jnp.cos(ang), jnp.sin(ang)


def rotate_half_axial(x):
    x1, x2, x3, x4 = jnp.split(x, 4, axis=-1)
    return jnp.concatenate([-x2, x1, -x4, x3], axis=-1)


def apply_axial_rope(x, cos, sin):
    return x * cos + rotate_half_axial(x) * sin


def attn_group(p, q_norm_w, k_norm_w, cos, sin):
    B, T, _ = p.shape
    q = p[..., :ATT_W].reshape(B, T, ATT_HEADS, HEAD_DIM)
    k = p[..., ATT_W:ATT_W + KV_W].reshape(B, T, ATT_KV_HEADS, HEAD_DIM)
    v = p[..., ATT_W + KV_W:].reshape(B, T, ATT_KV_HEADS, HEAD_DIM)
    q = apply_axial_rope(rmsnorm(q, q_norm_w), cos, sin).astype(p.dtype)
    k = apply_axial_rope(rmsnorm(k, k_norm_w), cos, sin).astype(p.dtype)
    nb = T // Q_BLOCK
    qb = q.reshape(B, nb, Q_BLOCK, ATT_KV_HEADS, ATT_GROUPS, HEAD_DIM).transpose(1, 0, 3, 4, 2, 5)
    kt = k.transpose(0, 2, 1, 3)
    vt = v.transpose(0, 2, 1, 3)
    scale = HEAD_DIM ** -0.5

    def block(qi):
        s = jnp.einsum('bhgqd,bhkd->bhgqk', qi, kt, preferred_element_type=jnp.float32) * scale
        pr = jax.nn.softmax(s, axis=-1).astype(vt.dtype)
        return jnp.einsum('bhgqk,bhkd->bhgqd', pr, vt)

    o = lax.map(block, qb)
    return o.transpose(1, 0, 4, 2, 3, 5).reshape(B, T, ATT_W)


def encoder_trunk(x, c, ada_w, ada_b, norm1_w, norm2_w, w_in, conv_w, a_log, dt_bias,
                  gdn_norm_w, q_norm_w, k_norm_w, w_out, w_up, w_down):
    B, T, _ = x.shape
    cos, sin = axial_rope_tables(T)
    for l in range(DEPTH):
        mod = jax.nn.silu(c) @ ada_w[l] + ada_b[l]
        sh1, sc1, gt1, sh2, sc2, gt2 = [m[:, None, :] for m in jnp.split(mod, N_MOD, axis=-1)]
        h = (rmsnorm(x, norm1_w[l]) * (1.0 + sc1) + sh1).astype(x.dtype)
        p = h @ w_in[l]
        ya = gdn_group(p[..., :GDN_IN], conv_w[l], a_log[l], dt_bias[l], gdn_norm_w[l])
        yb = attn_group(p[..., GDN_IN:], q_norm_w[l], k_norm_w[l], cos, sin)
        mix = jnp.concatenate([ya.astype(x.dtype), yb.astype(x.dtype)], axis=-1) @ w_out[l]
        x = x + gt1 * mix
        h2 = (rmsnorm(x, norm2_w[l]) * (1.0 + sc2) + sh2).astype(x.dtype)
        x = x + gt2 * (jnp.square(jax.nn.relu(h2 @ w_up[l])) @ w_down[l])
    return x


def setup_inputs(seed: int = 0) -> dict:
    key = jax.random.key(seed)
    ks = jax.random.split(key, 18)

    def nrm(k, shape, s):
        return jax.random.normal(k, shape, jnp.float32) * s

    dt = jnp.exp(jax.random.uniform(ks[11], (DEPTH, 2, GDN_HEADS), jnp.float32,
                                    math.log(1e-3), math.log(1e-1)))
    return {
        'x_prompt': nrm(ks[0], (BATCH, SEQ, D_MODEL), 1.0),
        'x_sample': nrm(ks[1], (DEC_BATCH, DEC_SEQ, D_MODEL), 1.0),
        'c_prompt': nrm(ks[2], (BATCH, D_MODEL), 1.0),
        'c_sample': nrm(ks[3], (DEC_BATCH, D_MODEL), 1.0),
        'ada_w': nrm(ks[4], (DEPTH, D_MODEL, N_MOD * D_MODEL), 0.5 * D_MODEL ** -0.5),
        'ada_b': nrm(ks[5], (DEPTH, N_MOD * D_MODEL), 0.02),
        'norm1_w': 1.0 + nrm(ks[6], (DEPTH, D_MODEL), 0.02),
        'norm2_w': 1.0 + nrm(ks[7], (DEPTH, D_MODEL), 0.02),
        'w_in': nrm(ks[8], (DEPTH, D_MODEL, IN_COLS), D_MODEL ** -0.5),
        'conv_w': nrm(ks[9], (DEPTH, CONV_K, 3 * GDN_W), CONV_K ** -0.5),
        'a_log': jnp.log(jax.random.uniform(ks[10], (DEPTH, 2, GDN_HEADS), jnp.float32, 1.0, 16.0)),
        'dt_bias': dt + jnp.log(-jnp.expm1(-dt)),
        'gdn_norm_w': 1.0 + nrm(ks[12], (DEPTH, HEAD_DIM), 0.02),
        'q_norm_w': 1.0 + nrm(ks[13], (DEPTH, HEAD_DIM), 0.02),
        'k_norm_w': 1.0 + nrm(ks[14], (DEPTH, HEAD_DIM), 0.02),
        'w_out': nrm(ks[15], (DEPTH, D_MIX, D_MODEL), D_MIX ** -0.5),
        'w_up': nrm(ks[16], (DEPTH, D_MODEL, D_FF), D_MODEL ** -0.5),
        'w_down': nrm(ks[17], (DEPTH, D_FF, D_MODEL), D_FF ** -0.5),
    }


def reference(x_prompt, x_sample, c_prompt, c_sample, ada_w, ada_b, norm1_w, norm2_w, w_in,
              conv_w, a_log, dt_bias, gdn_norm_w, q_norm_w, k_norm_w, w_out, w_up, w_down):
    y_prompt = encoder_trunk(x_prompt, c_prompt, ada_w, ada_b, norm1_w, norm2_w, w_in, conv_w,
                             a_log, dt_bias, gdn_norm_w, q_norm_w, k_norm_w, w_out, w_up, w_down)
    y_sample = encoder_trunk(x_sample, c_sample, ada_w, ada_b, norm1_w, norm2_w, w_in, conv_w,
                             a_log, dt_bias, gdn_norm_w, q_norm_w, k_norm_w, w_out, w_up, w_down)
    return (y_prompt, y_sample)
```

```python
import math
from contextlib import ExitStack

import numpy as np
import concourse.bass as bass
import concourse.mybir as mybir
from concourse.bass_utils import run_bass_kernel_spmd

F32 = mybir.dt.float32
BF16 = mybir.dt.bfloat16
ALU = mybir.AluOpType
AF = mybir.ActivationFunctionType
AX = mybir.AxisListType

D = 2048
DC = 16
HD = 128
GH = 8
AH = 8
KVH = 2
DFF = 8192
FC = 64
IN_COLS = 5664
EPS = 1e-6
NEG = -30000.0


class _Stop(Exception):
    pass


STOP_AFTER = [0]
CKPT = [{"PA.params", "PA.mrow_zpad", "PA.norm", "PA.wtiles", "PA.ba", "PA.gates"}]


class Buf:
    __slots__ = ("name", "w", "r")

    def __init__(self, name=""):
        self.name = name
        self.w = None
        self.r = []


class Sched:
    ENG = ("pe", "act", "dve", "pool", "sp")

    def __init__(self, nc, es):
        self.nc = nc
        self.es = es
        self.ops = {k: [] for k in self.ENG}
        self.esem = {k: es.enter_context(nc.semaphore("s_" + k)) for k in self.ENG}
        self.ebase = {k: 0 for k in self.ENG}
        self.dsem = {}
        self.dcount = {}
        self.ddirty = set()
        self.gen = 0
        self.nops = 0

    def _deps(self, reads, writes):
        g = self.gen
        deps = []
        for b in reads:
            if b.w is not None and b.w[3] == g:
                deps.append(b.w)
        for b in writes:
            if b.w is not None and b.w[3] == g:
                deps.append(b.w)
            for t in b.r:
                if t[3] == g:
                    deps.append(t)
        return deps

    def _mark(self, tok, reads, writes):
        for b in reads:
            b.r.append(tok)
            if len(b.r) > 48:
                b.r = b.r[-48:]
        for b in writes:
            b.w = tok
            b.r = []

    def op(self, eng, fn, reads=(), writes=()):
        deps = self._deps(reads, writes)
        lst = self.ops[eng]
        tok = ("E", eng, len(lst), self.gen)
        lst.append([fn, deps, None, False])
        self._mark(tok, reads, writes)
        return tok

    def dma(self, q, out, in_, reads=(), writes=(), key=None, **kw):
        deps = self._deps(reads, writes)
        if key not in self.dsem:
            self.dsem[key] = self.es.enter_context(self.nc.semaphore("d_%d" % len(self.dsem)))
            self.dcount[key] = 0
        self.dcount[key] += 16
        self.ddirty.add(key)
        tok = ("D", key, self.dcount[key], self.gen)
        sem = self.dsem[key]

        def fn(e, out=out, in_=in_, kw=kw):
            return e.dma_start(out=out, in_=in_, **kw)
        self.ops[q].append([fn, deps, (sem, 16), False])
        self._mark(tok, reads, writes)
        return tok

    def load(self, q, tl, dst_ap, src_ap, reads=(), **kw):
        return self.dma(q, dst_ap, src_ap, reads=reads, writes=[tl.b], key="ld_" + tl.b.name, **kw)

    def store(self, q, dst_ap, tl, src_ap, writes=(), **kw):
        return self.dma(q, dst_ap, src_ap, reads=[tl.b], writes=writes, key="st_" + tl.b.name, **kw)

    def flush(self):
        for eng, lst in self.ops.items():
            for rec in lst:
                for d in rec[1]:
                    if d[0] == "E" and not (d[1] == eng and eng == "pe"):
                        self.ops[d[1]][d[2]][3] = True
        cum = {}
        for eng, lst in self.ops.items():
            c = self.ebase[eng]
            arr = []
            for rec in lst:
                if rec[3]:
                    c += 1
                arr.append(c)
            cum[eng] = arr
        nc = self.nc
        with nc.Block() as block:
            def run(eng, e):
                seen = {}
                for rec in self.ops[eng]:
                    need = {}
                    for d in rec[1]:
                        if d[0] == "E":
                            if d[1] == eng and eng == "pe":
                                continue
                            s = ("E", d[1])
                            v = cum[d[1]][d[2]]
                        else:
                            s = ("D", d[1])
                            v = d[2]
                        if seen.get(s, 0) >= v:
                            continue
                        if need.get(s, 0) < v:
                            need[s] = v
                    for s, v in need.items():
                        sem = self.esem[s[1]] if s[0] == "E" else self.dsem[s[1]]
                        e.wait_ge(sem, v)
                        seen[s] = v
                    ins = rec[0](e)
                    if rec[2] is not None:
                        ins.then_inc(rec[2][0], rec[2][1])
                    elif rec[3]:
                        ins.then_inc(self.esem[eng], 1)
                if eng == "sp":
                    for key in sorted(self.ddirty):
                        e.wait_ge(self.dsem[key], self.dcount[key])

            block.tensor(lambda e: run("pe", e))
            block.scalar(lambda e: run("act", e))
            block.vector(lambda e: run("dve", e))
            block.gpsimd(lambda e: run("pool", e))
            block.sync(lambda e: run("sp", e))
        for eng in self.ENG:
            self.nops += len(self.ops[eng])
            self.ebase[eng] = cum[eng][-1] if cum[eng] else self.ebase[eng]
            self.ops[eng] = []
        self.ddirty = set()
        self.gen += 1
        if STOP_AFTER[0] and self.gen == STOP_AFTER[0]:
            raise _Stop()


class Tl:
    __slots__ = ("t", "b")

    def __init__(self, t, name=""):
        self.t = t
        self.b = Buf(name)


def make_consts(T):
    i = np.arange(128)
    J, I = np.meshgrid(i, i, indexing="ij")
    c32 = {}
    c32["ident"] = (J == I)
    c32["ones"] = np.ones((128, 128))
    c32["negones"] = -np.ones((128, 128))
    c32["cum_f"] = (J <= I)
    c32["cum_b"] = (J >= I)
    c32["rest_f"] = (J > I)
    c32["rest_b"] = (J < I)
    c32["nm_f"] = np.where(I >= J, 0.0, NEG)
    c32["nm_b"] = np.where(I <= J, 0.0, NEG)
    bd32 = (J // 32 == I // 32)
    bd64 = (J // 64 == I // 64)
    su = (I > J)
    sl = (I < J)
    for nm, st in (("f", su), ("b", sl)):
        c32["cma_" + nm] = -(st & bd32).astype(np.float64)
        c32["cmb_" + nm] = (st & bd64 & ~bd32)
        c32["cmc_" + nm] = (st & ~bd64)
    names32 = list(c32.keys())
    cf = np.concatenate([np.asarray(c32[k], np.float32) for k in names32], axis=1)
    Rl = np.zeros((128, 128), np.float32)
    for d in range(128):
        q = d // 32
        if q % 2 == 0:
            Rl[d + 32, d] = -1.0
        else:
            Rl[d - 32, d] = 1.0
    cb = np.concatenate([np.eye(128, dtype=np.float32), np.ones((128, 128), np.float32), Rl], axis=1)
    rows = T // 64
    row = np.repeat(np.arange(rows, dtype=np.float32), 64)
    col = np.tile(np.arange(64, dtype=np.float32), rows)
    inv = (10000.0 ** (-np.arange(0, 64, 2, dtype=np.float32) / 64)).astype(np.float32)
    ar = row[:, None] * inv[None, :]
    ac = col[:, None] * inv[None, :]
    ang = np.concatenate([ar, ar, ac, ac], axis=-1)
    cosT = np.ascontiguousarray(np.cos(ang).T.astype(np.float32))
    sinT = np.ascontiguousarray(np.sin(ang).T.astype(np.float32))
    return names32, cf, cb, cosT, sinT


def build_program(T, depth, dbg=()):
    assert T % 512 == 0
    NT = T // 128
    NB = T // 512
    NG = NT // 4
    names32, cf_np, cb_np, _, _ = make_consts(T)
    NCF = cf_np.shape[1]
    nc = bass.Bass("TRN2", target_bir_lowering=False)

    def din(name, shape, dt=F32):
        return nc.dram_tensor(name, list(shape), dt, kind="ExternalInput").ap()

    def dscr(name, shape, dt=F32):
        return nc.dram_tensor(name, list(shape), dt, kind="Internal").ap()

    x_in = din("x", [T, D])
    c_in = din("c", [D])
    ada_w = din("ada_w", [depth, D, 6 * D])
    ada_b = din("ada_b", [depth, 6 * D])
    n1w = din("norm1_w", [depth, D])
    n2w = din("norm2_w", [depth, D])
    w_in = din("w_in", [depth, D, IN_COLS])
    conv_w = din("conv_w", [depth, 5, 3072])
    a_log = din("a_log", [depth, 16])
    dt_bias = din("dt_bias", [depth, 16])
    gnw = din("gdn_norm_w", [depth, 128])
    qnw = din("q_norm_w", [depth, 128])
    knw = din("k_norm_w", [depth, 128])
    w_out = din("w_out", [depth, D, D])
    w_up = din("w_up", [depth, D, DFF])
    w_down = din("w_down", [depth, DFF, D])
    cf_in = din("cf", [128, NCF])
    cb_in = din("cb", [128, 384])
    cos_in = din("cosT", [128, T])
    sin_in = din("sinT", [128, T])
    mrow_in = din("mask_row", [1, T])
    mcol_in = din("mask_col", [128, NT])
    y_out = nc.dram_tensor("y", [T, D], F32, kind="ExternalOutput").ap()
    dbg_out = {}

    xres = dscr("xres", [D, T])
    pqkv = dscr("pqkv", [3072, T + 4])
    szs = dscr("szs", [1024, T], BF16)
    aqs = dscr("aqs", [1024, T], BF16)
    mixs = dscr("mixs", [D, T], BF16)
    modrow = dscr("modrow", [depth, 6 * D])
    xres_v = xres.rearrange("(c p) t -> p c t", p=128)
    pq_v = pqkv.rearrange("(c p) t -> p c t", p=128)
    sz_v = szs.rearrange("(c p) t -> p c t", p=128)
    aq_v = aqs.rearrange("(c p) t -> p c t", p=128)
    mix_v = mixs.rearrange("(c p) t -> p c t", p=128)
    bxres = [Buf("xres%d" % b) for b in range(NB)]
    bpq = [[Buf("pq") for b in range(NB)] for _ in range(6)]
    bpq_halo = Buf("pqh")
    bsz = [[Buf("sz") for b in range(NB)] for _ in range(2)]
    baq = [[Buf("aq") for b in range(NB)] for _ in range(2)]
    bmix_g = [Buf("mixg") for _ in range(GH)]
    bmix_a = [[Buf("mixa") for b in range(NB)] for _ in range(KVH)]
    bmod = Buf("modrow")

    es = ExitStack()
    try:
        with es:
            S = Sched(nc, es)
            cnt = [0]

            def ckpt(tag):
                if CKPT[0] is True or (CKPT[0] and tag in CKPT[0]):
                    S.flush()

            def mk_sb(scope):
                def sb(shape, dt=F32, name="t"):
                    cnt[0] += 1
                    nm = "%s_%d" % (name, cnt[0])
                    return Tl(scope.enter_context(nc.sbuf_tensor(nm, list(shape), dt)), nm)
                return sb
            sb0 = mk_sb(es)

            banks = [Tl(es.enter_context(nc.psum_tensor("bank%d" % i, [128, 512], F32)), "bank%d" % i) for i in range(8)]
            bank_i = [0]

            def bank():
                b = banks[bank_i[0] % 8]
                bank_i[0] += 1
                return b

            def v4(tl):
                return tl.t[:].rearrange("p a c -> p (a c)")

            def q4(ap):
                return ap.rearrange("p (a c) -> p a c", a=4)

            cf = sb0([128, NCF], F32, "cf")
            cb = sb0([128, 384], BF16, "cb")
            epsc = sb0([128, 1], F32, "eps")
            onec = sb0([128, 1], F32, "one")
            mcol = sb0([128, NT], F32, "mcol")
            S.load("sp", cf, cf.t[:], cf_in)
            S.load("sp", mcol, mcol.t[:], mcol_in)
            with ExitStack() as sc:
                sb = mk_sb(sc)
                cbf = sb([128, 384], F32, "cbf")
                S.load("sp", cbf, cbf.t[:], cb_in)
                S.op("dve", lambda e: e.tensor_copy(out=cb.t[:], in_=cbf.t[:]), reads=[cbf.b], writes=[cb.b])
                S.op("pool", lambda e: e.memset(epsc.t[:], EPS), writes=[epsc.b])
                S.op("pool", lambda e: e.memset(onec.t[:], 1.0), writes=[onec.b])
                S.flush()

            def C32(name):
                k = names32.index(name)
                return cf.t[:, k * 128:(k + 1) * 128]
            ident_bf = cb.t[:, 0:128]
            ones_bf = cb.t[:, 128:256]
            rl_bf = cb.t[:, 256:384]

            def dump(name, src_tl, src_ap, shape, dt=F32):
                if name in dbg and name not in dbg_out:
                    o = nc.dram_tensor("dbg_" + name, list(shape), dt, kind="ExternalOutput").ap()
                    dbg_out[name] = o
                    S.dma("sp", o, src_ap, reads=[src_tl.b], key="dbg_" + name)


            def load_cols(sbf, dst_tl, dst_ap, src_1d, n, reads=()):
                tmpT = sbf([n, 128], F32, "lc")
                S.load("sp", tmpT, tmpT.t[:], src_1d.rearrange("(c p) -> c p", p=128), reads=reads)
                bk = bank()
                S.op("pe", lambda e: e.transpose(out=bk.t[:, 0:n], in_=tmpT.t[:], identity=cf.t[0:n, 0:n]),
                     reads=[tmpT.b, cf.b], writes=[bk.b])
                S.op("dve", lambda e: e.tensor_copy(out=dst_ap, in_=bk.t[:, 0:n]), reads=[bk.b], writes=[dst_tl.b])

            def bload(tl, src_row):
                S.load("sp", tl, tl.t[:].unsqueeze(1), src_row.partition_broadcast(128))

            modT = [sb0([128, 6 * DC], F32, "modT") for _ in range(depth)]
            A1 = [sb0([128, DC], F32, "A1") for _ in range(depth)]
            A2 = [sb0([128, DC], F32, "A2") for _ in range(depth)]

            with ExitStack() as sc:
                sb = mk_sb(sc)
                xtok = [sb([128, D], F32, "xtok") for _ in range(2)]
                xTd = [sb([128, DC, 512], F32, "xT") for _ in range(2)]
                for b in range(NB):
                    xt = xTd[b % 2]
                    for tt in range(4):
                        ti = b * 4 + tt
                        xk = xtok[ti % 2]
                        S.load("sp", xk, xk.t[:], x_in[ti * 128:(ti + 1) * 128, :])
                        for cg in range(4):
                            bk = bank()
                            for cc in range(4):
                                c = cg * 4 + cc
                                S.op("pe", lambda e, bk=bk, xk=xk, c=c, cc=cc: e.transpose(
                                    out=bk.t[:, cc * 128:(cc + 1) * 128], in_=xk.t[:, c * 128:(c + 1) * 128], identity=C32("ident")),
                                    reads=[xk.b, cf.b], writes=[bk.b])
                            outv = xt.t[:, cg * 4:(cg + 1) * 4, tt * 128:(tt + 1) * 128]
                            inv = q4(bk.t[:])
                            if cg % 2 == 0:
                                S.op("act", lambda e, o=outv, i=inv: e.activation(out=o, in_=i, func=AF.Copy), reads=[bk.b], writes=[xt.b])
                            else:
                                S.op("dve", lambda e, o=outv, i=inv: e.tensor_copy(out=o, in_=i), reads=[bk.b], writes=[xt.b])
                    S.store("sp", xres_v[:, :, b * 512:(b + 1) * 512], xt, xt.t[:], writes=[bxres[b]])
                S.flush()

            with ExitStack() as sc:
                sb = mk_sb(sc)
                cT = sb([128, DC], F32, "cT")
                sc_ = sb([128, DC], F32, "silu_c")
                adaw_t = [sb([128, 2048], F32, "adaw") for _ in range(3)]
                modr = sb([1, 6 * D], F32, "modr")
                adab = sb([1, 6 * D], F32, "adab")
                n1T = [sb([128, DC], F32, "n1T") for _ in range(depth)]
                n2T = [sb([128, DC], F32, "n2T") for _ in range(depth)]
                load_cols(sb, cT, cT.t[:], c_in, DC)
                S.op("act", lambda e: e.activation(out=sc_.t[:], in_=cT.t[:], func=AF.Silu), reads=[cT.b], writes=[sc_.b])
                ai = 0
                for l in range(depth):
                    S.load("sp", adab, adab.t[:], ada_b[l:l + 1, :])
                    for g4 in range(6):
                        bks = [bank() for _ in range(4)]
                        for kc in range(DC):
                            wt = adaw_t[ai % 3]
                            ai += 1
                            S.load("sp", wt, wt.t[:], ada_w[l, kc * 128:(kc + 1) * 128, g4 * 2048:(g4 + 1) * 2048])
                            for q in range(4):
                                S.op("pe", lambda e, bk=bks[q], wt=wt, kc=kc, q=q: e.matmul(
                                    bk.t[0:1, :], lhsT=sc_.t[:, kc:kc + 1], rhs=wt.t[:, q * 512:(q + 1) * 512],
                                    start=(kc == 0), stop=(kc == DC - 1)), reads=[sc_.b, wt.b], writes=[bks[q].b])
                        for q in range(4):
                            col = g4 * 2048 + q * 512
                            S.op("dve", lambda e, bk=bks[q], col=col: e.tensor_tensor(
                                out=modr.t[0:1, col:col + 512], in0=bk.t[0:1, :], in1=adab.t[0:1, col:col + 512], op=ALU.add),
                                reads=[bks[q].b, adab.b], writes=[modr.b])
                    S.store("sp", modrow[l:l + 1, :], modr, modr.t[:], writes=[bmod])
                    load_cols(sb, modT[l], modT[l].t[:], modrow[l], 6 * DC, reads=[bmod])
                    load_cols(sb, n1T[l], n1T[l].t[:], n1w[l], DC)
                    load_cols(sb, n2T[l], n2T[l].t[:], n2w[l], DC)
                    S.op("dve", lambda e, l=l: e.scalar_tensor_tensor(out=A1[l].t[:], in0=modT[l].t[:, DC:2 * DC], scalar=1.0,
                                                                      in1=n1T[l].t[:], op0=ALU.add, op1=ALU.mult),
                         reads=[modT[l].b, n1T[l].b], writes=[A1[l].b])
                    S.op("dve", lambda e, l=l: e.scalar_tensor_tensor(out=A2[l].t[:], in0=modT[l].t[:, 4 * DC:5 * DC], scalar=1.0,
                                                                      in1=n2T[l].t[:], op0=ALU.add, op1=ALU.mult),
                         reads=[modT[l].b, n2T[l].b], writes=[A2[l].b])
                if "mod" in dbg:
                    dump("mod", modT[0], modT[0].t[:], [128, 6 * DC])
                S.flush()

            def make_common(sb, nw):
                cm = {}
                cm["sq"] = sb([128, 8, 512], BF16, "sq")
                cm["rstd"] = sb([128, 512], F32, "rstd")
                cm["tmpf"] = [sb([128, 512], F32, "tmpf") for _ in range(3)]
                cm["tmpi"] = 0
                cm["wbuf"] = [sb([128, DC, 512], BF16, "wbuf") for _ in range(nw)]
                cm["wbi"] = 0
                return cm

            def tmp(cm):
                t = cm["tmpf"][cm["tmpi"] % 3]
                cm["tmpi"] += 1
                return t

            def load_w(cm, src_ap, kparts=DC, cols=512):
                n = len(cm["wbuf"])
                wt = cm["wbuf"][cm["wbi"] % n]
                cm["wbi"] += 1
                S.load("pool", wt, wt.t[:, 0:kparts, 0:cols], src_ap.rearrange("(c p) m -> p c m", p=128))
                return wt

            def norm_block(cm, xt, Acol, Bcol, ht):
                sq, rstd = cm["sq"], cm["rstd"]
                bk = bank()
                for half in range(2):
                    S.op("act", lambda e, half=half: e.activation(out=sq.t[:], in_=xt.t[:, half * 8:(half + 1) * 8, :], func=AF.Square),
                         reads=[xt.b], writes=[sq.b])
                    for c8 in range(8):
                        c = half * 8 + c8
                        S.op("pe", lambda e, c=c, c8=c8, bk=bk: e.matmul(bk.t[:], lhsT=ones_bf, rhs=sq.t[:, c8, :], start=(c == 0), stop=(c == DC - 1)),
                             reads=[sq.b, cb.b], writes=[bk.b])
                S.op("act", lambda e, bk=bk: e.activation(out=rstd.t[:], in_=bk.t[:], func=AF.Sqrt, bias=epsc.t[:], scale=1.0 / D),
                     reads=[bk.b, epsc.b], writes=[rstd.b])
                S.op("dve", lambda e: e.reciprocal(out=rstd.t[:], in_=rstd.t[:]), reads=[rstd.b], writes=[rstd.b])
                for c in range(DC):
                    tp = tmp(cm)
                    S.op("dve", lambda e, c=c, tp=tp: e.scalar_tensor_tensor(out=tp.t[:], in0=xt.t[:, c, :], scalar=Acol[:, c:c + 1],
                                                                             in1=rstd.t[:], op0=ALU.mult, op1=ALU.mult),
                         reads=[xt.b, rstd.b], writes=[tp.b])
                    S.op("act", lambda e, c=c, tp=tp: e.activation(out=ht.t[:, c, :], in_=tp.t[:], func=AF.Identity, bias=Bcol[:, c:c + 1]),
                         reads=[tp.b], writes=[ht.b])

            for l in range(depth):
                mod = modT[l].t
                sh1 = mod[:, 0:DC]
                gt1 = mod[:, 2 * DC:3 * DC]
                sh2 = mod[:, 3 * DC:4 * DC]
                gt2 = mod[:, 5 * DC:6 * DC]
                ly = ExitStack()
                with ly:
                    sbl = mk_sb(ly)
                    gb_all = sbl([128, NT, 32], F32, "gb_all")
                    beta_all = sbl([128, NT, 16], F32, "beta_all")
                    g_all = sbl([128, NT, 16], F32, "g_all")
                    gc_all = sbl([128, NT, 16], F32, "gc_all")
                    negegc_all = sbl([128, NT, 16], F32, "negegc")
                    erest_all = sbl([128, NT, 16], F32, "erest")
                    egl_all = sbl([128, NT, 16], F32, "egl")
                    alog_r = sbl([128, 16], F32, "alog")
                    dtb_r = sbl([128, 16], F32, "dtb")
                    nexpalog = sbl([128, 16], F32, "nexpalog")
                    qnw_c = sbl([128, 1], F32, "qnw_c")
                    knw_c = sbl([128, 1], F32, "knw_c")
                    gnw_c = sbl([128, 1], F32, "gnw_c")
                    qnw_r = sbl([128, 128], F32, "qnw_r")
                    knw_r = sbl([128, 128], F32, "knw_r")
                    kbias = sbl([128, NT], F32, "kbias")
                    mq = sbl([128, 1], F32, "mq")
                    mk = sbl([128, 1], F32, "mk")
                    convw = sbl([128, 24, 5], F32, "convw")
                    att = ExitStack()
                    with att:
                        sba = mk_sb(att)
                        kT_att = sba([128, KVH, T], BF16, "kT_att")
                        v_att = sba([128, NT, 256], BF16, "v_att")
                        with ExitStack() as sc:
                            sb = mk_sb(sc)
                            cm = make_common(sb, 2)
                            with nc.allow_non_contiguous_dma(reason="tiny param loads"):
                                bload(alog_r, a_log[l:l + 1, :])
                                bload(dtb_r, dt_bias[l:l + 1, :])
                                S.load("sp", qnw_c, qnw_c.t[:], qnw[l].rearrange("(p o) -> p o", o=1))
                                S.load("sp", knw_c, knw_c.t[:], knw[l].rearrange("(p o) -> p o", o=1))
                                S.load("sp", gnw_c, gnw_c.t[:], gnw[l].rearrange("(p o) -> p o", o=1))
                                bload(qnw_r, qnw[l:l + 1, :])
                                bload(knw_r, knw[l:l + 1, :])
                            cw5 = sb([5, 3072], F32, "cw5")
                            S.load("sp", cw5, cw5.t[:], conv_w[l])
                            bkc = bank()
                            for c in range(24):
                                S.op("pe", lambda e, c=c: e.transpose(out=bkc.t[:, c * 5:(c + 1) * 5], in_=cw5.t[0:5, c * 128:(c + 1) * 128], identity=cf.t[0:5, 0:5]),
                                     reads=[cw5.b, cf.b], writes=[bkc.b])
                            S.op("dve", lambda e: e.tensor_copy(out=convw.t[:], in_=bkc.t[:, 0:120].rearrange("p (c k) -> p c k", k=5)), reads=[bkc.b], writes=[convw.b])
                            S.op("act", lambda e: e.activation(out=nexpalog.t[:], in_=alog_r.t[:], func=AF.Exp), reads=[alog_r.b], writes=[nexpalog.b])
                            S.op("dve", lambda e: e.tensor_scalar(out=nexpalog.t[:], in0=nexpalog.t[:], scalar1=-1.0, scalar2=None, op0=ALU.mult),
                                 reads=[nexpalog.b], writes=[nexpalog.b])
                            S.op("dve", lambda e: e.tensor_reduce(out=mq.t[:], in_=qnw_r.t[:], axis=AX.X, op=ALU.max, apply_absolute_value=True),
                                 reads=[qnw_r.b], writes=[mq.b])
                            S.op("dve", lambda e: e.tensor_reduce(out=mk.t[:], in_=knw_r.t[:], axis=AX.X, op=ALU.max, apply_absolute_value=True),
                                 reads=[knw_r.b], writes=[mk.b])
                            S.op("dve", lambda e: e.scalar_tensor_tensor(out=mq.t[:], in0=mq.t[:], scalar=-math.sqrt(128.0), in1=mk.t[:],
                                                                         op0=ALU.mult, op1=ALU.mult), reads=[mq.b, mk.b], writes=[mq.b])
                            S.op("dve", lambda e: e.tensor_scalar(out=kbias.t[:], in0=mcol.t[:], scalar1=-NEG, scalar2=NEG, op0=ALU.mult, op1=ALU.add),
                                 reads=[mcol.b], writes=[kbias.b])
                            S.op("dve", lambda e: e.tensor_scalar(out=kbias.t[:], in0=kbias.t[:], scalar1=mq.t[:, 0:1], scalar2=None, op0=ALU.add),
                                 reads=[kbias.b, mq.b], writes=[kbias.b])

                            ckpt("PA.params")
                            xt = sb([128, DC, 512], F32, "xT")
                            hTs = [sb([128, DC, 512], BF16, "hT")] * 2
                            mrow32 = sb([128, 512], F32, "mrow32")
                            cosT = sb([128, 512], F32, "cosT")
                            sinT = sb([128, 512], F32, "sinT")
                            ckpt("PA.mrow_zpad")
                            stage = [sb([128, 4, 516], F32, "stage")] * 2
                            for st_ in stage[:1]:
                                S.op("pool", lambda e, st_=st_: e.memset(st_.t[:], 0.0), writes=[st_.b])
                            stage_bf = [sb([128, 4, 512], BF16, "stage_bf")] * 2
                            qn_bf = [sb([128, 512], BF16, "qn_bf") for _ in range(2)]
                            sqh = [sb([128, 512], BF16, "sqh") for _ in range(2)]
                            rs_h = [sb([128, 512], F32, "rs_h") for _ in range(2)]
                            sti = 0
                            for b in range(NB):
                                tok = slice(b * 512, (b + 1) * 512)
                                ht = hTs[b % 2]
                                if b == 0:
                                    S.load("sp", xt, xt.t[:], xres_v[:, :, tok], reads=[bxres[b]])
                                bload(mrow32, mrow_in[:, tok])
                                S.load("sp", cosT, cosT.t[:], cos_in[:, tok])
                                S.load("sp", sinT, sinT.t[:], sin_in[:, tok])
                                norm_block(cm, xt, A1[l].t, sh1, ht)
                                if b + 1 < NB:
                                    S.load("sp", xt, xt.t[:], xres_v[:, :, (b + 1) * 512:(b + 2) * 512], reads=[bxres[b + 1]])
                                ckpt("PA.norm")
                                if b == 0 and l == 0:
                                    dump("hT", ht, ht.t[:], [128, DC, 512], BF16)
                                for wt_i in range(11):
                                    col0 = wt_i * 512 if wt_i < 8 else 4128 + (wt_i - 8) * 512
                                    wt = load_w(cm, w_in[l, :, col0:col0 + 512])
                                    chunks = 2 if wt_i == 10 else 4
                                    st = stage[sti % 2]
                                    stb = stage_bf[sti % 2]
                                    sti += 1
                                    for ch in range(chunks):
                                        bk = bank()
                                        for c in range(DC):
                                            S.op("pe", lambda e, bk=bk, wt=wt, c=c, ch=ch, ht=ht: e.matmul(
                                                bk.t[:], lhsT=wt.t[:, c, ch * 128:(ch + 1) * 128], rhs=ht.t[:, c, :], start=(c == 0), stop=(c == DC - 1)),
                                                reads=[wt.b, ht.b], writes=[bk.b])
                                        if wt_i < 6:
                                            S.op("dve", lambda e, bk=bk, st=st, ch=ch, tok=tok: e.tensor_tensor(
                                                out=st.t[:, ch, 2:514], in0=bk.t[:], in1=mrow32.t[:], op=ALU.mult), reads=[bk.b, mrow32.b], writes=[st.b])
                                        elif wt_i < 8:
                                            S.op("act", lambda e, bk=bk, stb=stb, ch=ch: e.activation(out=stb.t[:, ch, :], in_=bk.t[:], func=AF.Silu),
                                                 reads=[bk.b], writes=[stb.b])
                                        else:
                                            nw = qnw_c if wt_i < 10 else knw_c
                                            sh_ = sqh[ch % 2]
                                            rsx = rs_h[ch % 2]
                                            qb_ = qn_bf[ch % 2]
                                            S.op("act", lambda e, bk=bk, sh_=sh_: e.activation(out=sh_.t[:], in_=bk.t[:], func=AF.Square),
                                                 reads=[bk.b], writes=[sh_.b])
                                            bk2 = bank()
                                            S.op("pe", lambda e, bk2=bk2, sh_=sh_: e.matmul(bk2.t[:], lhsT=ones_bf, rhs=sh_.t[:], start=True, stop=True),
                                                 reads=[sh_.b, cb.b], writes=[bk2.b])
                                            S.op("act", lambda e, bk2=bk2, rsx=rsx: e.activation(out=rsx.t[:], in_=bk2.t[:], func=AF.Sqrt, bias=epsc.t[:],
                                                                                                 scale=1.0 / 128), reads=[bk2.b, epsc.b], writes=[rsx.b])
                                            S.op("dve", lambda e, rsx=rsx: e.reciprocal(out=rsx.t[:], in_=rsx.t[:]), reads=[rsx.b], writes=[rsx.b])
                                            qf = tmp(cm)
                                            S.op("dve", lambda e, bk=bk, qf=qf, nw=nw, rsx=rsx: e.scalar_tensor_tensor(
                                                out=qf.t[:], in0=bk.t[:], scalar=nw.t[:, 0:1], in1=rsx.t[:], op0=ALU.mult, op1=ALU.mult),
                                                reads=[bk.b, nw.b, rsx.b], writes=[qf.b])
                                            S.op("act", lambda e, qf=qf, qb_=qb_: e.activation(out=qb_.t[:], in_=qf.t[:], func=AF.Copy), reads=[qf.b], writes=[qb_.b])
                                            bk3 = bank()
                                            S.op("pe", lambda e, bk3=bk3, qb_=qb_: e.matmul(bk3.t[:], lhsT=rl_bf, rhs=qb_.t[:], start=True, stop=True),
                                                 reads=[qb_.b, cb.b], writes=[bk3.b])
                                            r1 = tmp(cm)
                                            S.op("dve", lambda e, bk3=bk3, r1=r1: e.tensor_tensor(out=r1.t[:], in0=bk3.t[:], in1=sinT.t[:], op=ALU.mult),
                                                 reads=[bk3.b, sinT.b], writes=[r1.b])
                                            S.op("dve", lambda e, qf=qf: e.tensor_tensor(out=qf.t[:], in0=qf.t[:], in1=cosT.t[:], op=ALU.mult),
                                                 reads=[qf.b, cosT.b], writes=[qf.b])
                                            if wt_i < 10:
                                                S.op("dve", lambda e, qf=qf, r1=r1, stb=stb, ch=ch: e.tensor_tensor(out=stb.t[:, ch, :], in0=qf.t[:], in1=r1.t[:], op=ALU.add),
                                                     reads=[qf.b, r1.b], writes=[stb.b])
                                            else:
                                                S.op("dve", lambda e, qf=qf, r1=r1, ch=ch, tok=tok: e.tensor_tensor(out=kT_att.t[:, ch, tok], in0=qf.t[:], in1=r1.t[:], op=ALU.add),
                                                     reads=[qf.b, r1.b], writes=[kT_att.b])
                                    if wt_i < 6:
                                        lo = 0 if b == 0 else 2
                                        hi = 516 if b == NB - 1 else 514
                                        S.store("sp", pq_v[:, wt_i * 4:(wt_i + 1) * 4, b * 512 + lo:b * 512 + hi], st, st.t[:, :, lo:hi], writes=[bpq[wt_i][b]])
                                    elif wt_i < 8:
                                        S.store("sp", sz_v[:, (wt_i - 6) * 4:(wt_i - 5) * 4, tok], stb, stb.t[:], writes=[bsz[wt_i - 6][b]])
                                    elif wt_i < 10:
                                        S.store("sp", aq_v[:, (wt_i - 8) * 4:(wt_i - 7) * 4, tok], stb, stb.t[:], writes=[baq[wt_i - 8][b]])
                                    else:
                                        for tt in range(4):
                                            bk = bank()
                                            for c in range(DC):
                                                S.op("pe", lambda e, bk=bk, wt=wt, c=c, tt=tt, ht=ht: e.matmul(
                                                    bk.t[:, 0:256], lhsT=ht.t[:, c, tt * 128:(tt + 1) * 128], rhs=wt.t[:, c, 256:512], start=(c == 0), stop=(c == DC - 1)),
                                                    reads=[wt.b, ht.b], writes=[bk.b])
                                            S.op("act", lambda e, bk=bk, tt=tt, b=b: e.activation(out=v_att.t[:, b * 4 + tt, :], in_=bk.t[:, 0:256], func=AF.Copy),
                                                 reads=[bk.b], writes=[v_att.b])
                                ckpt("PA.wtiles")
                                wt = load_w(cm, w_in[l, :, 4096:4128], cols=32)
                                bk = bank()
                                for tt in range(4):
                                    for c in range(DC):
                                        S.op("pe", lambda e, bk=bk, wt=wt, c=c, tt=tt, ht=ht: e.matmul(
                                            bk.t[:, tt * 32:(tt + 1) * 32], lhsT=ht.t[:, c, tt * 128:(tt + 1) * 128], rhs=wt.t[:, c, 0:32], start=(c == 0), stop=(c == DC - 1)),
                                            reads=[wt.b, ht.b], writes=[bk.b])
                                S.op("dve", lambda e, bk=bk, b=b: e.tensor_copy(out=gb_all.t[:, b * 4:(b + 1) * 4, :], in_=q4(bk.t[:, 0:128])),
                                     reads=[bk.b], writes=[gb_all.b])

                            ckpt("PA.ba")
                            S.op("act", lambda e: e.activation(out=beta_all.t[:], in_=gb_all.t[:, :, 0:16], func=AF.Sigmoid), reads=[gb_all.b], writes=[beta_all.b])
                            S.op("dve", lambda e: e.tensor_tensor(out=beta_all.t[:], in0=beta_all.t[:], in1=mcol.t[:].unsqueeze(2).to_broadcast([128, NT, 16]), op=ALU.mult),
                                 reads=[beta_all.b, mcol.b], writes=[beta_all.b])
                            S.op("dve", lambda e: e.tensor_tensor(out=g_all.t[:], in0=gb_all.t[:, :, 16:32], in1=dtb_r.t[:].unsqueeze(1).to_broadcast([128, NT, 16]), op=ALU.add),
                                 reads=[gb_all.b, dtb_r.b], writes=[g_all.b])
                            S.op("dve", lambda e: e.tensor_scalar(out=g_all.t[:], in0=g_all.t[:], scalar1=60.0, scalar2=None, op0=ALU.min), reads=[g_all.b], writes=[g_all.b])
                            S.op("act", lambda e: e.activation(out=g_all.t[:], in_=g_all.t[:], func=AF.Exp), reads=[g_all.b], writes=[g_all.b])
                            S.op("act", lambda e: e.activation(out=g_all.t[:], in_=g_all.t[:], func=AF.Ln, bias=onec.t[:]), reads=[g_all.b, onec.b], writes=[g_all.b])
                            S.op("dve", lambda e: e.tensor_tensor(out=g_all.t[:], in0=g_all.t[:], in1=nexpalog.t[:].unsqueeze(1).to_broadcast([128, NT, 16]), op=ALU.mult),
                                 reads=[g_all.b, nexpalog.b], writes=[g_all.b])
                            ckpt("PA.gates")
                            for nm, dst in (("cum", "gc"), ("rest", "rest"), ("ones", "gl")):
                                bk = bank()
                                for t in range(NT):
                                    for d_ in range(2):
                                        cname = "ones" if nm == "ones" else nm + ("_f" if d_ == 0 else "_b")
                                        S.op("pe", lambda e, bk=bk, t=t, d_=d_, cname=cname: e.matmul(
                                            bk.t[:, t * 16 + d_ * 8:t * 16 + d_ * 8 + 8], lhsT=C32(cname), rhs=g_all.t[:, t, d_ * 8:(d_ + 1) * 8], start=True, stop=True),
                                            reads=[g_all.b, cf.b], writes=[bk.b])
                                src = bk.t[:, 0:NT * 16].rearrange("p (a c) -> p a c", c=16)
                                if dst == "gc":
                                    S.op("dve", lambda e, src=src: e.tensor_copy(out=gc_all.t[:], in_=src), reads=[bk.b], writes=[gc_all.b])
                                    S.op("act", lambda e, src=src: e.activation(out=negegc_all.t[:], in_=src, func=AF.Exp), reads=[bk.b], writes=[negegc_all.b])
                                    S.op("dve", lambda e: e.tensor_scalar(out=negegc_all.t[:], in0=negegc_all.t[:], scalar1=-1.0, scalar2=None, op0=ALU.mult),
                                         reads=[negegc_all.b], writes=[negegc_all.b])
                                elif dst == "rest":
                                    S.op("act", lambda e, src=src: e.activation(out=erest_all.t[:], in_=src, func=AF.Exp), reads=[bk.b], writes=[erest_all.b])
                                else:
                                    S.op("act", lambda e, src=src: e.activation(out=egl_all.t[:], in_=src, func=AF.Exp), reads=[bk.b], writes=[egl_all.b])
                            if l == 0:
                                dump("beta", beta_all, beta_all.t[:], [128, NT, 16])
                                dump("g", g_all, g_all.t[:], [128, NT, 16])
                                dump("gc", gc_all, gc_all.t[:], [128, NT, 16])
                                dump("kT_att", kT_att, kT_att.t[:], [128, KVH, T], BF16)
                                dump("v_att", v_att, v_att.t[:], [128, NT, 256], BF16)
                            S.flush()

                        with ExitStack() as sc:
                            sb = mk_sb(sc)
                            qblk = [sb([128, 4, 512], BF16, "qblk") for _ in range(2)]
                            pT = [sb([128, 512], BF16, "pT") for _ in range(4)]
                            accsum = [sb([128, 512], F32, "accsum") for _ in range(2)]
                            rsum = [sb([128, 512], F32, "rsum") for _ in range(2)]
                            yb = [sb([128, 4, 512], BF16, "yb") for _ in range(2)]
                            pti = 0
                            pacc = 0
                            for g in range(KVH):
                                for qb in range(NB):
                                    tok = slice(qb * 512, (qb + 1) * 512)
                                    qk_ = qblk[(g * NB + qb) % 2]
                                    ybt = yb[(g * NB + qb) % 2]
                                    S.load("sp", qk_, qk_.t[:], aq_v[:, g * 4:(g + 1) * 4, tok], reads=[baq[g][qb]])
                                    for hq in range(4):
                                        bO, bS = banks[4 + 2 * (pacc % 2)], banks[5 + 2 * (pacc % 2)]
                                        accs = accsum[pacc % 2]
                                        pacc += 1

                                        def qk(kt, g=g, qk_=qk_, hq=hq):
                                            bs_ = banks[kt % 4]
                                            S.op("pe", lambda e: e.matmul(
                                                bs_.t[:], lhsT=kT_att.t[:, g, kt * 128:(kt + 1) * 128], rhs=qk_.t[:, hq, :], start=True, stop=True),
                                                reads=[kT_att.b, qk_.b], writes=[bs_.b])
                                        qk(0)
                                        if NT > 1:
                                            qk(1)
                                        for kt in range(NT):
                                            if kt + 2 < NT:
                                                qk(kt + 2)
                                            bs_ = banks[kt % 4]
                                            p_ = pT[pti % 4]
                                            pti += 1
                                            S.op("act", lambda e, bs_=bs_, p_=p_, kt=kt: e.activation(out=p_.t[:], in_=bs_.t[:], func=AF.Exp, bias=kbias.t[:, kt:kt + 1],
                                                                                                     scale=128.0 ** -0.5), reads=[bs_.b, kbias.b], writes=[p_.b])
                                            S.op("pe", lambda e, bO=bO, g=g, kt=kt, p_=p_: e.matmul(bO.t[:], lhsT=v_att.t[:, kt, g * 128:(g + 1) * 128], rhs=p_.t[:],
                                                                                                     start=(kt == 0), stop=(kt == NT - 1)), reads=[v_att.b, p_.b], writes=[bO.b])
                                            if kt == 0:
                                                S.op("dve", lambda e, accs=accs, p_=p_: e.tensor_copy(out=accs.t[:], in_=p_.t[:]), reads=[p_.b], writes=[accs.b])
                                            else:
                                                S.op("dve", lambda e, accs=accs, p_=p_: e.tensor_tensor(out=accs.t[:], in0=accs.t[:], in1=p_.t[:], op=ALU.add),
                                                     reads=[p_.b, accs.b], writes=[accs.b])
                                        S.op("pe", lambda e, bS=bS, accs=accs: e.matmul(bS.t[:], lhsT=C32("ones"), rhs=accs.t[:], start=True, stop=True),
                                             reads=[cf.b, accs.b], writes=[bS.b])
                                        rs = rsum[hq % 2]
                                        S.op("dve", lambda e, bS=bS, rs=rs: e.reciprocal(out=rs.t[:], in_=bS.t[:]), reads=[bS.b], writes=[rs.b])
                                        S.op("dve", lambda e, bO=bO, rs=rs, ybt=ybt, hq=hq: e.tensor_tensor(out=ybt.t[:, hq, :], in0=bO.t[:], in1=rs.t[:], op=ALU.mult),
                                             reads=[bO.b, rs.b], writes=[ybt.b])
                                    S.store("sp", mix_v[:, 8 + g * 4:8 + (g + 1) * 4, tok], ybt, ybt.t[:], writes=[bmix_a[g][qb]])
                            S.flush()

                    with ExitStack() as sc:
                        sb = mk_sb(sc)
                        praw = [[sb([128, 516], F32, "praw") for _ in range(2)] for _ in range(3)]
                        acc_c = [sb([128, 512], F32, "acc_c") for _ in range(2)]
                        qT_s = sb([128, T], BF16, "qT_s")
                        kT_s = sb([128, T], BF16, "kT_s")
                        vT_b = [sb([128, 512], BF16, "vT_b") for _ in range(2)]
                        k_tok = sb([128, NT, 128], BF16, "k_tok")
                        v_tok = sb([128, NT, 128], BF16, "v_tok")
                        o_all = sb([128, NT, 128], F32, "o_all")
                        szT = [sb([128, 512], BF16, "szT") for _ in range(2)]
                        yT = [sb([128, 512], BF16, "yT") for _ in range(2)]
                        sqh = [sb([128, 512], BF16, "sqh") for _ in range(2)]
                        rs_h = [sb([128, 512], F32, "rs_h") for _ in range(2)]
                        dg = sb([128, 4, 128], F32, "dg")
                        DTt = sb([128, 4, 128], F32, "DT")
                        egr = sb([128, 4, 128], F32, "egr")
                        Gb = sb([128, 4, 128], F32, "Gb")
                        DTx = [sb([128, 4, 128], F32, "DTx%d" % i) for i in range(3)]

                        def grp(name):
                            return [sb([128, 4, 128], BF16, name) for _ in range(2)]
                        MTx = [grp("MTx%d" % i) for i in range(3)]
                        Mx = [grp("Mx%d" % i) for i in range(3)]
                        Pk = [grp("Pk%d" % i) for i in range(2)]
                        PTk = [grp("PTk%d" % i) for i in range(2)]
                        Rr = grp("Rr")
                        RTt = grp("RTt")
                        Zz = grp("Zz")
                        RING = 3
                        NTt = [[sb([128, 4, 128], BF16, "NT") for _ in range(RING)] for _ in range(2)]
                        ATt = [[sb([128, 4, 128], BF16, "AT") for _ in range(RING)] for _ in range(2)]
                        qgT = [[sb([128, 4, 128], BF16, "qgT") for _ in range(RING)] for _ in range(2)]
                        kd = [[sb([128, 4, 128], BF16, "kd") for _ in range(RING)] for _ in range(2)]
                        S32 = [sb([128, 128], F32, "S32_%d" % i) for i in range(2)]
                        Sbf = [sb([128, 128], BF16, "Sbf_%d" % i) for i in range(2)]
                        rr_ = [sb([128, 128], BF16, "r_%d" % i) for i in range(2)]
                        vn_ = [sb([128, 128], BF16, "vn_%d" % i) for i in range(2)]
                        ssq = sb([128, NT], F32, "ssq")
                        junk = sb([128, 128], F32, "junk")
                        on_bf = [sb([128, 128], BF16, "on_bf") for _ in range(2)]
                        idb = ident_bf.unsqueeze(1).to_broadcast([128, 4, 128])
                        k2c = [0]

                        def bcast_u(ap3):
                            return ap3.to_broadcast([128, 4, 128])

                        def mm4(outb, lT, rh):
                            for u in range(4):
                                S.op("pe", lambda e, outb=outb, lT=lT, rh=rh, u=u: e.matmul(
                                    outb.t[:, u * 128:(u + 1) * 128], lhsT=lT.t[:, u, :], rhs=rh.t[:, u, :], start=True, stop=True),
                                    reads=[lT.b, rh.b], writes=[outb.b])

                        def precompute(h, d_, g4, slot):
                            sfx = "_f" if d_ == 0 else "_b"
                            col = d_ * 8 + h
                            k2 = k2c[0] % 2
                            k2c[0] += 1
                            tsl = slice(g4 * 4, g4 * 4 + 4)
                            S.op("pool", lambda e: e.tensor_tensor(
                                out=dg.t[:], in0=C32("ident").unsqueeze(1).to_broadcast([128, 4, 128]),
                                in1=bcast_u(gc_all.t[:, tsl, col:col + 1]), op=ALU.mult), reads=[gc_all.b, cf.b], writes=[dg.b])
                            bG, bKQ, bC, bD = bank(), bank(), bank(), bank()
                            for u in range(4):
                                t = g4 * 4 + u
                                ksl = kT_s.t[:, t * 128:(t + 1) * 128]
                                qsl = qT_s.t[:, t * 128:(t + 1) * 128]
                                us = slice(u * 128, (u + 1) * 128)
                                S.op("pe", lambda e, ksl=ksl, us=us: e.matmul(bG.t[:, us], lhsT=ksl, rhs=ksl, start=True, stop=True),
                                     reads=[kT_s.b], writes=[bG.b])
                                S.op("pe", lambda e, ksl=ksl, qsl=qsl, us=us: e.matmul(bKQ.t[:, us], lhsT=ksl, rhs=qsl, start=True, stop=True),
                                     reads=[kT_s.b, qT_s.b], writes=[bKQ.b])
                                S.op("pe", lambda e, u=u, us=us: e.matmul(bC.t[:, us], lhsT=C32("ones"), rhs=dg.t[:, u, :], start=True, stop=False),
                                     reads=[dg.b, cf.b], writes=[bC.b])
                                S.op("pe", lambda e, u=u, us=us: e.matmul(bC.t[:, us], lhsT=dg.t[:, u, :], rhs=C32("negones"), start=False, stop=False),
                                     reads=[dg.b, cf.b], writes=[bC.b])
                                S.op("pe", lambda e, us=us: e.matmul(bC.t[:, us], lhsT=C32("ident"), rhs=C32("nm" + sfx), start=False, stop=True),
                                     reads=[cf.b], writes=[bC.b])
                                S.op("pe", lambda e, u=u, us=us: e.matmul(bD.t[:, us], lhsT=C32("ones"), rhs=dg.t[:, u, :], start=True, stop=True),
                                     reads=[dg.b, cf.b], writes=[bD.b])
                            S.op("act", lambda e: e.activation(out=v4(DTt), in_=bC.t[:], func=AF.Exp), reads=[bC.b], writes=[DTt.b])
                            S.op("act", lambda e: e.activation(out=v4(egr), in_=bD.t[:], func=AF.Exp), reads=[bD.b], writes=[egr.b])
                            AT = ATt[d_][slot]
                            S.op("dve", lambda e: e.tensor_tensor(out=v4(AT), in0=bKQ.t[:], in1=v4(DTt), op=ALU.mult),
                                 reads=[bKQ.b, DTt.b], writes=[AT.b])
                            S.op("dve", lambda e: e.tensor_tensor(
                                out=Gb.t[:], in0=q4(bG.t[:]), in1=bcast_u(beta_all.t[:, tsl, col:col + 1]), op=ALU.mult),
                                reads=[bG.b, beta_all.b], writes=[Gb.b])
                            QG = qgT[d_][slot]
                            S.op("dve", lambda e: e.tensor_tensor(out=v4(QG), in0=qT_s.t[:, g4 * 512:(g4 + 1) * 512], in1=v4(egr), op=ALU.mult),
                                 reads=[qT_s.b, egr.b], writes=[QG.b])
                            KD = kd[d_][slot]
                            S.op("pool", lambda e: e.tensor_tensor(
                                out=KD.t[:], in0=k_tok.t[:, tsl, :], in1=bcast_u(erest_all.t[:, tsl, col:col + 1]), op=ALU.mult),
                                reads=[k_tok.b, erest_all.b], writes=[KD.b])
                            for xi, cmn in enumerate(("cma", "cmb", "cmc")):
                                S.op("pool", lambda e, xi=xi, cmn=cmn: e.tensor_tensor(
                                    out=DTx[xi].t[:], in0=DTt.t[:], in1=C32(cmn + sfx).unsqueeze(1).to_broadcast([128, 4, 128]), op=ALU.mult),
                                    reads=[DTt.b, cf.b], writes=[DTx[xi].b])
                                eng = "dve" if xi == 0 else "pool"
                                S.op(eng, lambda e, xi=xi: e.tensor_tensor(out=MTx[xi][k2].t[:], in0=Gb.t[:], in1=DTx[xi].t[:], op=ALU.mult),
                                     reads=[Gb.b, DTx[xi].b], writes=[MTx[xi][k2].b])
                            for xi in range(3):
                                bk = bank()
                                bkv = bk.t[:].bitcast(BF16)
                                for u in range(4):
                                    S.op("pe", lambda e, bkv=bkv, u=u, xi=xi: e.transpose(
                                        out=bkv[:, u * 128:(u + 1) * 128], in_=MTx[xi][k2].t[:, u, :], identity=ident_bf),
                                        reads=[MTx[xi][k2].b, cb.b], writes=[bk.b])
                                S.op("act", lambda e, bkv=bkv, xi=xi: e.activation(out=v4(Mx[xi][k2]), in_=bkv[:, 0:512], func=AF.Copy),
                                     reads=[bk.b], writes=[Mx[xi][k2].b])
                            R_, RT_ = Rr[k2], RTt[k2]
                            S.op("pool", lambda e: e.tensor_tensor(out=R_.t[:], in0=Mx[0][k2].t[:], in1=idb, op=ALU.add),
                                 reads=[Mx[0][k2].b, cb.b], writes=[R_.b])
                            S.op("pool", lambda e: e.tensor_tensor(out=RT_.t[:], in0=MTx[0][k2].t[:], in1=idb, op=ALU.add),
                                 reads=[MTx[0][k2].b, cb.b], writes=[RT_.b])
                            Pc, PTc = Mx[0][k2], MTx[0][k2]
                            for kk in range(4):
                                Pn, PTn = Pk[kk % 2][k2], PTk[kk % 2][k2]
                                b1 = bank()
                                mm4(b1, PTc, Pc)
                                if kk < 3:
                                    b2 = bank()
                                    mm4(b2, Pc, PTc)
                                S.op("act", lambda e, b1=b1, Pn=Pn: e.activation(out=v4(Pn), in_=b1.t[:], func=AF.Copy), reads=[b1.b], writes=[Pn.b])
                                if kk < 3:
                                    S.op("dve", lambda e, b2=b2, PTn=PTn: e.tensor_copy(out=v4(PTn), in_=b2.t[:]), reads=[b2.b], writes=[PTn.b])
                                b3, b4 = bank(), bank()
                                mm4(b3, RT_, Pn)
                                mm4(b4, Pn, RT_)
                                S.op("dve", lambda e, b3=b3: e.tensor_tensor(out=v4(R_), in0=b3.t[:], in1=v4(R_), op=ALU.add),
                                     reads=[b3.b, R_.b], writes=[R_.b])
                                S.op("dve", lambda e, b4=b4: e.tensor_tensor(out=v4(RT_), in0=b4.t[:], in1=v4(RT_), op=ALU.add),
                                     reads=[b4.b, RT_.b], writes=[RT_.b])
                                Pc, PTc = Pn, PTn
                            b1 = bank()
                            mm4(b1, Mx[1][k2], RT_)
                            b2 = bank()
                            mm4(b2, MTx[1][k2], R_)
                            Z1, Z2 = Zz[k2], Pk[0][k2]
                            S.op("act", lambda e, b1=b1: e.activation(out=v4(Z1), in_=b1.t[:], func=AF.Copy), reads=[b1.b], writes=[Z1.b])
                            S.op("dve", lambda e, b2=b2: e.tensor_copy(out=v4(Z2), in_=b2.t[:]), reads=[b2.b], writes=[Z2.b])
                            b3, b4 = bank(), bank()
                            mm4(b3, R_, Z1)
                            mm4(b4, RT_, Z2)
                            S.op("dve", lambda e, b3=b3: e.tensor_tensor(out=v4(RT_), in0=v4(RT_), in1=b3.t[:], op=ALU.subtract),
                                 reads=[b3.b, RT_.b], writes=[RT_.b])
                            S.op("dve", lambda e, b4=b4: e.tensor_tensor(out=v4(R_), in0=v4(R_), in1=b4.t[:], op=ALU.subtract),
                                 reads=[b4.b, R_.b], writes=[R_.b])
                            b1 = bank()
                            mm4(b1, Mx[2][k2], RT_)
                            S.op("act", lambda e, b1=b1: e.activation(out=v4(Z1), in_=b1.t[:], func=AF.Copy), reads=[b1.b], writes=[Z1.b])
                            b3 = bank()
                            mm4(b3, R_, Z1)
                            NTg = NTt[d_][slot]
                            S.op("dve", lambda e, b3=b3: e.tensor_tensor(out=v4(NTg), in0=v4(RT_), in1=b3.t[:], op=ALU.subtract),
                                 reads=[b3.b, RT_.b], writes=[NTg.b])

                        def rec_step(h, d_, t, slot, o_written):
                            u = t % 4
                            col = d_ * 8 + h
                            ksl = kT_s.t[:, t * 128:(t + 1) * 128]
                            bk1 = bank()
                            S.op("pe", lambda e: e.matmul(bk1.t[:, 0:128], lhsT=ksl, rhs=Sbf[d_].t[:], start=True, stop=True),
                                 reads=[kT_s.b, Sbf[d_].b], writes=[bk1.b])
                            S.op("dve", lambda e: e.scalar_tensor_tensor(
                                out=rr_[d_].t[:], in0=bk1.t[:, 0:128], scalar=negegc_all.t[:, t, col:col + 1], in1=v_tok.t[:, t, :], op0=ALU.mult, op1=ALU.add),
                                reads=[bk1.b, negegc_all.b, v_tok.b], writes=[rr_[d_].b])
                            bk2 = bank()
                            S.op("pe", lambda e: e.matmul(bk2.t[:, 0:128], lhsT=NTt[d_][slot].t[:, u, :], rhs=rr_[d_].t[:], start=True, stop=True),
                                 reads=[NTt[d_][slot].b, rr_[d_].b], writes=[bk2.b])
                            S.op("act", lambda e: e.activation(out=vn_[d_].t[:], in_=bk2.t[:, 0:128], func=AF.Identity, scale=beta_all.t[:, t, col:col + 1]),
                                 reads=[bk2.b, beta_all.b], writes=[vn_[d_].b])
                            bk3 = bank()
                            S.op("pe", lambda e: e.matmul(bk3.t[:, 0:128], lhsT=qgT[d_][slot].t[:, u, :], rhs=Sbf[d_].t[:], start=True, stop=False),
                                 reads=[qgT[d_][slot].b, Sbf[d_].b], writes=[bk3.b])
                            S.op("pe", lambda e: e.matmul(bk3.t[:, 0:128], lhsT=ATt[d_][slot].t[:, u, :], rhs=vn_[d_].t[:], start=False, stop=True),
                                 reads=[ATt[d_][slot].b, vn_[d_].b], writes=[bk3.b])
                            bk4 = bank()
                            S.op("pe", lambda e: e.matmul(bk4.t[:, 0:128], lhsT=kd[d_][slot].t[:, u, :], rhs=vn_[d_].t[:], start=True, stop=True),
                                 reads=[kd[d_][slot].b, vn_[d_].b], writes=[bk4.b])
                            S.op("dve", lambda e: e.scalar_tensor_tensor(
                                out=S32[d_].t[:], in0=S32[d_].t[:], scalar=egl_all.t[:, t, col:col + 1], in1=bk4.t[:, 0:128], op0=ALU.mult, op1=ALU.add),
                                reads=[bk4.b, egl_all.b, S32[d_].b], writes=[S32[d_].b])
                            S.op("act", lambda e: e.activation(out=Sbf[d_].t[:], in_=S32[d_].t[:], func=AF.Copy), reads=[S32[d_].b], writes=[Sbf[d_].b])
                            if t not in o_written:
                                o_written.add(t)
                                S.op("dve", lambda e: e.tensor_copy(out=o_all.t[:, t, :], in_=bk3.t[:, 0:128]), reads=[bk3.b], writes=[o_all.b])
                            else:
                                S.op("dve", lambda e: e.tensor_tensor(out=o_all.t[:, t, :], in0=bk3.t[:, 0:128], in1=o_all.t[:, t, :], op=ALU.add),
                                     reads=[bk3.b, o_all.b], writes=[o_all.b])

                        for h in range(GH):
                            li = 0
                            for i3 in range(3):
                                chn = i3 * 8 + h
                                wt_i = chn // 4
                                cw = convw.t[:, chn, :]
                                for b in range(NB):
                                    tok = slice(b * 512, (b + 1) * 512)
                                    pr = praw[i3][b % 2]
                                    rd = [bpq[wt_i][b], bpq_halo]
                                    if b > 0:
                                        rd.append(bpq[wt_i][b - 1])
                                    if b < NB - 1:
                                        rd.append(bpq[wt_i][b + 1])
                                    S.load("sp", pr, pr.t[:], pqkv[chn * 128:(chn + 1) * 128, b * 512:b * 512 + 516], reads=rd)
                                    ac = acc_c[li % 2]
                                    li += 1
                                    S.op("dve", lambda e, pr=pr, cw=cw, ac=ac: e.tensor_scalar(
                                        out=ac.t[:], in0=pr.t[:, 0:512], scalar1=cw[:, 0:1], scalar2=None, op0=ALU.mult),
                                        reads=[pr.b, convw.b], writes=[ac.b])
                                    for j in range(1, 5):
                                        S.op("dve", lambda e, pr=pr, cw=cw, ac=ac, j=j: e.scalar_tensor_tensor(
                                            out=ac.t[:], in0=pr.t[:, j:j + 512], scalar=cw[:, j:j + 1], in1=ac.t[:],
                                            op0=ALU.mult, op1=ALU.add), reads=[pr.b, convw.b, ac.b], writes=[ac.b])
                                    if i3 == 2:
                                        vb = vT_b[b % 2]
                                        S.op("act", lambda e, ac=ac, vb=vb: e.activation(out=vb.t[:], in_=ac.t[:], func=AF.Silu), reads=[ac.b], writes=[vb.b])
                                        bk = bank()
                                        bkv = bk.t[:].bitcast(BF16)
                                        for u in range(4):
                                            S.op("pe", lambda e, bkv=bkv, u=u, vb=vb: e.transpose(out=bkv[:, u * 128:(u + 1) * 128], in_=vb.t[:, u * 128:(u + 1) * 128], identity=ident_bf),
                                                 reads=[vb.b, cb.b], writes=[bk.b])
                                        S.op("act", lambda e, bkv=bkv, b=b: e.activation(out=v_tok.t[:, b * 4:(b + 1) * 4, :], in_=q4(bkv[:, 0:512]), func=AF.Copy),
                                             reads=[bk.b], writes=[v_tok.b])
                                    else:
                                        dstT = qT_s if i3 == 0 else kT_s
                                        S.op("act", lambda e, ac=ac: e.activation(out=ac.t[:], in_=ac.t[:], func=AF.Silu), reads=[ac.b], writes=[ac.b])
                                        sh_ = sqh[b % 2]
                                        rsx = rs_h[b % 2]
                                        S.op("act", lambda e, ac=ac, sh_=sh_: e.activation(out=sh_.t[:], in_=ac.t[:], func=AF.Square), reads=[ac.b], writes=[sh_.b])
                                        bk2 = bank()
                                        S.op("pe", lambda e, bk2=bk2, sh_=sh_: e.matmul(bk2.t[:], lhsT=ones_bf, rhs=sh_.t[:], start=True, stop=True),
                                             reads=[sh_.b, cb.b], writes=[bk2.b])
                                        S.op("act", lambda e, bk2=bk2, rsx=rsx: e.activation(out=rsx.t[:], in_=bk2.t[:], func=AF.Sqrt, bias=epsc.t[:], scale=1.0),
                                             reads=[bk2.b, epsc.b], writes=[rsx.b])
                                        S.op("dve", lambda e, rsx=rsx: e.reciprocal(out=rsx.t[:], in_=rsx.t[:]), reads=[rsx.b], writes=[rsx.b])
                                        scq = (128.0 ** -0.5) if i3 == 0 else 1.0
                                        S.op("dve", lambda e, tok=tok, rsx=rsx, ac=ac, scq=scq, dstT=dstT: e.scalar_tensor_tensor(
                                            out=dstT.t[:, tok], in0=ac.t[:], scalar=scq, in1=rsx.t[:], op0=ALU.mult, op1=ALU.mult),
                                            reads=[ac.b, rsx.b], writes=[dstT.b])
                                        if i3 == 1:
                                            bk = bank()
                                            bkv = bk.t[:].bitcast(BF16)
                                            for u in range(4):
                                                t = b * 4 + u
                                                S.op("pe", lambda e, bkv=bkv, u=u, t=t: e.transpose(out=bkv[:, u * 128:(u + 1) * 128], in_=kT_s.t[:, t * 128:(t + 1) * 128], identity=ident_bf),
                                                     reads=[kT_s.b, cb.b], writes=[bk.b])
                                            S.op("act", lambda e, bkv=bkv, b=b: e.activation(out=k_tok.t[:, b * 4:(b + 1) * 4, :], in_=q4(bkv[:, 0:512]), func=AF.Copy),
                                                 reads=[bk.b], writes=[k_tok.b])
                            if h == 0 and l == 0:
                                dump("qT", qT_s, qT_s.t[:], [128, T], BF16)
                                dump("kT", kT_s, kT_s.t[:], [128, T], BF16)
                                dump("v_tok", v_tok, v_tok.t[:], [128, NT, 128], BF16)

                            for d_ in range(2):
                                S.op("pool", lambda e, d_=d_: e.memset(S32[d_].t[:], 0.0), writes=[S32[d_].b])
                                S.op("pool", lambda e, d_=d_: e.memset(Sbf[d_].t[:], 0.0), writes=[Sbf[d_].b])
                            o_written = set()
                            precompute(h, 0, 0, 0)
                            precompute(h, 1, NG - 1, 0)
                            if h == 0 and l == 0:
                                dump("NT_f", NTt[0][0], NTt[0][0].t[:], [128, 4, 128], BF16)
                                dump("AT_f", ATt[0][0], ATt[0][0].t[:], [128, 4, 128], BF16)
                                dump("NT_b", NTt[1][0], NTt[1][0].t[:], [128, 4, 128], BF16)
                            for gi in range(NG):
                                if gi + 1 < NG:
                                    precompute(h, 0, gi + 1, (gi + 1) % RING)
                                    precompute(h, 1, NG - 2 - gi, (gi + 1) % RING)
                                for s_ in range(4):
                                    rec_step(h, 0, gi * 4 + s_, gi % RING, o_written)
                                    rec_step(h, 1, (NG - 1 - gi) * 4 + 3 - s_, gi % RING, o_written)
                            if h == 0 and l == 0:
                                dump("o_all", o_all, o_all.t[:], [128, NT, 128])
                            for t in range(NT):
                                S.op("act", lambda e, t=t: e.activation(out=junk.t[:], in_=o_all.t[:, t, :], func=AF.Square, accum_out=ssq.t[:, t:t + 1]),
                                     reads=[o_all.b], writes=[junk.b, ssq.b])
                            S.op("act", lambda e: e.activation(out=ssq.t[:], in_=ssq.t[:], func=AF.Sqrt, bias=epsc.t[:], scale=1.0 / 128), reads=[ssq.b, epsc.b], writes=[ssq.b])
                            S.op("dve", lambda e: e.reciprocal(out=ssq.t[:], in_=ssq.t[:]), reads=[ssq.b], writes=[ssq.b])
                            for g4 in range(NG):
                                sz = szT[g4 % 2]
                                yt = yT[g4 % 2]
                                S.load("sp", sz, sz.t[:], szs[h * 128:(h + 1) * 128, g4 * 512:(g4 + 1) * 512], reads=[bsz[h // 4][g4]])
                                bk = bank()
                                bkv = bk.t[:].bitcast(BF16)
                                for u in range(4):
                                    t = g4 * 4 + u
                                    ob = on_bf[t % 2]
                                    S.op("dve", lambda e, t=t, ob=ob: e.tensor_scalar(out=ob.t[:], in0=o_all.t[:, t, :], scalar1=ssq.t[:, t:t + 1], scalar2=None, op0=ALU.mult),
                                         reads=[o_all.b, ssq.b], writes=[ob.b])
                                    S.op("pe", lambda e, bkv=bkv, u=u, ob=ob: e.transpose(out=bkv[:, u * 128:(u + 1) * 128], in_=ob.t[:], identity=ident_bf),
                                         reads=[ob.b, cb.b], writes=[bk.b])
                                S.op("dve", lambda e, bkv=bkv, sz=sz, yt=yt: e.scalar_tensor_tensor(
                                    out=yt.t[:], in0=bkv[:, 0:512], scalar=gnw_c.t[:, 0:1], in1=sz.t[:], op0=ALU.mult, op1=ALU.mult),
                                    reads=[bk.b, gnw_c.b, sz.b], writes=[yt.b])
                                S.store("sp", mixs[h * 128:(h + 1) * 128, g4 * 512:(g4 + 1) * 512], yt, yt.t[:], writes=[bmix_g[h]])
                        S.flush()

                with ExitStack() as sc:
                    sb = mk_sb(sc)
                    cm = make_common(sb, 3)
                    xt = sb([128, DC, 512], F32, "xT")
                    hTs = [sb([128, DC, 512], BF16, "hT")] * 2
                    actT = sb([128, FC, 512], BF16, "actT")
                    last = (l == depth - 1)
                    if last:
                        xtok = [sb([128, D], F32, "xtok") for _ in range(2)]
                    for b in range(NB):
                        tok = slice(b * 512, (b + 1) * 512)
                        mt = hTs[0]
                        h2 = hTs[1]
                        S.load("sp", xt, xt.t[:], xres_v[:, :, tok], reads=[bxres[b]])
                        S.load("sp", mt, mt.t[:], mix_v[:, :, tok], reads=bmix_g + [bmix_a[0][b], bmix_a[1][b]])
                        if b == 0 and l == 0:
                            dump("mixT", mt, mt.t[:], [128, DC, 512], BF16)
                        for wi in range(4):
                            wt = load_w(cm, w_out[l, :, wi * 512:(wi + 1) * 512])
                            for ch in range(4):
                                j = wi * 4 + ch
                                bk = bank()
                                for c in range(DC):
                                    S.op("pe", lambda e, bk=bk, wt=wt, c=c, ch=ch, mt=mt: e.matmul(
                                        bk.t[:], lhsT=wt.t[:, c, ch * 128:(ch + 1) * 128], rhs=mt.t[:, c, :], start=(c == 0), stop=(c == DC - 1)),
                                        reads=[wt.b, mt.b], writes=[bk.b])
                                S.op("dve", lambda e, bk=bk, j=j: e.scalar_tensor_tensor(
                                    out=xt.t[:, j, :], in0=bk.t[:], scalar=gt1[:, j:j + 1], in1=xt.t[:, j, :], op0=ALU.mult, op1=ALU.add),
                                    reads=[bk.b, xt.b, modT[l].b], writes=[xt.b])
                        norm_block(cm, xt, A2[l].t, sh2, h2)
                        for wi in range(16):
                            wt = load_w(cm, w_up[l, :, wi * 512:(wi + 1) * 512])
                            for ch in range(4):
                                f = wi * 4 + ch
                                bk = bank()
                                for c in range(DC):
                                    S.op("pe", lambda e, bk=bk, wt=wt, c=c, ch=ch: e.matmul(
                                        bk.t[:], lhsT=wt.t[:, c, ch * 128:(ch + 1) * 128], rhs=h2.t[:, c, :], start=(c == 0), stop=(c == DC - 1)),
                                        reads=[wt.b, h2.b], writes=[bk.b])
                                r1 = tmp(cm)
                                S.op("act", lambda e, bk=bk, r1=r1: e.activation(out=r1.t[:], in_=bk.t[:], func=AF.Relu), reads=[bk.b], writes=[r1.b])
                                S.op("dve", lambda e, r1=r1, f=f: e.tensor_tensor(out=actT.t[:, f, :], in0=r1.t[:], in1=r1.t[:], op=ALU.mult),
                                     reads=[r1.b], writes=[actT.b])
                        for jg in range(4):
                            bks = [bank() for _ in range(4)]
                            for fq in range(4):
                                wt = load_w(cm, w_down[l, fq * 2048:(fq + 1) * 2048, jg * 512:(jg + 1) * 512])
                                for ch in range(4):
                                    for fc in range(16):
                                        f = fq * 16 + fc
                                        S.op("pe", lambda e, bk=bks[ch], wt=wt, fc=fc, ch=ch, f=f: e.matmul(
                                            bk.t[:], lhsT=wt.t[:, fc, ch * 128:(ch + 1) * 128], rhs=actT.t[:, f, :], start=(f == 0), stop=(f == FC - 1)),
                                            reads=[wt.b, actT.b], writes=[bks[ch].b])
                            for ch in range(4):
                                j = jg * 4 + ch
                                S.op("dve", lambda e, bk=bks[ch], j=j: e.scalar_tensor_tensor(
                                    out=xt.t[:, j, :], in0=bk.t[:], scalar=gt2[:, j:j + 1], in1=xt.t[:, j, :], op0=ALU.mult, op1=ALU.add),
                                    reads=[bks[ch].b, xt.b, modT[l].b], writes=[xt.b])
                        if not last:
                            S.store("sp", xres_v[:, :, tok], xt, xt.t[:], writes=[bxres[b]])
                        else:
                            for tt in range(4):
                                ti = b * 4 + tt
                                xk = xtok[ti % 2]
                                for cg in range(4):
                                    bk = bank()
                                    for cc in range(4):
                                        c = cg * 4 + cc
                                        S.op("pe", lambda e, bk=bk, c=c, cc=cc, tt=tt: e.transpose(
                                            out=bk.t[:, cc * 128:(cc + 1) * 128], in_=xt.t[:, c, tt * 128:(tt + 1) * 128], identity=C32("ident")),
                                            reads=[xt.b, cf.b], writes=[bk.b])
                                    if cg % 2 == 0:
                                        S.op("act", lambda e, bk=bk, xk=xk, cg=cg: e.activation(out=xk.t[:, cg * 512:(cg + 1) * 512], in_=bk.t[:], func=AF.Copy),
                                             reads=[bk.b], writes=[xk.b])
                                    else:
                                        S.op("dve", lambda e, bk=bk, xk=xk, cg=cg: e.tensor_copy(out=xk.t[:, cg * 512:(cg + 1) * 512], in_=bk.t[:]),
                                             reads=[bk.b], writes=[xk.b])
                                S.store("sp", y_out[ti * 128:(ti + 1) * 128, :], xk, xk.t[:])
                    S.flush()
    except _Stop:
        pass
    return nc, dbg_out


_PROG_CACHE = {}


def _core_inputs(x_seq, c_vec, T, shared):
    tv = x_seq.shape[0]
    xp = np.zeros((T, D), np.float32)
    xp[:tv] = x_seq
    mask = np.zeros((T,), np.float32)
    mask[:tv] = 1.0
    m = dict(shared)
    m["x"] = xp
    m["c"] = np.ascontiguousarray(c_vec, dtype=np.float32)
    m["mask_row"] = mask.reshape(1, T).copy()
    m["mask_col"] = np.ascontiguousarray(mask.reshape(T // 128, 128).T)
    return m


def run_trunk(seqs, cvecs, weights, T, depth, dbg=(), n_cores=None):
    key = (T, depth, tuple(dbg))
    if key not in _PROG_CACHE:
        _PROG_CACHE[key] = build_program(T, depth, dbg)
    nc, dbg_out = _PROG_CACHE[key]
    names32, cf_np, cb_np, cosT, sinT = make_consts(T)
    shared = {
        "ada_w": weights["ada_w"], "ada_b": weights["ada_b"], "norm1_w": weights["norm1_w"], "norm2_w": weights["norm2_w"],
        "w_in": weights["w_in"], "conv_w": weights["conv_w"],
        "a_log": np.ascontiguousarray(weights["a_log"].reshape(depth, 16)),
        "dt_bias": np.ascontiguousarray(weights["dt_bias"].reshape(depth, 16)),
        "gdn_norm_w": weights["gdn_norm_w"], "q_norm_w": weights["q_norm_w"], "k_norm_w": weights["k_norm_w"],
        "w_out": weights["w_out"], "w_up": weights["w_up"], "w_down": weights["w_down"],
        "cf": cf_np, "cb": cb_np, "cosT": cosT, "sinT": sinT,
    }
    shared = {k: np.ascontiguousarray(np.asarray(v, dtype=np.float32)) for k, v in shared.items()}
    in_maps = [_core_inputs(s, c, T, shared) for s, c in zip(seqs, cvecs)]
    res = run_bass_kernel_spmd(nc, in_maps, core_ids=list(range(len(in_maps))))
    return res


def kernel(x_prompt, x_sample, c_prompt, c_sample, ada_w, ada_b, norm1_w, norm2_w, w_in, conv_w, a_log, dt_bias,
           gdn_norm_w, q_norm_w, k_norm_w, w_out, w_up, w_down):
    x_prompt = np.asarray(x_prompt, np.float32)
    x_sample = np.asarray(x_sample, np.float32)
    c_prompt = np.asarray(c_prompt, np.float32)
    c_sample = np.asarray(c_sample, np.float32)
    depth = int(np.asarray(ada_w).shape[0])
    T = x_prompt.shape[1]
    weights = dict(ada_w=ada_w, ada_b=ada_b, norm1_w=norm1_w, norm2_w=norm2_w, w_in=w_in, conv_w=conv_w, a_log=np.asarray(a_log),
                   dt_bias=np.asarray(dt_bias), gdn_norm_w=gdn_norm_w, q_norm_w=q_norm_w, k_norm_w=k_norm_w, w_out=w_out, w_up=w_up,
                   w_down=w_down)
    seqs = [x_prompt[i] for i in range(4)] + [x_sample[i] for i in range(4)]
    cvecs = [c_prompt[i] for i in range(4)] + [c_sample[i] for i in range(4)]
    res = run_trunk(seqs, cvecs, weights, T, depth)
    ys = [r["y"] for r in res.results]
    y_prompt = np.stack([ys[i] for i in range(4)], axis=0).astype(np.float32)
    ts = x_sample.shape[1]
    y_sample = np.stack([ys[4 + i][:ts] for i in range(4)], axis=0).astype(np.float32)
    return (y_prompt, y_sample)
```

```python
import math
from contextlib import ExitStack

import numpy as np
import concourse.bass as bass
import concourse.mybir as mybir
from concourse.bass_utils import run_bass_kernel_spmd

F32 = mybir.dt.float32
BF16 = mybir.dt.bfloat16
ALU = mybir.AluOpType
AF = mybir.ActivationFunctionType
AX = mybir.AxisListType

D = 2048
DC = 16
HD = 128
GH = 8
AH = 8
KVH = 2
DFF = 8192
FC = 64
IN_COLS = 5664
EPS = 1e-6
NEG = -30000.0


class _Stop(Exception):
    pass


STOP_AFTER = [0]
CKPT = [{"PA.params", "PA.mrow_zpad", "PA.norm", "PA.wtiles", "PA.ba", "PA.gates"}]


class Buf:
    __slots__ = ("name", "w", "r")

    def __init__(self, name=""):
        self.name = name
        self.w = None
        self.r = []


class Sched:
    ENG = ("pe", "act", "dve", "pool", "sp")

    def __init__(self, nc, es):
        self.nc = nc
        self.es = es
        self.ops = {k: [] for k in self.ENG}
        self.esem = {k: es.enter_context(nc.semaphore("s_" + k)) for k in self.ENG}
        self.ebase = {k: 0 for k in self.ENG}
        self.dsem = {}
        self.dcount = {}
        self.ddirty = set()
        self.gen = 0
        self.nops = 0

    def _deps(self, reads, writes):
        g = self.gen
        deps = []
        for b in reads:
            if b.w is not None and b.w[3] == g:
                deps.append(b.w)
        for b in writes:
            if b.w is not None and b.w[3] == g:
                deps.append(b.w)
            for t in b.r:
                if t[3] == g:
                    deps.append(t)
        return deps

    def _mark(self, tok, reads, writes):
        for b in reads:
            b.r.append(tok)
            if len(b.r) > 48:
                b.r = b.r[-48:]
        for b in writes:
            b.w = tok
            b.r = []

    def op(self, eng, fn, reads=(), writes=()):
        deps = self._deps(reads, writes)
        lst = self.ops[eng]
        tok = ("E", eng, len(lst), self.gen)
        lst.append([fn, deps, None, False])
        self._mark(tok, reads, writes)
        return tok

    def dma(self, q, out, in_, reads=(), writes=(), key=None, **kw):
        deps = self._deps(reads, writes)
        if key not in self.dsem:
            self.dsem[key] = self.es.enter_context(self.nc.semaphore("d_%d" % len(self.dsem)))
            self.dcount[key] = 0
        self.dcount[key] += 16
        self.ddirty.add(key)
        tok = ("D", key, self.dcount[key], self.gen)
        sem = self.dsem[key]

        def fn(e, out=out, in_=in_, kw=kw):
            return e.dma_start(out=out, in_=in_, **kw)
        self.ops[q].append([fn, deps, (sem, 16), False])
        self._mark(tok, reads, writes)
        return tok

    def load(self, q, tl, dst_ap, src_ap, reads=(), **kw):
        return self.dma(q, dst_ap, src_ap, reads=reads, writes=[tl.b], key="ld_" + tl.b.name, **kw)

    def store(self, q, dst_ap, tl, src_ap, writes=(), **kw):
        return self.dma(q, dst_ap, src_ap, reads=[tl.b], writes=writes, key="st_" + tl.b.name, **kw)

    def flush(self):
        for eng, lst in self.ops.items():
            for rec in lst:
                for d in rec[1]:
                    if d[0] == "E" and not (d[1] == eng and eng == "pe"):
                        self.ops[d[1]][d[2]][3] = True
        cum = {}
        for eng, lst in self.ops.items():
            c = self.ebase[eng]
            arr = []
            for rec in lst:
                if rec[3]:
                    c += 1
                arr.append(c)
            cum[eng] = arr
        nc = self.nc
        with nc.Block() as block:
            def run(eng, e):
                seen = {}
                for rec in self.ops[eng]:
                    need = {}
                    for d in rec[1]:
                        if d[0] == "E":
                            if d[1] == eng and eng == "pe":
                                continue
                            s = ("E", d[1])
                            v = cum[d[1]][d[2]]
                        else:
                            s = ("D", d[1])
                            v = d[2]
                        if seen.get(s, 0) >= v:
                            continue
                        if need.get(s, 0) < v:
                            need[s] = v
                    for s, v in need.items():
                        sem = self.esem[s[1]] if s[0] == "E" else self.dsem[s[1]]
                        e.wait_ge(sem, v)
                        seen[s] = v
                    ins = rec[0](e)
                    if rec[2] is not None:
                        ins.then_inc(rec[2][0], rec[2][1])
                    elif rec[3]:
                        ins.then_inc(self.esem[eng], 1)
                if eng == "sp":
                    for key in sorted(self.ddirty):
                        e.wait_ge(self.dsem[key], self.dcount[key])

            block.tensor(lambda e: run("pe", e))
            block.scalar(lambda e: run("act", e))
            block.vector(lambda e: run("dve", e))
            block.gpsimd(lambda e: run("pool", e))
            block.sync(lambda e: run("sp", e))
        for eng in self.ENG:
            self.nops += len(self.ops[eng])
            self.ebase[eng] = cum[eng][-1] if cum[eng] else self.ebase[eng]
            self.ops[eng] = []
        self.ddirty = set()
        self.gen += 1
        if STOP_AFTER[0] and self.gen == STOP_AFTER[0]:
            raise _Stop()


class Tl:
    __slots__ = ("t", "b")

    def __init__(self, t, name=""):
        self.t = t
        self.b = Buf(name)


def make_consts(T):
    i = np.arange(128)
    J, I = np.meshgrid(i, i, indexing="ij")
    c32 = {}
    c32["ident"] = (J == I)
    c32["ones"] = np.ones((128, 128))
    c32["negones"] = -np.ones((128, 128))
    c32["cum_f"] = (J <= I)
    c32["cum_b"] = (J >= I)
    c32["rest_f"] = (J > I)
    c32["rest_b"] = (J < I)
    c32["nm_f"] = np.where(I >= J, 0.0, NEG)
    c32["nm_b"] = np.where(I <= J, 0.0, NEG)
    bd32 = (J // 32 == I // 32)
    bd64 = (J // 64 == I // 64)
    su = (I > J)
    sl = (I < J)
    for nm, st in (("f", su), ("b", sl)):
        c32["cma_" + nm] = -(st & bd32).astype(np.float64)
        c32["cmb_" + nm] = (st & bd64 & ~bd32)
        c32["cmc_" + nm] = (st & ~bd64)
    names32 = list(c32.keys())
    cf = np.concatenate([np.asarray(c32[k], np.float32) for k in names32], axis=1)
    Rl = np.zeros((128, 128), np.float32)
    for d in range(128):
        q = d // 32
        if q % 2 == 0:
            Rl[d + 32, d] = -1.0
        else:
            Rl[d - 32, d] = 1.0
    cb = np.concatenate([np.eye(128, dtype=np.float32), np.ones((128, 128), np.float32), Rl], axis=1)
    rows = T // 64
    row = np.repeat(np.arange(rows, dtype=np.float32), 64)
    col = np.tile(np.arange(64, dtype=np.float32), rows)
    inv = (10000.0 ** (-np.arange(0, 64, 2, dtype=np.float32) / 64)).astype(np.float32)
    ar = row[:, None] * inv[None, :]
    ac = col[:, None] * inv[None, :]
    ang = np.concatenate([ar, ar, ac, ac], axis=-1)
    cosT = np.ascontiguousarray(np.cos(ang).T.astype(np.float32))
    sinT = np.ascontiguousarray(np.sin(ang).T.astype(np.float32))
    return names32, cf, cb, cosT, sinT


def build_program(T, depth, dbg=()):
    assert T % 512 == 0
    NT = T // 128
    NB = T // 512
    NG = NT // 4
    names32, cf_np, cb_np, _, _ = make_consts(T)
    NCF = cf_np.shape[1]
    nc = bass.Bass("TRN2", target_bir_lowering=False)

    def din(name, shape, dt=F32):
        return nc.dram_tensor(name, list(shape), dt, kind="ExternalInput").ap()

    def dscr(name, shape, dt=F32):
        return nc.dram_tensor(name, list(shape), dt, kind="Internal").ap()

    x_in = din("x", [T, D])
    c_in = din("c", [D])
    ada_w = din("ada_w", [depth, D, 6 * D])
    ada_b = din("ada_b", [depth, 6 * D])
    n1w = din("norm1_w", [depth, D])
    n2w = din("norm2_w", [depth, D])
    w_in = din("w_in", [depth, D, IN_COLS])
    conv_w = din("conv_w", [depth, 5, 3072])
    a_log = din("a_log", [depth, 16])
    dt_bias = din("dt_bias", [depth, 16])
    gnw = din("gdn_norm_w", [depth, 128])
    qnw = din("q_norm_w", [depth, 128])
    knw = din("k_norm_w", [depth, 128])
    w_out = din("w_out", [depth, D, D])
    w_up = din("w_up", [depth, D, DFF])
    w_down = din("w_down", [depth, DFF, D])
    cf_in = din("cf", [128, NCF])
    cb_in = din("cb", [128, 384])
    cos_in = din("cosT", [128, T])
    sin_in = din("sinT", [128, T])
    mrow_in = din("mask_row", [1, T])
    mcol_in = din("mask_col", [128, NT])
    y_out = nc.dram_tensor("y", [T, D], F32, kind="ExternalOutput").ap()
    dbg_out = {}

    xres = dscr("xres", [D, T])
    pqkv = dscr("pqkv", [3072, T + 4])
    szs = dscr("szs", [1024, T], BF16)
    aqs = dscr("aqs", [1024, T], BF16)
    mixs = dscr("mixs", [D, T], BF16)
    modrow = dscr("modrow", [depth, 6 * D])
    xres_v = xres.rearrange("(c p) t -> p c t", p=128)
    pq_v = pqkv.rearrange("(c p) t -> p c t", p=128)
    sz_v = szs.rearrange("(c p) t -> p c t", p=128)
    aq_v = aqs.rearrange("(c p) t -> p c t", p=128)
    mix_v = mixs.rearrange("(c p) t -> p c t", p=128)
    bxres = [Buf("xres%d" % b) for b in range(NB)]
    bpq = [[Buf("pq") for b in range(NB)] for _ in range(6)]
    bpq_halo = Buf("pqh")
    bsz = [[Buf("sz") for b in range(NB)] for _ in range(2)]
    baq = [[Buf("aq") for b in range(NB)] for _ in range(2)]
    bmix_g = [Buf("mixg") for _ in range(GH)]
    bmix_a = [[Buf("mixa") for b in range(NB)] for _ in range(KVH)]
    bmod = Buf("modrow")

    es = ExitStack()
    try:
        with es:
            S = Sched(nc, es)
            cnt = [0]

            def ckpt(tag):
                if CKPT[0] is True or (CKPT[0] and tag in CKPT[0]):
                    S.flush()

            def mk_sb(scope):
                def sb(shape, dt=F32, name="t"):
                    cnt[0] += 1
                    nm = "%s_%d" % (name, cnt[0])
                    return Tl(scope.enter_context(nc.sbuf_tensor(nm, list(shape), dt)), nm)
                return sb
            sb0 = mk_sb(es)

            banks = [Tl(es.enter_context(nc.psum_tensor("bank%d" % i, [128, 512], F32)), "bank%d" % i) for i in range(8)]
            bank_i = [0]

            def bank():
                b = banks[bank_i[0] % 8]
                bank_i[0] += 1
                return b

            def v4(tl):
                return tl.t[:].rearrange("p a c -> p (a c)")

            def q4(ap):
                return ap.rearrange("p (a c) -> p a c", a=4)

            cf = sb0([128, NCF], F32, "cf")
            cb = sb0([128, 384], BF16, "cb")
            epsc = sb0([128, 1], F32, "eps")
            onec = sb0([128, 1], F32, "one")
            mcol = sb0([128, NT], F32, "mcol")
            S.load("sp", cf, cf.t[:], cf_in)
            S.load("sp", mcol, mcol.t[:], mcol_in)
            with ExitStack() as sc:
                sb = mk_sb(sc)
                cbf = sb([128, 384], F32, "cbf")
                S.load("sp", cbf, cbf.t[:], cb_in)
                S.op("dve", lambda e: e.tensor_copy(out=cb.t[:], in_=cbf.t[:]), reads=[cbf.b], writes=[cb.b])
                S.op("pool", lambda e: e.memset(epsc.t[:], EPS), writes=[epsc.b])
                S.op("pool", lambda e: e.memset(onec.t[:], 1.0), writes=[onec.b])
                S.flush()

            def C32(name):
                k = names32.index(name)
                return cf.t[:, k * 128:(k + 1) * 128]
            ident_bf = cb.t[:, 0:128]
            ones_bf = cb.t[:, 128:256]
            rl_bf = cb.t[:, 256:384]

            def dump(name, src_tl, src_ap, shape, dt=F32):
                if name in dbg and name not in dbg_out:
                    o = nc.dram_tensor("dbg_" + name, list(shape), dt, kind="ExternalOutput").ap()
                    dbg_out[name] = o
                    S.dma("sp", o, src_ap, reads=[src_tl.b], key="dbg_" + name)


            def load_cols(sbf, dst_tl, dst_ap, src_1d, n, reads=()):
                tmpT = sbf([n, 128], F32, "lc")
                S.load("sp", tmpT, tmpT.t[:], src_1d.rearrange("(c p) -> c p", p=128), reads=reads)
                bk = bank()
                S.op("pe", lambda e: e.transpose(out=bk.t[:, 0:n], in_=tmpT.t[:], identity=cf.t[0:n, 0:n]),
                     reads=[tmpT.b, cf.b], writes=[bk.b])
                S.op("dve", lambda e: e.tensor_copy(out=dst_ap, in_=bk.t[:, 0:n]), reads=[bk.b], writes=[dst_tl.b])

            def bload(tl, src_row):
                S.load("sp", tl, tl.t[:].unsqueeze(1), src_row.partition_broadcast(128))

            modT = [sb0([128, 6 * DC], F32, "modT") for _ in range(depth)]
            A1 = [sb0([128, DC], F32, "A1") for _ in range(depth)]
            A2 = [sb0([128, DC], F32, "A2") for _ in range(depth)]

            with ExitStack() as sc:
                sb = mk_sb(sc)
                xtok = [sb([128, D], F32, "xtok") for _ in range(2)]
                xTd = [sb([128, DC, 512], F32, "xT") for _ in range(2)]
                for b in range(NB):
                    xt = xTd[b % 2]
                    for tt in range(4):
                        ti = b * 4 + tt
                        xk = xtok[ti % 2]
                        S.load("sp", xk, xk.t[:], x_in[ti * 128:(ti + 1) * 128, :])
                        for cg in range(4):
                            bk = bank()
                            for cc in range(4):
                                c = cg * 4 + cc
                                S.op("pe", lambda e, bk=bk, xk=xk, c=c, cc=cc: e.transpose(
                                    out=bk.t[:, cc * 128:(cc + 1) * 128], in_=xk.t[:, c * 128:(c + 1) * 128], identity=C32("ident")),
                                    reads=[xk.b, cf.b], writes=[bk.b])
                            outv = xt.t[:, cg * 4:(cg + 1) * 4, tt * 128:(tt + 1) * 128]
                            inv = q4(bk.t[:])
                            if cg % 2 == 0:
                                S.op("act", lambda e, o=outv, i=inv: e.activation(out=o, in_=i, func=AF.Copy), reads=[bk.b], writes=[xt.b])
                            else:
                                S.op("dve", lambda e, o=outv, i=inv: e.tensor_copy(out=o, in_=i), reads=[bk.b], writes=[xt.b])
                    S.store("sp", xres_v[:, :, b * 512:(b + 1) * 512], xt, xt.t[:], writes=[bxres[b]])
                S.flush()

            with ExitStack() as sc:
                sb = mk_sb(sc)
                cT = sb([128, DC], F32, "cT")
                sc_ = sb([128, DC], F32, "silu_c")
                adaw_t = [sb([128, 2048], F32, "adaw") for _ in range(3)]
                modr = sb([1, 6 * D], F32, "modr")
                adab = sb([1, 6 * D], F32, "adab")
                n1T = [sb([128, DC], F32, "n1T") for _ in range(depth)]
                n2T = [sb([128, DC], F32, "n2T") for _ in range(depth)]
                load_cols(sb, cT, cT.t[:], c_in, DC)
                S.op("act", lambda e: e.activation(out=sc_.t[:], in_=cT.t[:], func=AF.Silu), reads=[cT.b], writes=[sc_.b])
                ai = 0
                for l in range(depth):
                    S.load("sp", adab, adab.t[:], ada_b[l:l + 1, :])
                    for g4 in range(6):
                        bks = [bank() for _ in range(4)]
                        for kc in range(DC):
                            wt = adaw_t[ai % 3]
                            ai += 1
                            S.load("sp", wt, wt.t[:], ada_w[l, kc * 128:(kc + 1) * 128, g4 * 2048:(g4 + 1) * 2048])
                            for q in range(4):
                                S.op("pe", lambda e, bk=bks[q], wt=wt, kc=kc, q=q: e.matmul(
                                    bk.t[0:1, :], lhsT=sc_.t[:, kc:kc + 1], rhs=wt.t[:, q * 512:(q + 1) * 512],
                                    start=(kc == 0), stop=(kc == DC - 1)), reads=[sc_.b, wt.b], writes=[bks[q].b])
                        for q in range(4):
                            col = g4 * 2048 + q * 512
                            S.op("dve", lambda e, bk=bks[q], col=col: e.tensor_tensor(
                                out=modr.t[0:1, col:col + 512], in0=bk.t[0:1, :], in1=adab.t[0:1, col:col + 512], op=ALU.add),
                                reads=[bks[q].b, adab.b], writes=[modr.b])
                    S.store("sp", modrow[l:l + 1, :], modr, modr.t[:], writes=[bmod])
                    load_cols(sb, modT[l], modT[l].t[:], modrow[l], 6 * DC, reads=[bmod])
                    load_cols(sb, n1T[l], n1T[l].t[:], n1w[l], DC)
                    load_cols(sb, n2T[l], n2T[l].t[:], n2w[l], DC)
                    S.op("dve", lambda e, l=l: e.scalar_tensor_tensor(out=A1[l].t[:], in0=modT[l].t[:, DC:2 * DC], scalar=1.0,
                                                                      in1=n1T[l].t[:], op0=ALU.add, op1=ALU.mult),
                         reads=[modT[l].b, n1T[l].b], writes=[A1[l].b])
                    S.op("dve", lambda e, l=l: e.scalar_tensor_tensor(out=A2[l].t[:], in0=modT[l].t[:, 4 * DC:5 * DC], scalar=1.0,
                                                                      in1=n2T[l].t[:], op0=ALU.add, op1=ALU.mult),
                         reads=[modT[l].b, n2T[l].b], writes=[A2[l].b])
                if "mod" in dbg:
                    dump("mod", modT[0], modT[0].t[:], [128, 6 * DC])
                S.flush()

            def make_common(sb, nw):
                cm = {}
                cm["sq"] = sb([128, 8, 512], BF16, "sq")
                cm["rstd"] = sb([128, 512], F32, "rstd")
                cm["tmpf"] = [sb([128, 512], F32, "tmpf") for _ in range(3)]
                cm["tmpi"] = 0
                cm["wbuf"] = [sb([128, DC, 512], BF16, "wbuf") for _ in range(nw)]
                cm["wbi"] = 0
                return cm

            def tmp(cm):
                t = cm["tmpf"][cm["tmpi"] % 3]
                cm["tmpi"] += 1
                return t

            def load_w(cm, src_ap, kparts=DC, cols=512):
                n = len(cm["wbuf"])
                wt = cm["wbuf"][cm["wbi"] % n]
                cm["wbi"] += 1
                S.load("pool", wt, wt.t[:, 0:kparts, 0:cols], src_ap.rearrange("(c p) m -> p c m", p=128))
                return wt

            def norm_block(cm, xt, Acol, Bcol, ht):
                sq, rstd = cm["sq"], cm["rstd"]
                bk = bank()
                for half in range(2):
                    S.op("act", lambda e, half=half: e.activation(out=sq.t[:], in_=xt.t[:, half * 8:(half + 1) * 8, :], func=AF.Square),
                         reads=[xt.b], writes=[sq.b])
                    for c8 in range(8):
                        c = half * 8 + c8
                        S.op("pe", lambda e, c=c, c8=c8, bk=bk: e.matmul(bk.t[:], lhsT=ones_bf, rhs=sq.t[:, c8, :], start=(c == 0), stop=(c == DC - 1)),
                             reads=[sq.b, cb.b], writes=[bk.b])
                S.op("act", lambda e, bk=bk: e.activation(out=rstd.t[:], in_=bk.t[:], func=AF.Sqrt, bias=epsc.t[:], scale=1.0 / D),
                     reads=[bk.b, epsc.b], writes=[rstd.b])
                S.op("dve", lambda e: e.reciprocal(out=rstd.t[:], in_=rstd.t[:]), reads=[rstd.b], writes=[rstd.b])
                for c in range(DC):
                    tp = tmp(cm)
                    S.op("dve", lambda e, c=c, tp=tp: e.scalar_tensor_tensor(out=tp.t[:], in0=xt.t[:, c, :], scalar=Acol[:, c:c + 1],
                                                                             in1=rstd.t[:], op0=ALU.mult, op1=ALU.mult),
                         reads=[xt.b, rstd.b], writes=[tp.b])
                    S.op("act", lambda e, c=c, tp=tp: e.activation(out=ht.t[:, c, :], in_=tp.t[:], func=AF.Identity, bias=Bcol[:, c:c + 1]),
                         reads=[tp.b], writes=[ht.b])

            for l in range(depth):
                mod = modT[l].t
                sh1 = mod[:, 0:DC]
                gt1 = mod[:, 2 * DC:3 * DC]
                sh2 = mod[:, 3 * DC:4 * DC]
                gt2 = mod[:, 5 * DC:6 * DC]
                ly = ExitStack()
                with ly:
                    sbl = mk_sb(ly)
                    gb_all = sbl([128, NT, 32], F32, "gb_all")
                    beta_all = sbl([128, NT, 16], F32, "beta_all")
                    g_all = sbl([128, NT, 16], F32, "g_all")
                    gc_all = sbl([128, NT, 16], F32, "gc_all")
                    negegc_all = sbl([128, NT, 16], F32, "negegc")
                    erest_all = sbl([128, NT, 16], F32, "erest")
                    egl_all = sbl([128, NT, 16], F32, "egl")
                    alog_r = sbl([128, 16], F32, "alog")
                    dtb_r = sbl([128, 16], F32, "dtb")
                    nexpalog = sbl([128, 16], F32, "nexpalog")
                    qnw_c = sbl([128, 1], F32, "qnw_c")
                    knw_c = sbl([128, 1], F32, "knw_c")
                    gnw_c = sbl([128, 1], F32, "gnw_c")
                    qnw_r = sbl([128, 128], F32, "qnw_r")
                    knw_r = sbl([128, 128], F32, "knw_r")
                    kbias = sbl([128, NT], F32, "kbias")
                    mq = sbl([128, 1], F32, "mq")
                    mk = sbl([128, 1], F32, "mk")
                    convw = sbl([128, 24, 5], F32, "convw")
                    att = ExitStack()
                    with att:
                        sba = mk_sb(att)
                        kT_att = sba([128, KVH, T], BF16, "kT_att")
                        v_att = sba([128, NT, 256], BF16, "v_att")
                        with ExitStack() as sc:
                            sb = mk_sb(sc)
                            cm = make_common(sb, 2)
                            with nc.allow_non_contiguous_dma(reason="tiny param loads"):
                                bload(alog_r, a_log[l:l + 1, :])
                                bload(dtb_r, dt_bias[l:l + 1, :])
                                S.load("sp", qnw_c, qnw_c.t[:], qnw[l].rearrange("(p o) -> p o", o=1))
                                S.load("sp", knw_c, knw_c.t[:], knw[l].rearrange("(p o) -> p o", o=1))
                                S.load("sp", gnw_c, gnw_c.t[:], gnw[l].rearrange("(p o) -> p o", o=1))
                                bload(qnw_r, qnw[l:l + 1, :])
                                bload(knw_r, knw[l:l + 1, :])
                            cw5 = sb([5, 3072], F32, "cw5")
                            S.load("sp", cw5, cw5.t[:], conv_w[l])
                            bkc = bank()
                            for c in range(24):
                                S.op("pe", lambda e, c=c: e.transpose(out=bkc.t[:, c * 5:(c + 1) * 5], in_=cw5.t[0:5, c * 128:(c + 1) * 128], identity=cf.t[0:5, 0:5]),
                                     reads=[cw5.b, cf.b], writes=[bkc.b])
                            S.op("dve", lambda e: e.tensor_copy(out=convw.t[:], in_=bkc.t[:, 0:120].rearrange("p (c k) -> p c k", k=5)), reads=[bkc.b], writes=[convw.b])
                            S.op("act", lambda e: e.activation(out=nexpalog.t[:], in_=alog_r.t[:], func=AF.Exp), reads=[alog_r.b], writes=[nexpalog.b])
                            S.op("dve", lambda e: e.tensor_scalar(out=nexpalog.t[:], in0=nexpalog.t[:], scalar1=-1.0, scalar2=None, op0=ALU.mult),
                                 reads=[nexpalog.b], writes=[nexpalog.b])
                            S.op("dve", lambda e: e.tensor_reduce(out=mq.t[:], in_=qnw_r.t[:], axis=AX.X, op=ALU.max, apply_absolute_value=True),
                                 reads=[qnw_r.b], writes=[mq.b])
                            S.op("dve", lambda e: e.tensor_reduce(out=mk.t[:], in_=knw_r.t[:], axis=AX.X, op=ALU.max, apply_absolute_value=True),
                                 reads=[knw_r.b], writes=[mk.b])
                            S.op("dve", lambda e: e.scalar_tensor_tensor(out=mq.t[:], in0=mq.t[:], scalar=-math.sqrt(128.0), in1=mk.t[:],
                                                                         op0=ALU.mult, op1=ALU.mult), reads=[mq.b, mk.b], writes=[mq.b])
                            S.op("dve", lambda e: e.tensor_scalar(out=kbias.t[:], in0=mcol.t[:], scalar1=-NEG, scalar2=NEG, op0=ALU.mult, op1=ALU.add),
                                 reads=[mcol.b], writes=[kbias.b])
                            S.op("dve", lambda e: e.tensor_scalar(out=kbias.t[:], in0=kbias.t[:], scalar1=mq.t[:, 0:1], scalar2=None, op0=ALU.add),
                                 reads=[kbias.b, mq.b], writes=[kbias.b])

                            ckpt("PA.params")
                            xt = sb([128, DC, 512], F32, "xT")
                            hTs = [sb([128, DC, 512], BF16, "hT")] * 2
                            mrow32 = sb([128, 512], F32, "mrow32")
                            cosT = sb([128, 512], F32, "cosT")
                            sinT = sb([128, 512], F32, "sinT")
                            ckpt("PA.mrow_zpad")
                            stage = [sb([128, 4, 516], F32, "stage")] * 2
                            for st_ in stage[:1]:
                                S.op("pool", lambda e, st_=st_: e.memset(st_.t[:], 0.0), writes=[st_.b])
                            stage_bf = [sb([128, 4, 512], BF16, "stage_bf")] * 2
                            qn_bf = [sb([128, 512], BF16, "qn_bf") for _ in range(2)]
                            sqh = [sb([128, 512], BF16, "sqh") for _ in range(2)]
                            rs_h = [sb([128, 512], F32, "rs_h") for _ in range(2)]
                            sti = 0
                            for b in range(NB):
                                tok = slice(b * 512, (b + 1) * 512)
                                ht = hTs[b % 2]
                                if b == 0:
                                    S.load("sp", xt, xt.t[:], xres_v[:, :, tok], reads=[bxres[b]])
                                bload(mrow32, mrow_in[:, tok])
                                S.load("sp", cosT, cosT.t[:], cos_in[:, tok])
                                S.load("sp", sinT, sinT.t[:], sin_in[:, tok])
                                norm_block(cm, xt, A1[l].t, sh1, ht)
                                if b + 1 < NB:
                                    S.load("sp", xt, xt.t[:], xres_v[:, :, (b + 1) * 512:(b + 2) * 512], reads=[bxres[b + 1]])
                                ckpt("PA.norm")
                                if b == 0 and l == 0:
                                    dump("hT", ht, ht.t[:], [128, DC, 512], BF16)
                                for wt_i in range(11):
                                    col0 = wt_i * 512 if wt_i < 8 else 4128 + (wt_i - 8) * 512
                                    wt = load_w(cm, w_in[l, :, col0:col0 + 512])
                                    chunks = 2 if wt_i == 10 else 4
                                    st = stage[sti % 2]
                                    stb = stage_bf[sti % 2]
                                    sti += 1
                                    for ch in range(chunks):
                                        bk = bank()
                                        for c in range(DC):
                                            S.op("pe", lambda e, bk=bk, wt=wt, c=c, ch=ch, ht=ht: e.matmul(
                                                bk.t[:], lhsT=wt.t[:, c, ch * 128:(ch + 1) * 128], rhs=ht.t[:, c, :], start=(c == 0), stop=(c == DC - 1)),
                                                reads=[wt.b, ht.b], writes=[bk.b])
                                        if wt_i < 6:
                                            S.op("dve", lambda e, bk=bk, st=st, ch=ch, tok=tok: e.tensor_tensor(
                                                out=st.t[:, ch, 2:514], in0=bk.t[:], in1=mrow32.t[:], op=ALU.mult), reads=[bk.b, mrow32.b], writes=[st.b])
                                        elif wt_i < 8:
                                            S.op("act", lambda e, bk=bk, stb=stb, ch=ch: e.activation(out=stb.t[:, ch, :], in_=bk.t[:], func=AF.Silu),
                                                 reads=[bk.b], writes=[stb.b])
                                        else:
                                            nw = qnw_c if wt_i < 10 else knw_c
                                            sh_ = sqh[ch % 2]
                                            rsx = rs_h[ch % 2]
                                            qb_ = qn_bf[ch % 2]
                                            S.op("act", lambda e, bk=bk, sh_=sh_: e.activation(out=sh_.t[:], in_=bk.t[:], func=AF.Square),
                                                 reads=[bk.b], writes=[sh_.b])
                                            bk2 = bank()
                                            S.op("pe", lambda e, bk2=bk2, sh_=sh_: e.matmul(bk2.t[:], lhsT=ones_bf, rhs=sh_.t[:], start=True, stop=True),
                                                 reads=[sh_.b, cb.b], writes=[bk2.b])
                                            S.op("act", lambda e, bk2=bk2, rsx=rsx: e.activation(out=rsx.t[:], in_=bk2.t[:], func=AF.Sqrt, bias=epsc.t[:],
                                                                                                 scale=1.0 / 128), reads=[bk2.b, epsc.b], writes=[rsx.b])
                                            S.op("dve", lambda e, rsx=rsx: e.reciprocal(out=rsx.t[:], in_=rsx.t[:]), reads=[rsx.b], writes=[rsx.b])
                                            qf = tmp(cm)
                                            S.op("dve", lambda e, bk=bk, qf=qf, nw=nw, rsx=rsx: e.scalar_tensor_tensor(
                                                out=qf.t[:], in0=bk.t[:], scalar=nw.t[:, 0:1], in1=rsx.t[:], op0=ALU.mult, op1=ALU.mult),
                                                reads=[bk.b, nw.b, rsx.b], writes=[qf.b])
                                            S.op("act", lambda e, qf=qf, qb_=qb_: e.activation(out=qb_.t[:], in_=qf.t[:], func=AF.Copy), reads=[qf.b], writes=[qb_.b])
                                            bk3 = bank()
                                            S.op("pe", lambda e, bk3=bk3, qb_=qb_: e.matmul(bk3.t[:], lhsT=rl_bf, rhs=qb_.t[:], start=True, stop=True),
                                                 reads=[qb_.b, cb.b], writes=[bk3.b])
                                            r1 = tmp(cm)
                                            S.op("dve", lambda e, bk3=bk3, r1=r1: e.tensor_tensor(out=r1.t[:], in0=bk3.t[:], in1=sinT.t[:], op=ALU.mult),
                                                 reads=[bk3.b, sinT.b], writes=[r1.b])
                                            S.op("dve", lambda e, qf=qf: e.tensor_tensor(out=qf.t[:], in0=qf.t[:], in1=cosT.t[:], op=ALU.mult),
                                                 reads=[qf.b, cosT.b], writes=[qf.b])
                                            if wt_i < 10:
                                                S.op("dve", lambda e, qf=qf, r1=r1, stb=stb, ch=ch: e.tensor_tensor(out=stb.t[:, ch, :], in0=qf.t[:], in1=r1.t[:], op=ALU.add),
                                                     reads=[qf.b, r1.b], writes=[stb.b])
                                            else:
                                                S.op("dve", lambda e, qf=qf, r1=r1, ch=ch, tok=tok: e.tensor_tensor(out=kT_att.t[:, ch, tok], in0=qf.t[:], in1=r1.t[:], op=ALU.add),
                                                     reads=[qf.b, r1.b], writes=[kT_att.b])
                                    if wt_i < 6:
                                        lo = 0 if b == 0 else 2
                                        hi = 516 if b == NB - 1 else 514
                                        S.store("sp", pq_v[:, wt_i * 4:(wt_i + 1) * 4, b * 512 + lo:b * 512 + hi], st, st.t[:, :, lo:hi], writes=[bpq[wt_i][b]])
                                    elif wt_i < 8:
                                        S.store("sp", sz_v[:, (wt_i - 6) * 4:(wt_i - 5) * 4, tok], stb, stb.t[:], writes=[bsz[wt_i - 6][b]])
                                    elif wt_i < 10:
                                        S.store("sp", aq_v[:, (wt_i - 8) * 4:(wt_i - 7) * 4, tok], stb, stb.t[:], writes=[baq[wt_i - 8][b]])
                                    else:
                                        for tt in range(4):
                                            bk = bank()
                                            for c in range(DC):
                                                S.op("pe", lambda e, bk=bk, wt=wt, c=c, tt=tt, ht=ht: e.matmul(
                                                    bk.t[:, 0:256], lhsT=ht.t[:, c, tt * 128:(tt + 1) * 128], rhs=wt.t[:, c, 256:512], start=(c == 0), stop=(c == DC - 1)),
                                                    reads=[wt.b, ht.b], writes=[bk.b])
                                            S.op("act", lambda e, bk=bk, tt=tt, b=b: e.activation(out=v_att.t[:, b * 4 + tt, :], in_=bk.t[:, 0:256], func=AF.Copy),
                                                 reads=[bk.b], writes=[v_att.b])
                                ckpt("PA.wtiles")
                                wt = load_w(cm, w_in[l, :, 4096:4128], cols=32)
                                bk = bank()
                                for tt in range(4):
                                    for c in range(DC):
                                        S.op("pe", lambda e, bk=bk, wt=wt, c=c, tt=tt, ht=ht: e.matmul(
                                            bk.t[:, tt * 32:(tt + 1) * 32], lhsT=ht.t[:, c, tt * 128:(tt + 1) * 128], rhs=wt.t[:, c, 0:32], start=(c == 0), stop=(c == DC - 1)),
                                            reads=[wt.b, ht.b], writes=[bk.b])
                                S.op("dve", lambda e, bk=bk, b=b: e.tensor_copy(out=gb_all.t[:, b * 4:(b + 1) * 4, :], in_=q4(bk.t[:, 0:128])),
                                     reads=[bk.b], writes=[gb_all.b])

                            ckpt("PA.ba")
                            S.op("act", lambda e: e.activation(out=beta_all.t[:], in_=gb_all.t[:, :, 0:16], func=AF.Sigmoid), reads=[gb_all.b], writes=[beta_all.b])
                            S.op("dve", lambda e: e.tensor_tensor(out=beta_all.t[:], in0=beta_all.t[:], in1=mcol.t[:].unsqueeze(2).to_broadcast([128, NT, 16]), op=ALU.mult),
                                 reads=[beta_all.b, mcol.b], writes=[beta_all.b])
                            S.op("dve", lambda e: e.tensor_tensor(out=g_all.t[:], in0=gb_all.t[:, :, 16:32], in1=dtb_r.t[:].unsqueeze(1).to_broadcast([128, NT, 16]), op=ALU.add),
                                 reads=[gb_all.b, dtb_r.b], writes=[g_all.b])
                            S.op("dve", lambda e: e.tensor_scalar(out=g_all.t[:], in0=g_all.t[:], scalar1=60.0, scalar2=None, op0=ALU.min), reads=[g_all.b], writes=[g_all.b])
                            S.op("act", lambda e: e.activation(out=g_all.t[:], in_=g_all.t[:], func=AF.Exp), reads=[g_all.b], writes=[g_all.b])
                            S.op("act", lambda e: e.activation(out=g_all.t[:], in_=g_all.t[:], func=AF.Ln, bias=onec.t[:]), reads=[g_all.b, onec.b], writes=[g_all.b])
                            S.op("dve", lambda e: e.tensor_tensor(out=g_all.t[:], in0=g_all.t[:], in1=nexpalog.t[:].unsqueeze(1).to_broadcast([128, NT, 16]), op=ALU.mult),
                                 reads=[g_all.b, nexpalog.b], writes=[g_all.b])
                            ckpt("PA.gates")
                            for nm, dst in (("cum", "gc"), ("rest", "rest"), ("ones", "gl")):
                                bk = bank()
                                for t in range(NT):
                                    for d_ in range(2):
                                        cname = "ones" if nm == "ones" else nm + ("_f" if d_ == 0 else "_b")
                                        S.op("pe", lambda e, bk=bk, t=t, d_=d_, cname=cname: e.matmul(
                                            bk.t[:, t * 16 + d_ * 8:t * 16 + d_ * 8 + 8], lhsT=C32(cname), rhs=g_all.t[:, t, d_ * 8:(d_ + 1) * 8], start=True, stop=True),
                                            reads=[g_all.b, cf.b], writes=[bk.b])
                                src = bk.t[:, 0:NT * 16].rearrange("p (a c) -> p a c", c=16)
                                if dst == "gc":
                                    S.op("dve", lambda e, src=src: e.tensor_copy(out=gc_all.t[:], in_=src), reads=[bk.b], writes=[gc_all.b])
                                    S.op("act", lambda e, src=src: e.activation(out=negegc_all.t[:], in_=src, func=AF.Exp), reads=[bk.b], writes=[negegc_all.b])
                                    S.op("dve", lambda e: e.tensor_scalar(out=negegc_all.t[:], in0=negegc_all.t[:], scalar1=-1.0, scalar2=None, op0=ALU.mult),
                                         reads=[negegc_all.b], writes=[negegc_all.b])
                                elif dst == "rest":
                                    S.op("act", lambda e, src=src: e.activation(out=erest_all.t[:], in_=src, func=AF.Exp), reads=[bk.b], writes=[erest_all.b])
                                else:
                                    S.op("act", lambda e, src=src: e.activation(out=egl_all.t[:], in_=src, func=AF.Exp), reads=[bk.b], writes=[egl_all.b])
                            if l == 0:
                                dump("beta", beta_all, beta_all.t[:], [128, NT, 16])
                                dump("g", g_all, g_all.t[:], [128, NT, 16])
                                dump("gc", gc_all, gc_all.t[:], [128, NT, 16])
                                dump("kT_att", kT_att, kT_att.t[:], [128, KVH, T], BF16)
                                dump("v_att", v_att, v_att.t[:], [128, NT, 256], BF16)
                            S.flush()

                        with ExitStack() as sc:
                            sb = mk_sb(sc)
                            qblk = [sb([128, 4, 512], BF16, "qblk") for _ in range(2)]
                            pT = [sb([128, 512], BF16, "pT") for _ in range(4)]
                            accsum = [sb([128, 512], F32, "accsum") for _ in range(2)]
                            rsum = [sb([128, 512], F32, "rsum") for _ in range(2)]
                            yb = [sb([128, 4, 512], BF16, "yb") for _ in range(2)]
                            pti = 0
                            pacc = 0
                            for g in range(KVH):
                                for qb in range(NB):
                                    tok = slice(qb * 512, (qb + 1) * 512)
                                    qk_ = qblk[(g * NB + qb) % 2]
                                    ybt = yb[(g * NB + qb) % 2]
                                    S.load("sp", qk_, qk_.t[:], aq_v[:, g * 4:(g + 1) * 4, tok], reads=[baq[g][qb]])
                                    for hq in range(4):
                                        bO, bS = banks[4 + 2 * (pacc % 2)], banks[5 + 2 * (pacc % 2)]
                                        accs = accsum[pacc % 2]
                                        pacc += 1

                                        def qk(kt, g=g, qk_=qk_, hq=hq):
                                            bs_ = banks[kt % 4]
                                            S.op("pe", lambda e: e.matmul(
                                                bs_.t[:], lhsT=kT_att.t[:, g, kt * 128:(kt + 1) * 128], rhs=qk_.t[:, hq, :], start=True, stop=True),
                                                reads=[kT_att.b, qk_.b], writes=[bs_.b])
                                        qk(0)
                                        if NT > 1:
                                            qk(1)
                                        for kt in range(NT):
                                            if kt + 2 < NT:
                                                qk(kt + 2)
                                            bs_ = banks[kt % 4]
                                            p_ = pT[pti % 4]
                                            pti += 1
                                            S.op("act", lambda e, bs_=bs_, p_=p_, kt=kt: e.activation(out=p_.t[:], in_=bs_.t[:], func=AF.Exp, bias=kbias.t[:, kt:kt + 1],
                                                                                                     scale=128.0 ** -0.5), reads=[bs_.b, kbias.b], writes=[p_.b])
                                            S.op("pe", lambda e, bO=bO, g=g, kt=kt, p_=p_: e.matmul(bO.t[:], lhsT=v_att.t[:, kt, g * 128:(g + 1) * 128], rhs=p_.t[:],
                                                                                                     start=(kt == 0), stop=(kt == NT - 1)), reads=[v_att.b, p_.b], writes=[bO.b])
                                            if kt == 0:
                                                S.op("dve", lambda e, accs=accs, p_=p_: e.tensor_copy(out=accs.t[:], in_=p_.t[:]), reads=[p_.b], writes=[accs.b])
                                            else:
                                                S.op("dve", lambda e, accs=accs, p_=p_: e.tensor_tensor(out=accs.t[:], in0=accs.t[:], in1=p_.t[:], op=ALU.add),
                                                     reads=[p_.b, accs.b], writes=[accs.b])
                                        S.op("pe", lambda e, bS=bS, accs=accs: e.matmul(bS.t[:], lhsT=C32("ones"), rhs=accs.t[:], start=True, stop=True),
                                             reads=[cf.b, accs.b], writes=[bS.b])
                                        rs = rsum[hq % 2]
                                        S.op("dve", lambda e, bS=bS, rs=rs: e.reciprocal(out=rs.t[:], in_=bS.t[:]), reads=[bS.b], writes=[rs.b])
                                        S.op("dve", lambda e, bO=bO, rs=rs, ybt=ybt, hq=hq: e.tensor_tensor(out=ybt.t[:, hq, :], in0=bO.t[:], in1=rs.t[:], op=ALU.mult),
                                             reads=[bO.b, rs.b], writes=[ybt.b])
                                    S.store("sp", mix_v[:, 8 + g * 4:8 + (g + 1) * 4, tok], ybt, ybt.t[:], writes=[bmix_a[g][qb]])
                            S.flush()

                    with ExitStack() as sc:
                        sb = mk_sb(sc)
                        praw = [[sb([128, 516], F32, "praw") for _ in range(2)] for _ in range(3)]
                        acc_c = [sb([128, 512], F32, "acc_c") for _ in range(2)]
                        qT_s = sb([128, T], BF16, "qT_s")
                        kT_s = sb([128, T], BF16, "kT_s")
                        vT_b = [sb([128, 512], BF16, "vT_b") for _ in range(2)]
                        k_tok = sb([128, NT, 128], BF16, "k_tok")
                        v_tok = sb([128, NT, 128], BF16, "v_tok")
                        o_all = sb([128, NT, 128], F32, "o_all")
                        szT = [sb([128, 512], BF16, "szT") for _ in range(2)]
                        yT = [sb([128, 512], BF16, "yT") for _ in range(2)]
                        sqh = [sb([128, 512], BF16, "sqh") for _ in range(2)]
                        rs_h = [sb([128, 512], F32, "rs_h") for _ in range(2)]
                        dg2 = [sb([128, 4, 128], F32, "dg") for _ in range(2)]
                        DTt2 = [sb([128, 4, 128], F32, "DT") for _ in range(2)]
                        egr2 = [sb([128, 4, 128], F32, "egr") for _ in range(2)]
                        Gb2 = [sb([128, 4, 128], F32, "Gb") for _ in range(2)]
                        DTx2 = [[sb([128, 4, 128], F32, "DTx%d" % i) for i in range(3)] for _ in range(2)]

                        def grp(name):
                            return [sb([128, 4, 128], BF16, name) for _ in range(2)]
                        MTx = [grp("MTx%d" % i) for i in range(3)]
                        Mx = [grp("Mx%d" % i) for i in range(3)]
                        Pk = [grp("Pk%d" % i) for i in range(2)]
                        PTk = [grp("PTk%d" % i) for i in range(2)]
                        Rr = grp("Rr")
                        RTt = grp("RTt")
                        Zz = grp("Zz")
                        RING = 3
                        NTt = [[sb([128, 4, 128], BF16, "NT") for _ in range(RING)] for _ in range(2)]
                        ATt = [[sb([128, 4, 128], BF16, "AT") for _ in range(RING)] for _ in range(2)]
                        qgT = [[sb([128, 4, 128], BF16, "qgT") for _ in range(RING)] for _ in range(2)]
                        kd = [[sb([128, 4, 128], BF16, "kd") for _ in range(RING)] for _ in range(2)]
                        S32 = [sb([128, 128], F32, "S32_%d" % i) for i in range(2)]
                        Sbf = [sb([128, 128], BF16, "Sbf_%d" % i) for i in range(2)]
                        rr_ = [sb([128, 128], BF16, "r_%d" % i) for i in range(2)]
                        vn_ = [sb([128, 128], BF16, "vn_%d" % i) for i in range(2)]
                        ssq = sb([128, NT], F32, "ssq")
                        junk = sb([128, 128], F32, "junk")
                        on_bf = [sb([128, 128], BF16, "on_bf") for _ in range(2)]
                        idb = ident_bf.unsqueeze(1).to_broadcast([128, 4, 128])
                        k2c = [0]

                        def bcast_u(ap3):
                            return ap3.to_broadcast([128, 4, 128])

                        def mm4(outb, lT, rh):
                            for u in range(4):
                                S.op("pe", lambda e, outb=outb, lT=lT, rh=rh, u=u: e.matmul(
                                    outb.t[:, u * 128:(u + 1) * 128], lhsT=lT.t[:, u, :], rhs=rh.t[:, u, :], start=True, stop=True),
                                    reads=[lT.b, rh.b], writes=[outb.b])

                        def mkbank(lo):
                            st_ = [0]

                            def f():
                                b_ = banks[lo + st_[0] % 2]
                                st_[0] += 1
                                return b_
                            return f

                        def precompute(h, d_, g4, slot):
                            sfx = "_f" if d_ == 0 else "_b"
                            col = d_ * 8 + h
                            k2 = d_
                            pbank = mkbank(2 * d_)
                            dg, DTt, egr, Gb, DTx = dg2[d_], DTt2[d_], egr2[d_], Gb2[d_], DTx2[d_]
                            tsl = slice(g4 * 4, g4 * 4 + 4)
                            S.op("pool", lambda e: e.tensor_tensor(
                                out=dg.t[:], in0=C32("ident").unsqueeze(1).to_broadcast([128, 4, 128]),
                                in1=bcast_u(gc_all.t[:, tsl, col:col + 1]), op=ALU.mult), reads=[gc_all.b, cf.b], writes=[dg.b])
                            yield
                            bC, bD = pbank(), pbank()
                            for u in range(4):
                                us = slice(u * 128, (u + 1) * 128)
                                S.op("pe", lambda e, u=u, us=us: e.matmul(bC.t[:, us], lhsT=C32("ones"), rhs=dg.t[:, u, :], start=True, stop=False),
                                     reads=[dg.b, cf.b], writes=[bC.b])
                                S.op("pe", lambda e, u=u, us=us: e.matmul(bC.t[:, us], lhsT=dg.t[:, u, :], rhs=C32("negones"), start=False, stop=False),
                                     reads=[dg.b, cf.b], writes=[bC.b])
                                S.op("pe", lambda e, us=us: e.matmul(bC.t[:, us], lhsT=C32("ident"), rhs=C32("nm" + sfx), start=False, stop=True),
                                     reads=[cf.b], writes=[bC.b])
                                S.op("pe", lambda e, u=u, us=us: e.matmul(bD.t[:, us], lhsT=C32("ones"), rhs=dg.t[:, u, :], start=True, stop=True),
                                     reads=[dg.b, cf.b], writes=[bD.b])
                            yield
                            S.op("act", lambda e: e.activation(out=v4(DTt), in_=bC.t[:], func=AF.Exp), reads=[bC.b], writes=[DTt.b])
                            S.op("act", lambda e: e.activation(out=v4(egr), in_=bD.t[:], func=AF.Exp), reads=[bD.b], writes=[egr.b])
                            yield
                            bG, bKQ = pbank(), pbank()
                            for u in range(4):
                                t = g4 * 4 + u
                                ksl = kT_s.t[:, t * 128:(t + 1) * 128]
                                qsl = qT_s.t[:, t * 128:(t + 1) * 128]
                                us = slice(u * 128, (u + 1) * 128)
                                S.op("pe", lambda e, ksl=ksl, us=us: e.matmul(bG.t[:, us], lhsT=ksl, rhs=ksl, start=True, stop=True),
                                     reads=[kT_s.b], writes=[bG.b])
                                S.op("pe", lambda e, ksl=ksl, qsl=qsl, us=us: e.matmul(bKQ.t[:, us], lhsT=ksl, rhs=qsl, start=True, stop=True),
                                     reads=[kT_s.b, qT_s.b], writes=[bKQ.b])
                            yield
                            S.op("dve", lambda e: e.tensor_tensor(
                                out=Gb.t[:], in0=q4(bG.t[:]), in1=bcast_u(beta_all.t[:, tsl, col:col + 1]), op=ALU.mult),
                                reads=[bG.b, beta_all.b], writes=[Gb.b])
                            KD = kd[d_][slot]
                            S.op("pool", lambda e: e.tensor_tensor(
                                out=KD.t[:], in0=k_tok.t[:, tsl, :], in1=bcast_u(erest_all.t[:, tsl, col:col + 1]), op=ALU.mult),
                                reads=[k_tok.b, erest_all.b], writes=[KD.b])
                            yield
                            AT = ATt[d_][slot]
                            S.op("dve", lambda e: e.tensor_tensor(out=v4(AT), in0=bKQ.t[:], in1=v4(DTt), op=ALU.mult),
                                 reads=[bKQ.b, DTt.b], writes=[AT.b])
                            QG = qgT[d_][slot]
                            S.op("dve", lambda e: e.tensor_tensor(out=v4(QG), in0=qT_s.t[:, g4 * 512:(g4 + 1) * 512], in1=v4(egr), op=ALU.mult),
                                 reads=[qT_s.b, egr.b], writes=[QG.b])
                            for xi, cmn in enumerate(("cma", "cmb", "cmc")):
                                S.op("pool", lambda e, xi=xi, cmn=cmn: e.tensor_tensor(
                                    out=DTx[xi].t[:], in0=DTt.t[:], in1=C32(cmn + sfx).unsqueeze(1).to_broadcast([128, 4, 128]), op=ALU.mult),
                                    reads=[DTt.b, cf.b], writes=[DTx[xi].b])
                            yield
                            for xi in range(3):
                                eng = "dve" if xi == 0 else "pool"
                                S.op(eng, lambda e, xi=xi: e.tensor_tensor(out=MTx[xi][k2].t[:], in0=Gb.t[:], in1=DTx[xi].t[:], op=ALU.mult),
                                     reads=[Gb.b, DTx[xi].b], writes=[MTx[xi][k2].b])
                            yield
                            tb = []
                            bkA, bkB = pbank(), pbank()
                            for xi in range(3):
                                bk = bkA if xi < 2 else bkB
                                off = 512 if xi == 1 else 0
                                bkv = bk.t[:].bitcast(BF16)
                                tb.append((bk, bkv, off))
                                for u in range(4):
                                    S.op("pe", lambda e, bkv=bkv, u=u, xi=xi, off=off: e.transpose(
                                        out=bkv[:, off + u * 128:off + (u + 1) * 128], in_=MTx[xi][k2].t[:, u, :], identity=ident_bf),
                                        reads=[MTx[xi][k2].b, cb.b], writes=[bk.b])
                            yield
                            for xi in range(3):
                                bk, bkv, off = tb[xi]
                                S.op("act", lambda e, bkv=bkv, xi=xi, off=off: e.activation(out=v4(Mx[xi][k2]), in_=bkv[:, off:off + 512], func=AF.Copy),
                                     reads=[bk.b], writes=[Mx[xi][k2].b])
                            yield
                            R_, RT_ = Rr[k2], RTt[k2]
                            S.op("pool", lambda e: e.tensor_tensor(out=R_.t[:], in0=Mx[0][k2].t[:], in1=idb, op=ALU.add),
                                 reads=[Mx[0][k2].b, cb.b], writes=[R_.b])
                            S.op("pool", lambda e: e.tensor_tensor(out=RT_.t[:], in0=MTx[0][k2].t[:], in1=idb, op=ALU.add),
                                 reads=[MTx[0][k2].b, cb.b], writes=[RT_.b])
                            Pc, PTc = Mx[0][k2], MTx[0][k2]
                            for kk in range(4):
                                Pn, PTn = Pk[kk % 2][k2], PTk[kk % 2][k2]
                                b1 = pbank()
                                mm4(b1, PTc, Pc)
                                if kk < 3:
                                    b2 = pbank()
                                    mm4(b2, Pc, PTc)
                                yield
                                S.op("act", lambda e, b1=b1, Pn=Pn: e.activation(out=v4(Pn), in_=b1.t[:], func=AF.Copy), reads=[b1.b], writes=[Pn.b])
                                if kk < 3:
                                    S.op("act", lambda e, b2=b2, PTn=PTn: e.activation(out=v4(PTn), in_=b2.t[:], func=AF.Copy), reads=[b2.b], writes=[PTn.b])
                                yield
                                b3, b4 = pbank(), pbank()
                                mm4(b3, RT_, Pn)
                                mm4(b4, Pn, RT_)
                                yield
                                S.op("dve", lambda e, b3=b3: e.tensor_tensor(out=v4(R_), in0=b3.t[:], in1=v4(R_), op=ALU.add),
                                     reads=[b3.b, R_.b], writes=[R_.b])
                                S.op("dve", lambda e, b4=b4: e.tensor_tensor(out=v4(RT_), in0=b4.t[:], in1=v4(RT_), op=ALU.add),
                                     reads=[b4.b, RT_.b], writes=[RT_.b])
                                yield
                                Pc, PTc = Pn, PTn
                            b1 = pbank()
                            mm4(b1, Mx[1][k2], RT_)
                            b2 = pbank()
                            mm4(b2, MTx[1][k2], R_)
                            yield
                            Z1, Z2 = Zz[k2], Pk[0][k2]
                            S.op("act", lambda e: e.activation(out=v4(Z1), in_=b1.t[:], func=AF.Copy), reads=[b1.b], writes=[Z1.b])
                            S.op("act", lambda e: e.activation(out=v4(Z2), in_=b2.t[:], func=AF.Copy), reads=[b2.b], writes=[Z2.b])
                            yield
                            b3, b4 = pbank(), pbank()
                            mm4(b3, R_, Z1)
                            mm4(b4, RT_, Z2)
                            yield
                            S.op("dve", lambda e: e.tensor_tensor(out=v4(RT_), in0=v4(RT_), in1=b3.t[:], op=ALU.subtract),
                                 reads=[b3.b, RT_.b], writes=[RT_.b])
                            S.op("dve", lambda e: e.tensor_tensor(out=v4(R_), in0=v4(R_), in1=b4.t[:], op=ALU.subtract),
                                 reads=[b4.b, R_.b], writes=[R_.b])
                            yield
                            b5 = pbank()
                            mm4(b5, Mx[2][k2], RT_)
                            yield
                            S.op("act", lambda e: e.activation(out=v4(Z1), in_=b5.t[:], func=AF.Copy), reads=[b5.b], writes=[Z1.b])
                            yield
                            b6 = pbank()
                            mm4(b6, R_, Z1)
                            yield
                            NTg = NTt[d_][slot]
                            S.op("dve", lambda e: e.tensor_tensor(out=v4(NTg), in0=v4(RT_), in1=b6.t[:], op=ALU.subtract),
                                 reads=[b6.b, RT_.b], writes=[NTg.b])

                        def rec_group(h, d_, tiles, slot, o_written):
                            col = d_ * 8 + h
                            rbank = mkbank(4 + 2 * d_)
                            for t in tiles:
                                u = t % 4
                                ksl = kT_s.t[:, t * 128:(t + 1) * 128]
                                bk1 = rbank()
                                S.op("pe", lambda e, ksl=ksl, bk1=bk1: e.matmul(bk1.t[:, 0:128], lhsT=ksl, rhs=Sbf[d_].t[:], start=True, stop=True),
                                     reads=[kT_s.b, Sbf[d_].b], writes=[bk1.b])
                                yield
                                S.op("dve", lambda e, bk1=bk1, t=t: e.scalar_tensor_tensor(
                                    out=rr_[d_].t[:], in0=bk1.t[:, 0:128], scalar=negegc_all.t[:, t, col:col + 1], in1=v_tok.t[:, t, :], op0=ALU.mult, op1=ALU.add),
                                    reads=[bk1.b, negegc_all.b, v_tok.b], writes=[rr_[d_].b])
                                yield
                                bk2 = rbank()
                                S.op("pe", lambda e, bk2=bk2, u=u: e.matmul(bk2.t[:, 0:128], lhsT=NTt[d_][slot].t[:, u, :], rhs=rr_[d_].t[:], start=True, stop=True),
                                     reads=[NTt[d_][slot].b, rr_[d_].b], writes=[bk2.b])
                                yield
                                S.op("act", lambda e, bk2=bk2, t=t: e.activation(out=vn_[d_].t[:], in_=bk2.t[:, 0:128], func=AF.Identity, scale=beta_all.t[:, t, col:col + 1]),
                                     reads=[bk2.b, beta_all.b], writes=[vn_[d_].b])
                                yield
                                bk4 = rbank()
                                S.op("pe", lambda e, bk4=bk4, u=u: e.matmul(bk4.t[:, 0:128], lhsT=kd[d_][slot].t[:, u, :], rhs=vn_[d_].t[:], start=True, stop=True),
                                     reads=[kd[d_][slot].b, vn_[d_].b], writes=[bk4.b])
                                bk3 = bk4
                                S.op("pe", lambda e, bk3=bk3, u=u: e.matmul(bk3.t[:, 128:256], lhsT=qgT[d_][slot].t[:, u, :], rhs=Sbf[d_].t[:], start=True, stop=False),
                                     reads=[qgT[d_][slot].b, Sbf[d_].b], writes=[bk3.b])
                                S.op("pe", lambda e, bk3=bk3, u=u: e.matmul(bk3.t[:, 128:256], lhsT=ATt[d_][slot].t[:, u, :], rhs=vn_[d_].t[:], start=False, stop=True),
                                     reads=[ATt[d_][slot].b, vn_[d_].b], writes=[bk3.b])
                                yield
                                S.op("dve", lambda e, bk4=bk4, t=t: e.scalar_tensor_tensor(
                                    out=S32[d_].t[:], in0=S32[d_].t[:], scalar=egl_all.t[:, t, col:col + 1], in1=bk4.t[:, 0:128], op0=ALU.mult, op1=ALU.add),
                                    reads=[bk4.b, egl_all.b, S32[d_].b], writes=[S32[d_].b])
                                yield
                                S.op("act", lambda e: e.activation(out=Sbf[d_].t[:], in_=S32[d_].t[:], func=AF.Copy), reads=[S32[d_].b], writes=[Sbf[d_].b])
                                if t not in o_written:
                                    o_written.add(t)
                                    S.op("dve", lambda e, bk3=bk3, t=t: e.tensor_copy(out=o_all.t[:, t, :], in_=bk3.t[:, 128:256]), reads=[bk3.b], writes=[o_all.b])
                                else:
                                    S.op("dve", lambda e, bk3=bk3, t=t: e.tensor_tensor(out=o_all.t[:, t, :], in0=bk3.t[:, 128:256], in1=o_all.t[:, t, :], op=ALU.add),
                                         reads=[bk3.b, o_all.b], writes=[o_all.b])
                                yield

                        def lockstep(gens):
                            gens = list(gens)
                            while gens:
                                nxt = []
                                for g_ in gens:
                                    try:
                                        next(g_)
                                        nxt.append(g_)
                                    except StopIteration:
                                        pass
                                gens = nxt

                        for h in range(GH):
                            li = 0
                            for i3 in range(3):
                                chn = i3 * 8 + h
                                wt_i = chn // 4
                                cw = convw.t[:, chn, :]
                                for b in range(NB):
                                    tok = slice(b * 512, (b + 1) * 512)
                                    pr = praw[i3][b % 2]
                                    rd = [bpq[wt_i][b], bpq_halo]
                                    if b > 0:
                                        rd.append(bpq[wt_i][b - 1])
                                    if b < NB - 1:
                                        rd.append(bpq[wt_i][b + 1])
                                    S.load("sp", pr, pr.t[:], pqkv[chn * 128:(chn + 1) * 128, b * 512:b * 512 + 516], reads=rd)
                                    ac = acc_c[li % 2]
                                    li += 1
                                    S.op("dve", lambda e, pr=pr, cw=cw, ac=ac: e.tensor_scalar(
                                        out=ac.t[:], in0=pr.t[:, 0:512], scalar1=cw[:, 0:1], scalar2=None, op0=ALU.mult),
                                        reads=[pr.b, convw.b], writes=[ac.b])
                                    for j in range(1, 5):
                                        S.op("dve", lambda e, pr=pr, cw=cw, ac=ac, j=j: e.scalar_tensor_tensor(
                                            out=ac.t[:], in0=pr.t[:, j:j + 512], scalar=cw[:, j:j + 1], in1=ac.t[:],
                                            op0=ALU.mult, op1=ALU.add), reads=[pr.b, convw.b, ac.b], writes=[ac.b])
                                    if i3 == 2:
                                        vb = vT_b[b % 2]
                                        S.op("act", lambda e, ac=ac, vb=vb: e.activation(out=vb.t[:], in_=ac.t[:], func=AF.Silu), reads=[ac.b], writes=[vb.b])
                                        bk = bank()
                                        bkv = bk.t[:].bitcast(BF16)
                                        for u in range(4):
                                            S.op("pe", lambda e, bkv=bkv, u=u, vb=vb: e.transpose(out=bkv[:, u * 128:(u + 1) * 128], in_=vb.t[:, u * 128:(u + 1) * 128], identity=ident_bf),
                                                 reads=[vb.b, cb.b], writes=[bk.b])
                                        S.op("act", lambda e, bkv=bkv, b=b: e.activation(out=v_tok.t[:, b * 4:(b + 1) * 4, :], in_=q4(bkv[:, 0:512]), func=AF.Copy),
                                             reads=[bk.b], writes=[v_tok.b])
                                    else:
                                        dstT = qT_s if i3 == 0 else kT_s
                                        S.op("act", lambda e, ac=ac: e.activation(out=ac.t[:], in_=ac.t[:], func=AF.Silu), reads=[ac.b], writes=[ac.b])
                                        sh_ = sqh[b % 2]
                                        rsx = rs_h[b % 2]
                                        S.op("act", lambda e, ac=ac, sh_=sh_: e.activation(out=sh_.t[:], in_=ac.t[:], func=AF.Square), reads=[ac.b], writes=[sh_.b])
                                        bk2 = bank()
                                        S.op("pe", lambda e, bk2=bk2, sh_=sh_: e.matmul(bk2.t[:], lhsT=ones_bf, rhs=sh_.t[:], start=True, stop=True),
                                             reads=[sh_.b, cb.b], writes=[bk2.b])
                                        S.op("act", lambda e, bk2=bk2, rsx=rsx: e.activation(out=rsx.t[:], in_=bk2.t[:], func=AF.Sqrt, bias=epsc.t[:], scale=1.0),
                                             reads=[bk2.b, epsc.b], writes=[rsx.b])
                                        S.op("dve", lambda e, rsx=rsx: e.reciprocal(out=rsx.t[:], in_=rsx.t[:]), reads=[rsx.b], writes=[rsx.b])
                                        scq = (128.0 ** -0.5) if i3 == 0 else 1.0
                                        S.op("dve", lambda e, tok=tok, rsx=rsx, ac=ac, scq=scq, dstT=dstT: e.scalar_tensor_tensor(
                                            out=dstT.t[:, tok], in0=ac.t[:], scalar=scq, in1=rsx.t[:], op0=ALU.mult, op1=ALU.mult),
                                            reads=[ac.b, rsx.b], writes=[dstT.b])
                                        if i3 == 1:
                                            bk = bank()
                                            bkv = bk.t[:].bitcast(BF16)
                                            for u in range(4):
                                                t = b * 4 + u
                                                S.op("pe", lambda e, bkv=bkv, u=u, t=t: e.transpose(out=bkv[:, u * 128:(u + 1) * 128], in_=kT_s.t[:, t * 128:(t + 1) * 128], identity=ident_bf),
                                                     reads=[kT_s.b, cb.b], writes=[bk.b])
                                            S.op("act", lambda e, bkv=bkv, b=b: e.activation(out=k_tok.t[:, b * 4:(b + 1) * 4, :], in_=q4(bkv[:, 0:512]), func=AF.Copy),
                                                 reads=[bk.b], writes=[k_tok.b])
                            if h == 0 and l == 0:
                                dump("qT", qT_s, qT_s.t[:], [128, T], BF16)
                                dump("kT", kT_s, kT_s.t[:], [128, T], BF16)
                                dump("v_tok", v_tok, v_tok.t[:], [128, NT, 128], BF16)

                            for d_ in range(2):
                                S.op("pool", lambda e, d_=d_: e.memset(S32[d_].t[:], 0.0), writes=[S32[d_].b])
                                S.op("pool", lambda e, d_=d_: e.memset(Sbf[d_].t[:], 0.0), writes=[Sbf[d_].b])
                            o_written = set()
                            lockstep([precompute(h, 0, 0, 0), precompute(h, 1, NG - 1, 0)])
                            if h == 0 and l == 0:
                                dump("NT_f", NTt[0][0], NTt[0][0].t[:], [128, 4, 128], BF16)
                                dump("AT_f", ATt[0][0], ATt[0][0].t[:], [128, 4, 128], BF16)
                                dump("NT_b", NTt[1][0], NTt[1][0].t[:], [128, 4, 128], BF16)
                            for gi in range(NG):
                                gens = []
                                if gi + 1 < NG:
                                    gens.append(precompute(h, 0, gi + 1, (gi + 1) % RING))
                                    gens.append(precompute(h, 1, NG - 2 - gi, (gi + 1) % RING))
                                gens.append(rec_group(h, 0, [gi * 4 + s_ for s_ in range(4)], gi % RING, o_written))
                                gens.append(rec_group(h, 1, [(NG - 1 - gi) * 4 + 3 - s_ for s_ in range(4)], gi % RING, o_written))
                                lockstep(gens)
                            if h == 0 and l == 0:
                                dump("o_all", o_all, o_all.t[:], [128, NT, 128])
                            for t in range(NT):
                                S.op("act", lambda e, t=t: e.activation(out=junk.t[:], in_=o_all.t[:, t, :], func=AF.Square, accum_out=ssq.t[:, t:t + 1]),
                                     reads=[o_all.b], writes=[junk.b, ssq.b])
                            S.op("act", lambda e: e.activation(out=ssq.t[:], in_=ssq.t[:], func=AF.Sqrt, bias=epsc.t[:], scale=1.0 / 128), reads=[ssq.b, epsc.b], writes=[ssq.b])
                            S.op("dve", lambda e: e.reciprocal(out=ssq.t[:], in_=ssq.t[:]), reads=[ssq.b], writes=[ssq.b])
                            for g4 in range(NG):
                                sz = szT[g4 % 2]
                                yt = yT[g4 % 2]
                                S.load("sp", sz, sz.t[:], szs[h * 128:(h + 1) * 128, g4 * 512:(g4 + 1) * 512], reads=[bsz[h // 4][g4]])
                                bk = bank()
                                bkv = bk.t[:].bitcast(BF16)
                                for u in range(4):
                                    t = g4 * 4 + u
                                    ob = on_bf[t % 2]
                                    S.op("dve", lambda e, t=t, ob=ob: e.tensor_scalar(out=ob.t[:], in0=o_all.t[:, t, :], scalar1=ssq.t[:, t:t + 1], scalar2=None, op0=ALU.mult),
                                         reads=[o_all.b, ssq.b], writes=[ob.b])
                                    S.op("pe", lambda e, bkv=bkv, u=u, ob=ob: e.transpose(out=bkv[:, u * 128:(u + 1) * 128], in_=ob.t[:], identity=ident_bf),
                                         reads=[ob.b, cb.b], writes=[bk.b])
                                S.op("dve", lambda e, bkv=bkv, sz=sz, yt=yt: e.scalar_tensor_tensor(
                                    out=yt.t[:], in0=bkv[:, 0:512], scalar=gnw_c.t[:, 0:1], in1=sz.t[:], op0=ALU.mult, op1=ALU.mult),
                                    reads=[bk.b, gnw_c.b, sz.b], writes=[yt.b])
                                S.store("sp", mixs[h * 128:(h + 1) * 128, g4 * 512:(g4 + 1) * 512], yt, yt.t[:], writes=[bmix_g[h]])
                        S.flush()

                with ExitStack() as sc:
                    sb = mk_sb(sc)
                    cm = make_common(sb, 3)
                    xt = sb([128, DC, 512], F32, "xT")
                    hTs = [sb([128, DC, 512], BF16, "hT")] * 2
                    actT = sb([128, FC, 512], BF16, "actT")
                    last = (l == depth - 1)
                    if last:
                        xtok = [sb([128, D], F32, "xtok") for _ in range(2)]
                    for b in range(NB):
                        tok = slice(b * 512, (b + 1) * 512)
                        mt = hTs[0]
                        h2 = hTs[1]
                        S.load("sp", xt, xt.t[:], xres_v[:, :, tok], reads=[bxres[b]])
                        S.load("sp", mt, mt.t[:], mix_v[:, :, tok], reads=bmix_g + [bmix_a[0][b], bmix_a[1][b]])
                        if b == 0 and l == 0:
                            dump("mixT", mt, mt.t[:], [128, DC, 512], BF16)
                        for wi in range(4):
                            wt = load_w(cm, w_out[l, :, wi * 512:(wi + 1) * 512])
                            for ch in range(4):
                                j = wi * 4 + ch
                                bk = bank()
                                for c in range(DC):
                                    S.op("pe", lambda e, bk=bk, wt=wt, c=c, ch=ch, mt=mt: e.matmul(
                                        bk.t[:], lhsT=wt.t[:, c, ch * 128:(ch + 1) * 128], rhs=mt.t[:, c, :], start=(c == 0), stop=(c == DC - 1)),
                                        reads=[wt.b, mt.b], writes=[bk.b])
                                S.op("dve", lambda e, bk=bk, j=j: e.scalar_tensor_tensor(
                                    out=xt.t[:, j, :], in0=bk.t[:], scalar=gt1[:, j:j + 1], in1=xt.t[:, j, :], op0=ALU.mult, op1=ALU.add),
                                    reads=[bk.b, xt.b, modT[l].b], writes=[xt.b])
                        norm_block(cm, xt, A2[l].t, sh2, h2)
                        for wi in range(16):
                            wt = load_w(cm, w_up[l, :, wi * 512:(wi + 1) * 512])
                            for ch in range(4):
                                f = wi * 4 + ch
                                bk = bank()
                                for c in range(DC):
                                    S.op("pe", lambda e, bk=bk, wt=wt, c=c, ch=ch: e.matmul(
                                        bk.t[:], lhsT=wt.t[:, c, ch * 128:(ch + 1) * 128], rhs=h2.t[:, c, :], start=(c == 0), stop=(c == DC - 1)),
                                        reads=[wt.b, h2.b], writes=[bk.b])
                                r1 = tmp(cm)
                                S.op("act", lambda e, bk=bk, r1=r1: e.activation(out=r1.t[:], in_=bk.t[:], func=AF.Relu), reads=[bk.b], writes=[r1.b])
                                S.op("dve", lambda e, r1=r1, f=f: e.tensor_tensor(out=actT.t[:, f, :], in0=r1.t[:], in1=r1.t[:], op=ALU.mult),
                                     reads=[r1.b], writes=[actT.b])
                        for jg in range(4):
                            bks = [bank() for _ in range(4)]
                            for fq in range(4):
                                wt = load_w(cm, w_down[l, fq * 2048:(fq + 1) * 2048, jg * 512:(jg + 1) * 512])
                                for ch in range(4):
                                    for fc in range(16):
                                        f = fq * 16 + fc
                                        S.op("pe", lambda e, bk=bks[ch], wt=wt, fc=fc, ch=ch, f=f: e.matmul(
                                            bk.t[:], lhsT=wt.t[:, fc, ch * 128:(ch + 1) * 128], rhs=actT.t[:, f, :], start=(f == 0), stop=(f == FC - 1)),
                                            reads=[wt.b, actT.b], writes=[bks[ch].b])
                            for ch in range(4):
                                j = jg * 4 + ch
                                S.op("dve", lambda e, bk=bks[ch], j=j: e.scalar_tensor_tensor(
                                    out=xt.t[:, j, :], in0=bk.t[:], scalar=gt2[:, j:j + 1], in1=xt.t[:, j, :], op0=ALU.mult, op1=ALU.add),
                                    reads=[bks[ch].b, xt.b, modT[l].b], writes=[xt.b])
                        if not last:
                            S.store("sp", xres_v[:, :, tok], xt, xt.t[:], writes=[bxres[b]])
                        else:
                            for tt in range(4):
                                ti = b * 4 + tt
                                xk = xtok[ti % 2]
                                for cg in range(4):
                                    bk = bank()
                                    for cc in range(4):
                                        c = cg * 4 + cc
                                        S.op("pe", lambda e, bk=bk, c=c, cc=cc, tt=tt: e.transpose(
                                            out=bk.t[:, cc * 128:(cc + 1) * 128], in_=xt.t[:, c, tt * 128:(tt + 1) * 128], identity=C32("ident")),
                                            reads=[xt.b, cf.b], writes=[bk.b])
                                    if cg % 2 == 0:
                                        S.op("act", lambda e, bk=bk, xk=xk, cg=cg: e.activation(out=xk.t[:, cg * 512:(cg + 1) * 512], in_=bk.t[:], func=AF.Copy),
                                             reads=[bk.b], writes=[xk.b])
                                    else:
                                        S.op("dve", lambda e, bk=bk, xk=xk, cg=cg: e.tensor_copy(out=xk.t[:, cg * 512:(cg + 1) * 512], in_=bk.t[:]),
                                             reads=[bk.b], writes=[xk.b])
                                S.store("sp", y_out[ti * 128:(ti + 1) * 128, :], xk, xk.t[:])
                    S.flush()
    except _Stop:
        pass
    return nc, dbg_out


_PROG_CACHE = {}


def _core_inputs(x_seq, c_vec, T, shared):
    tv = x_seq.shape[0]
    xp = np.zeros((T, D), np.float32)
    xp[:tv] = x_seq
    mask = np.zeros((T,), np.float32)
    mask[:tv] = 1.0
    m = dict(shared)
    m["x"] = xp
    m["c"] = np.ascontiguousarray(c_vec, dtype=np.float32)
    m["mask_row"] = mask.reshape(1, T).copy()
    m["mask_col"] = np.ascontiguousarray(mask.reshape(T // 128, 128).T)
    return m


def run_trunk(seqs, cvecs, weights, T, depth, dbg=(), n_cores=None):
    key = (T, depth, tuple(dbg))
    if key not in _PROG_CACHE:
        _PROG_CACHE[key] = build_program(T, depth, dbg)
    nc, dbg_out = _PROG_CACHE[key]
    names32, cf_np, cb_np, cosT, sinT = make_consts(T)
    shared = {
        "ada_w": weights["ada_w"], "ada_b": weights["ada_b"], "norm1_w": weights["norm1_w"], "norm2_w": weights["norm2_w"],
        "w_in": weights["w_in"], "conv_w": weights["conv_w"],
        "a_log": np.ascontiguousarray(weights["a_log"].reshape(depth, 16)),
        "dt_bias": np.ascontiguousarray(weights["dt_bias"].reshape(depth, 16)),
        "gdn_norm_w": weights["gdn_norm_w"], "q_norm_w": weights["q_norm_w"], "k_norm_w": weights["k_norm_w"],
        "w_out": weights["w_out"], "w_up": weights["w_up"], "w_down": weights["w_down"],
        "cf": cf_np, "cb": cb_np, "cosT": cosT, "sinT": sinT,
    }
    shared = {k: np.ascontiguousarray(np.asarray(v, dtype=np.float32)) for k, v in shared.items()}
    in_maps = [_core_inputs(s, c, T, shared) for s, c in zip(seqs, cvecs)]
    res = run_bass_kernel_spmd(nc, in_maps, core_ids=list(range(len(in_maps))))
    return res


def kernel(x_prompt, x_sample, c_prompt, c_sample, ada_w, ada_b, norm1_w, norm2_w, w_in, conv_w, a_log, dt_bias,
           gdn_norm_w, q_norm_w, k_norm_w, w_out, w_up, w_down):
    x_prompt = np.asarray(x_prompt, np.float32)
    x_sample = np.asarray(x_sample, np.float32)
    c_prompt = np.asarray(c_prompt, np.float32)
    c_sample = np.asarray(c_sample, np.float32)
    depth = int(np.asarray(ada_w).shape[0])
    T = x_prompt.shape[1]
    weights = dict(ada_w=ada_w, ada_b=ada_b, norm1_w=norm1_w, norm2_w=norm2_w, w_in=w_in, conv_w=conv_w, a_log=np.asarray(a_log),
                   dt_bias=np.asarray(dt_bias), gdn_norm_w=gdn_norm_w, q_norm_w=q_norm_w, k_norm_w=k_norm_w, w_out=w_out, w_up=w_up,
                   w_down=w_down)
    seqs = [x_prompt[i] for i in range(4)] + [x_sample[i] for i in range(4)]
    cvecs = [c_prompt[i] for i in range(4)] + [c_sample[i] for i in range(4)]
    res = run_trunk(seqs, cvecs, weights, T, depth)
    ys = [r["y"] for r in res.results]
    y_prompt = np.stack([ys[i] for i in range(4)], axis=0).astype(np.float32)
    ts = x_sample.shape[1]
    y_sample = np.stack([ys[4 + i][:ts] for i in range(4)], axis=0).astype(np.float32)
    return (y_prompt, y_sample)
```

```python
import math
from contextlib import ExitStack

import numpy as np
import concourse.bass as bass
import concourse.mybir as mybir
from concourse.bass_utils import run_bass_kernel_spmd

F32 = mybir.dt.float32
BF16 = mybir.dt.bfloat16
ALU = mybir.AluOpType
AF = mybir.ActivationFunctionType
AX = mybir.AxisListType

D = 2048
DC = 16
HD = 128
GH = 8
AH = 8
KVH = 2
DFF = 8192
FC = 64
IN_COLS = 5664
EPS = 1e-6
NEG = -30000.0


class _Stop(Exception):
    pass


STOP_AFTER = [0]
CKPT = [{"PA.params", "PA.mrow_zpad", "PA.norm", "PA.wtiles", "PA.ba", "PA.gates"}]


class Buf:
    __slots__ = ("name", "w", "r")

    def __init__(self, name=""):
        self.name = name
        self.w = None
        self.r = []


class Sched:
    ENG = ("pe", "act", "dve", "pool", "sp")

    def __init__(self, nc, es):
        self.nc = nc
        self.es = es
        self.ops = {k: [] for k in self.ENG}
        self.esem = {k: es.enter_context(nc.semaphore("s_" + k)) for k in self.ENG}
        self.ebase = {k: 0 for k in self.ENG}
        self.dsem = {}
        self.dcount = {}
        self.ddirty = set()
        self.gen = 0
        self.nops = 0

    def _deps(self, reads, writes):
        g = self.gen
        deps = []
        for b in reads:
            if b.w is not None and b.w[3] == g:
                deps.append(b.w)
        for b in writes:
            if b.w is not None and b.w[3] == g:
                deps.append(b.w)
            for t in b.r:
                if t[3] == g:
                    deps.append(t)
        return deps

    def _mark(self, tok, reads, writes):
        src = (tok[0], tok[1])
        for b in reads:
            r = b.r
            for i in range(len(r)):
                if (r[i][0], r[i][1]) == src:
                    r[i] = tok
                    break
            else:
                r.append(tok)
        for b in writes:
            b.w = tok
            b.r = []

    def op(self, eng, fn, reads=(), writes=()):
        deps = self._deps(reads, writes)
        lst = self.ops[eng]
        tok = ("E", eng, len(lst), self.gen)
        lst.append([fn, deps, None, False])
        self._mark(tok, reads, writes)
        return tok

    def dma(self, q, out, in_, reads=(), writes=(), key=None, **kw):
        deps = self._deps(reads, writes)
        if key not in self.dsem:
            self.dsem[key] = self.es.enter_context(self.nc.semaphore("d_%d" % len(self.dsem)))
            self.dcount[key] = 0
        self.dcount[key] += 16
        self.ddirty.add(key)
        tok = ("D", key, self.dcount[key], self.gen)
        sem = self.dsem[key]

        def fn(e, out=out, in_=in_, kw=kw):
            return e.dma_start(out=out, in_=in_, **kw)
        self.ops[q].append([fn, deps, (sem, 16), False])
        self._mark(tok, reads, writes)
        return tok

    def load(self, q, tl, dst_ap, src_ap, reads=(), **kw):
        return self.dma(q, dst_ap, src_ap, reads=reads, writes=[tl.b], key="ld_" + tl.b.name, **kw)

    def store(self, q, dst_ap, tl, src_ap, writes=(), **kw):
        return self.dma(q, dst_ap, src_ap, reads=[tl.b], writes=writes, key="st_" + tl.b.name, **kw)

    def flush(self):
        for eng, lst in self.ops.items():
            for rec in lst:
                for d in rec[1]:
                    if d[0] == "E" and not (d[1] == eng and eng == "pe"):
                        self.ops[d[1]][d[2]][3] = True
        cum = {}
        for eng, lst in self.ops.items():
            c = self.ebase[eng]
            arr = []
            for rec in lst:
                if rec[3]:
                    c += 1
                arr.append(c)
            cum[eng] = arr
        nc = self.nc
        with nc.Block() as block:
            def run(eng, e):
                seen = {}
                for rec in self.ops[eng]:
                    need = {}
                    for d in rec[1]:
                        if d[0] == "E":
                            if d[1] == eng and eng == "pe":
                                continue
                            s = ("E", d[1])
                            v = cum[d[1]][d[2]]
                        else:
                            s = ("D", d[1])
                            v = d[2]
                        if seen.get(s, 0) >= v:
                            continue
                        if need.get(s, 0) < v:
                            need[s] = v
                    for s, v in need.items():
                        sem = self.esem[s[1]] if s[0] == "E" else self.dsem[s[1]]
                        e.wait_ge(sem, v)
                        seen[s] = v
                    ins = rec[0](e)
                    if rec[2] is not None:
                        ins.then_inc(rec[2][0], rec[2][1])
                    elif rec[3]:
                        ins.then_inc(self.esem[eng], 1)
                if eng == "sp":
                    for key in sorted(self.ddirty):
                        e.wait_ge(self.dsem[key], self.dcount[key])

            block.tensor(lambda e: run("pe", e))
            block.scalar(lambda e: run("act", e))
            block.vector(lambda e: run("dve", e))
            block.gpsimd(lambda e: run("pool", e))
            block.sync(lambda e: run("sp", e))
        for eng in self.ENG:
            self.nops += len(self.ops[eng])
            self.ebase[eng] = cum[eng][-1] if cum[eng] else self.ebase[eng]
            self.ops[eng] = []
        self.ddirty = set()
        self.gen += 1
        if STOP_AFTER[0] and self.gen == STOP_AFTER[0]:
            raise _Stop()


class Tl:
    __slots__ = ("t", "b")

    def __init__(self, t, name=""):
        self.t = t
        self.b = Buf(name)


def make_consts(T):
    i = np.arange(128)
    J, I = np.meshgrid(i, i, indexing="ij")
    c32 = {}
    c32["ident"] = (J == I)
    c32["ones"] = np.ones((128, 128))
    c32["negones"] = -np.ones((128, 128))
    c32["cum_f"] = (J <= I)
    c32["cum_b"] = (J >= I)
    c32["rest_f"] = (J > I)
    c32["rest_b"] = (J < I)
    c32["nm_f"] = np.where(I >= J, 0.0, NEG)
    c32["nm_b"] = np.where(I <= J, 0.0, NEG)
    bd32 = (J // 32 == I // 32)
    bd64 = (J // 64 == I // 64)
    su = (I > J)
    sl = (I < J)
    for nm, st in (("f", su), ("b", sl)):
        c32["cma_" + nm] = -(st & bd32).astype(np.float64)
        c32["cmb_" + nm] = (st & bd64 & ~bd32)
        c32["cmc_" + nm] = (st & ~bd64)
    names32 = list(c32.keys())
    cf = np.concatenate([np.asarray(c32[k], np.float32) for k in names32], axis=1)
    Rl = np.zeros((128, 128), np.float32)
    for d in range(128):
        q = d // 32
        if q % 2 == 0:
            Rl[d + 32, d] = -1.0
        else:
            Rl[d - 32, d] = 1.0
    cb = np.concatenate([np.eye(128, dtype=np.float32), np.ones((128, 128), np.float32), Rl], axis=1)
    rows = T // 64
    row = np.repeat(np.arange(rows, dtype=np.float32), 64)
    col = np.tile(np.arange(64, dtype=np.float32), rows)
    inv = (10000.0 ** (-np.arange(0, 64, 2, dtype=np.float32) / 64)).astype(np.float32)
    ar = row[:, None] * inv[None, :]
    ac = col[:, None] * inv[None, :]
    ang = np.concatenate([ar, ar, ac, ac], axis=-1)
    cosT = np.ascontiguousarray(np.cos(ang).T.astype(np.float32))
    sinT = np.ascontiguousarray(np.sin(ang).T.astype(np.float32))
    return names32, cf, cb, cosT, sinT


def build_program(T, depth, dbg=()):
    assert T % 512 == 0
    NT = T // 128
    NB = T // 512
    NG = NT // 4
    names32, cf_np, cb_np, _, _ = make_consts(T)
    NCF = cf_np.shape[1]
    nc = bass.Bass("TRN2", target_bir_lowering=False)

    def din(name, shape, dt=F32):
        return nc.dram_tensor(name, list(shape), dt, kind="ExternalInput").ap()

    def dscr(name, shape, dt=F32):
        return nc.dram_tensor(name, list(shape), dt, kind="Internal").ap()

    x_in = din("x", [T, D])
    c_in = din("c", [D])
    ada_w = din("ada_w", [depth, D, 6 * D])
    ada_b = din("ada_b", [depth, 6 * D])
    n1w = din("norm1_w", [depth, D])
    n2w = din("norm2_w", [depth, D])
    w_in = din("w_in", [depth, D, IN_COLS])
    conv_w = din("conv_w", [depth, 5, 3072])
    a_log = din("a_log", [depth, 16])
    dt_bias = din("dt_bias", [depth, 16])
    gnw = din("gdn_norm_w", [depth, 128])
    qnw = din("q_norm_w", [depth, 128])
    knw = din("k_norm_w", [depth, 128])
    w_out = din("w_out", [depth, D, D])
    w_up = din("w_up", [depth, D, DFF])
    w_down = din("w_down", [depth, DFF, D])
    cf_in = din("cf", [128, NCF])
    cb_in = din("cb", [128, 384])
    cos_in = din("cosT", [128, T])
    sin_in = din("sinT", [128, T])
    mrow_in = din("mask_row", [1, T])
    mcol_in = din("mask_col", [128, NT])
    y_out = nc.dram_tensor("y", [T, D], F32, kind="ExternalOutput").ap()
    dbg_out = {}

    xres = dscr("xres", [D, T])
    pqkv = dscr("pqkv", [3072, T + 4])
    szs = dscr("szs", [1024, T], BF16)
    aqs = dscr("aqs", [1024, T], BF16)
    mixs = dscr("mixs", [D, T], BF16)
    modrow = dscr("modrow", [depth, 6 * D])
    xres_v = xres.rearrange("(c p) t -> p c t", p=128)
    pq_v = pqkv.rearrange("(c p) t -> p c t", p=128)
    sz_v = szs.rearrange("(c p) t -> p c t", p=128)
    aq_v = aqs.rearrange("(c p) t -> p c t", p=128)
    mix_v = mixs.rearrange("(c p) t -> p c t", p=128)
    bxres = [Buf("xres%d" % b) for b in range(NB)]
    bpq = [[Buf("pq") for b in range(NB)] for _ in range(6)]
    bpq_halo = Buf("pqh")
    bsz = [[Buf("sz") for b in range(NB)] for _ in range(2)]
    baq = [[Buf("aq") for b in range(NB)] for _ in range(2)]
    bmix_g = [Buf("mixg") for _ in range(GH)]
    bmix_a = [[Buf("mixa") for b in range(NB)] for _ in range(KVH)]
    bmod = Buf("modrow")

    es = ExitStack()
    try:
        with es:
            S = Sched(nc, es)
            cnt = [0]

            def ckpt(tag):
                if CKPT[0] is True or (CKPT[0] and tag in CKPT[0]):
                    S.flush()

            def mk_sb(scope):
                def sb(shape, dt=F32, name="t"):
                    cnt[0] += 1
                    nm = "%s_%d" % (name, cnt[0])
                    return Tl(scope.enter_context(nc.sbuf_tensor(nm, list(shape), dt)), nm)
                return sb
            sb0 = mk_sb(es)

            banks = [Tl(es.enter_context(nc.psum_tensor("bank%d" % i, [128, 512], F32)), "bank%d" % i) for i in range(8)]
            bank_i = [0]

            def bank():
                b = banks[bank_i[0] % 8]
                bank_i[0] += 1
                return b

            def v4(tl):
                return tl.t[:].rearrange("p a c -> p (a c)")

            def q4(ap):
                return ap.rearrange("p (a c) -> p a c", a=4)

            cf = sb0([128, NCF], F32, "cf")
            cb = sb0([128, 384], BF16, "cb")
            epsc = sb0([128, 1], F32, "eps")
            onec = sb0([128, 1], F32, "one")
            mcol = sb0([128, NT], F32, "mcol")
            S.load("sp", cf, cf.t[:], cf_in)
            S.load("sp", mcol, mcol.t[:], mcol_in)
            with ExitStack() as sc:
                sb = mk_sb(sc)
                cbf = sb([128, 384], F32, "cbf")
                S.load("sp", cbf, cbf.t[:], cb_in)
                S.op("dve", lambda e: e.tensor_copy(out=cb.t[:], in_=cbf.t[:]), reads=[cbf.b], writes=[cb.b])
                S.op("pool", lambda e: e.memset(epsc.t[:], EPS), writes=[epsc.b])
                S.op("pool", lambda e: e.memset(onec.t[:], 1.0), writes=[onec.b])
                S.flush()

            def C32(name):
                k = names32.index(name)
                return cf.t[:, k * 128:(k + 1) * 128]
            ident_bf = cb.t[:, 0:128]
            ones_bf = cb.t[:, 128:256]
            rl_bf = cb.t[:, 256:384]

            def dump(name, src_tl, src_ap, shape, dt=F32):
                if name in dbg and name not in dbg_out:
                    o = nc.dram_tensor("dbg_" + name, list(shape), dt, kind="ExternalOutput").ap()
                    dbg_out[name] = o
                    S.dma("sp", o, src_ap, reads=[src_tl.b], key="dbg_" + name)


            def load_cols(sbf, dst_tl, dst_ap, src_1d, n, reads=()):
                tmpT = sbf([n, 128], F32, "lc")
                S.load("sp", tmpT, tmpT.t[:], src_1d.rearrange("(c p) -> c p", p=128), reads=reads)
                bk = bank()
                S.op("pe", lambda e: e.transpose(out=bk.t[:, 0:n], in_=tmpT.t[:], identity=cf.t[0:n, 0:n]),
                     reads=[tmpT.b, cf.b], writes=[bk.b])
                S.op("dve", lambda e: e.tensor_copy(out=dst_ap, in_=bk.t[:, 0:n]), reads=[bk.b], writes=[dst_tl.b])

            def bload(tl, src_row):
                S.load("sp", tl, tl.t[:].unsqueeze(1), src_row.partition_broadcast(128))

            modT = [sb0([128, 6 * DC], F32, "modT") for _ in range(depth)]
            A1 = [sb0([128, DC], F32, "A1") for _ in range(depth)]
            A2 = [sb0([128, DC], F32, "A2") for _ in range(depth)]

            with ExitStack() as sc:
                sb = mk_sb(sc)
                xtok = [sb([128, D], F32, "xtok") for _ in range(2)]
                xTd = [sb([128, DC, 512], F32, "xT") for _ in range(2)]
                for b in range(NB):
                    xt = xTd[b % 2]
                    for tt in range(4):
                        ti = b * 4 + tt
                        xk = xtok[ti % 2]
                        S.load("sp", xk, xk.t[:], x_in[ti * 128:(ti + 1) * 128, :])
                        for cg in range(4):
                            bk = bank()
                            for cc in range(4):
                                c = cg * 4 + cc
                                S.op("pe", lambda e, bk=bk, xk=xk, c=c, cc=cc: e.transpose(
                                    out=bk.t[:, cc * 128:(cc + 1) * 128], in_=xk.t[:, c * 128:(c + 1) * 128], identity=C32("ident")),
                                    reads=[xk.b, cf.b], writes=[bk.b])
                            outv = xt.t[:, cg * 4:(cg + 1) * 4, tt * 128:(tt + 1) * 128]
                            inv = q4(bk.t[:])
                            if cg % 2 == 0:
                                S.op("act", lambda e, o=outv, i=inv: e.activation(out=o, in_=i, func=AF.Copy), reads=[bk.b], writes=[xt.b])
                            else:
                                S.op("dve", lambda e, o=outv, i=inv: e.tensor_copy(out=o, in_=i), reads=[bk.b], writes=[xt.b])
                    S.store("sp", xres_v[:, :, b * 512:(b + 1) * 512], xt, xt.t[:], writes=[bxres[b]])
                S.flush()

            with ExitStack() as sc:
                sb = mk_sb(sc)
                cT = sb([128, DC], F32, "cT")
                sc_ = sb([128, DC], F32, "silu_c")
                adaw_t = [sb([128, 2048], F32, "adaw") for _ in range(3)]
                modr = sb([1, 6 * D], F32, "modr")
                adab = sb([1, 6 * D], F32, "adab")
                n1T = [sb([128, DC], F32, "n1T") for _ in range(depth)]
                n2T = [sb([128, DC], F32, "n2T") for _ in range(depth)]
                load_cols(sb, cT, cT.t[:], c_in, DC)
                S.op("act", lambda e: e.activation(out=sc_.t[:], in_=cT.t[:], func=AF.Silu), reads=[cT.b], writes=[sc_.b])
                ai = 0
                for l in range(depth):
                    S.load("sp", adab, adab.t[:], ada_b[l:l + 1, :])
                    for g4 in range(6):
                        bks = [bank() for _ in range(4)]
                        for kc in range(DC):
                            wt = adaw_t[ai % 3]
                            ai += 1
                            S.load("sp", wt, wt.t[:], ada_w[l, kc * 128:(kc + 1) * 128, g4 * 2048:(g4 + 1) * 2048])
                            for q in range(4):
                                S.op("pe", lambda e, bk=bks[q], wt=wt, kc=kc, q=q: e.matmul(
                                    bk.t[0:1, :], lhsT=sc_.t[:, kc:kc + 1], rhs=wt.t[:, q * 512:(q + 1) * 512],
                                    start=(kc == 0), stop=(kc == DC - 1)), reads=[sc_.b, wt.b], writes=[bks[q].b])
                        for q in range(4):
                            col = g4 * 2048 + q * 512
                            S.op("dve", lambda e, bk=bks[q], col=col: e.tensor_tensor(
                                out=modr.t[0:1, col:col + 512], in0=bk.t[0:1, :], in1=adab.t[0:1, col:col + 512], op=ALU.add),
                                reads=[bks[q].b, adab.b], writes=[modr.b])
                    S.store("sp", modrow[l:l + 1, :], modr, modr.t[:], writes=[bmod])
                    load_cols(sb, modT[l], modT[l].t[:], modrow[l], 6 * DC, reads=[bmod])
                    load_cols(sb, n1T[l], n1T[l].t[:], n1w[l], DC)
                    load_cols(sb, n2T[l], n2T[l].t[:], n2w[l], DC)
                    S.op("dve", lambda e, l=l: e.scalar_tensor_tensor(out=A1[l].t[:], in0=modT[l].t[:, DC:2 * DC], scalar=1.0,
                                                                      in1=n1T[l].t[:], op0=ALU.add, op1=ALU.mult),
                         reads=[modT[l].b, n1T[l].b], writes=[A1[l].b])
                    S.op("dve", lambda e, l=l: e.scalar_tensor_tensor(out=A2[l].t[:], in0=modT[l].t[:, 4 * DC:5 * DC], scalar=1.0,
                                                                      in1=n2T[l].t[:], op0=ALU.add, op1=ALU.mult),
                         reads=[modT[l].b, n2T[l].b], writes=[A2[l].b])
                if "mod" in dbg:
                    dump("mod", modT[0], modT[0].t[:], [128, 6 * DC])
                S.flush()

            def make_common(sb, nw):
                cm = {}
                cm["sq"] = sb([128, 8, 512], BF16, "sq")
                cm["rstd"] = sb([128, 512], F32, "rstd")
                cm["tmpf"] = [sb([128, 512], F32, "tmpf") for _ in range(3)]
                cm["tmpi"] = 0
                cm["wbuf"] = [sb([128, DC, 512], BF16, "wbuf") for _ in range(nw)]
                cm["wbi"] = 0
                return cm

            def tmp(cm):
                t = cm["tmpf"][cm["tmpi"] % 3]
                cm["tmpi"] += 1
                return t

            def load_w(cm, src_ap, kparts=DC, cols=512):
                n = len(cm["wbuf"])
                wt = cm["wbuf"][cm["wbi"] % n]
                cm["wbi"] += 1
                S.load("pool", wt, wt.t[:, 0:kparts, 0:cols], src_ap.rearrange("(c p) m -> p c m", p=128))
                return wt

            def norm_block(cm, xt, Acol, Bcol, ht):
                sq, rstd = cm["sq"], cm["rstd"]
                bk = bank()
                for half in range(2):
                    S.op("act", lambda e, half=half: e.activation(out=sq.t[:], in_=xt.t[:, half * 8:(half + 1) * 8, :], func=AF.Square),
                         reads=[xt.b], writes=[sq.b])
                    for c8 in range(8):
                        c = half * 8 + c8
                        S.op("pe", lambda e, c=c, c8=c8, bk=bk: e.matmul(bk.t[:], lhsT=ones_bf, rhs=sq.t[:, c8, :], start=(c == 0), stop=(c == DC - 1)),
                             reads=[sq.b, cb.b], writes=[bk.b])
                S.op("act", lambda e, bk=bk: e.activation(out=rstd.t[:], in_=bk.t[:], func=AF.Sqrt, bias=epsc.t[:], scale=1.0 / D),
                     reads=[bk.b, epsc.b], writes=[rstd.b])
                S.op("dve", lambda e: e.reciprocal(out=rstd.t[:], in_=rstd.t[:]), reads=[rstd.b], writes=[rstd.b])
                for c in range(DC):
                    tp = tmp(cm)
                    S.op("dve", lambda e, c=c, tp=tp: e.scalar_tensor_tensor(out=tp.t[:], in0=xt.t[:, c, :], scalar=Acol[:, c:c + 1],
                                                                             in1=rstd.t[:], op0=ALU.mult, op1=ALU.mult),
                         reads=[xt.b, rstd.b], writes=[tp.b])
                    S.op("act", lambda e, c=c, tp=tp: e.activation(out=ht.t[:, c, :], in_=tp.t[:], func=AF.Identity, bias=Bcol[:, c:c + 1]),
                         reads=[tp.b], writes=[ht.b])

            for l in range(depth):
                mod = modT[l].t
                sh1 = mod[:, 0:DC]
                gt1 = mod[:, 2 * DC:3 * DC]
                sh2 = mod[:, 3 * DC:4 * DC]
                gt2 = mod[:, 5 * DC:6 * DC]
                ly = ExitStack()
                with ly:
                    sbl = mk_sb(ly)
                    gb_all = sbl([128, NT, 32], F32, "gb_all")
                    beta_all = sbl([128, NT, 16], F32, "beta_all")
                    g_all = sbl([128, NT, 16], F32, "g_all")
                    gc_all = sbl([128, NT, 16], F32, "gc_all")
                    negegc_all = sbl([128, NT, 16], F32, "negegc")
                    erest_all = sbl([128, NT, 16], F32, "erest")
                    egl_all = sbl([128, NT, 16], F32, "egl")
                    alog_r = sbl([128, 16], F32, "alog")
                    dtb_r = sbl([128, 16], F32, "dtb")
                    nexpalog = sbl([128, 16], F32, "nexpalog")
                    qnw_c = sbl([128, 1], F32, "qnw_c")
                    knw_c = sbl([128, 1], F32, "knw_c")
                    gnw_c = sbl([128, 1], F32, "gnw_c")
                    qnw_r = sbl([128, 128], F32, "qnw_r")
                    knw_r = sbl([128, 128], F32, "knw_r")
                    kbias = sbl([128, NT], F32, "kbias")
                    mq = sbl([128, 1], F32, "mq")
                    mk = sbl([128, 1], F32, "mk")
                    convw = sbl([128, 24, 5], F32, "convw")
                    att = ExitStack()
                    with att:
                        sba = mk_sb(att)
                        kT_att = sba([128, KVH, T], BF16, "kT_att")
                        v_att = sba([128, NT, 256], BF16, "v_att")
                        with ExitStack() as sc:
                            sb = mk_sb(sc)
                            cm = make_common(sb, 2)
                            with nc.allow_non_contiguous_dma(reason="tiny param loads"):
                                bload(alog_r, a_log[l:l + 1, :])
                                bload(dtb_r, dt_bias[l:l + 1, :])
                                S.load("sp", qnw_c, qnw_c.t[:], qnw[l].rearrange("(p o) -> p o", o=1))
                                S.load("sp", knw_c, knw_c.t[:], knw[l].rearrange("(p o) -> p o", o=1))
                                S.load("sp", gnw_c, gnw_c.t[:], gnw[l].rearrange("(p o) -> p o", o=1))
                                bload(qnw_r, qnw[l:l + 1, :])
                                bload(knw_r, knw[l:l + 1, :])
                            cw5 = sb([5, 3072], F32, "cw5")
                            S.load("sp", cw5, cw5.t[:], conv_w[l])
                            bkc = bank()
                            for c in range(24):
                                S.op("pe", lambda e, c=c: e.transpose(out=bkc.t[:, c * 5:(c + 1) * 5], in_=cw5.t[0:5, c * 128:(c + 1) * 128], identity=cf.t[0:5, 0:5]),
                                     reads=[cw5.b, cf.b], writes=[bkc.b])
                            S.op("dve", lambda e: e.tensor_copy(out=convw.t[:], in_=bkc.t[:, 0:120].rearrange("p (c k) -> p c k", k=5)), reads=[bkc.b], writes=[convw.b])
                            S.op("act", lambda e: e.activation(out=nexpalog.t[:], in_=alog_r.t[:], func=AF.Exp), reads=[alog_r.b], writes=[nexpalog.b])
                            S.op("dve", lambda e: e.tensor_scalar(out=nexpalog.t[:], in0=nexpalog.t[:], scalar1=-1.0, scalar2=None, op0=ALU.mult),
                                 reads=[nexpalog.b], writes=[nexpalog.b])
                            S.op("dve", lambda e: e.tensor_reduce(out=mq.t[:], in_=qnw_r.t[:], axis=AX.X, op=ALU.max, apply_absolute_value=True),
                                 reads=[qnw_r.b], writes=[mq.b])
                            S.op("dve", lambda e: e.tensor_reduce(out=mk.t[:], in_=knw_r.t[:], axis=AX.X, op=ALU.max, apply_absolute_value=True),
                                 reads=[knw_r.b], writes=[mk.b])
                            S.op("dve", lambda e: e.scalar_tensor_tensor(out=mq.t[:], in0=mq.t[:], scalar=-math.sqrt(128.0), in1=mk.t[:],
                                                                         op0=ALU.mult, op1=ALU.mult), reads=[mq.b, mk.b], writes=[mq.b])
                            S.op("dve", lambda e: e.tensor_scalar(out=kbias.t[:], in0=mcol.t[:], scalar1=-NEG, scalar2=NEG, op0=ALU.mult, op1=ALU.add),
                                 reads=[mcol.b], writes=[kbias.b])
                            S.op("dve", lambda e: e.tensor_scalar(out=kbias.t[:], in0=kbias.t[:], scalar1=mq.t[:, 0:1], scalar2=None, op0=ALU.add),
                                 reads=[kbias.b, mq.b], writes=[kbias.b])

                            ckpt("PA.params")
                            xt = sb([128, DC, 512], F32, "xT")
                            hTs = [sb([128, DC, 512], BF16, "hT")] * 2
                            mrow32 = sb([128, 512], F32, "mrow32")
                            cosT = sb([128, 512], F32, "cosT")
                            sinT = sb([128, 512], F32, "sinT")
                            ckpt("PA.mrow_zpad")
                            stage = [sb([128, 4, 516], F32, "stage")] * 2
                            for st_ in stage[:1]:
                                S.op("pool", lambda e, st_=st_: e.memset(st_.t[:], 0.0), writes=[st_.b])
                            stage_bf = [sb([128, 4, 512], BF16, "stage_bf")] * 2
                            qn_bf = [sb([128, 512], BF16, "qn_bf") for _ in range(2)]
                            sqh = [sb([128, 512], BF16, "sqh") for _ in range(2)]
                            rs_h = [sb([128, 512], F32, "rs_h") for _ in range(2)]
                            sti = 0
                            for b in range(NB):
                                tok = slice(b * 512, (b + 1) * 512)
                                ht = hTs[b % 2]
                                if b == 0:
                                    S.load("sp", xt, xt.t[:], xres_v[:, :, tok], reads=[bxres[b]])
                                bload(mrow32, mrow_in[:, tok])
                                S.load("sp", cosT, cosT.t[:], cos_in[:, tok])
                                S.load("sp", sinT, sinT.t[:], sin_in[:, tok])
                                norm_block(cm, xt, A1[l].t, sh1, ht)
                                if b + 1 < NB:
                                    S.load("sp", xt, xt.t[:], xres_v[:, :, (b + 1) * 512:(b + 2) * 512], reads=[bxres[b + 1]])
                                ckpt("PA.norm")
                                if b == 0 and l == 0:
                                    dump("hT", ht, ht.t[:], [128, DC, 512], BF16)
                                for wt_i in range(11):
                                    col0 = wt_i * 512 if wt_i < 8 else 4128 + (wt_i - 8) * 512
                                    wt = load_w(cm, w_in[l, :, col0:col0 + 512])
                                    chunks = 2 if wt_i == 10 else 4
                                    st = stage[sti % 2]
                                    stb = stage_bf[sti % 2]
                                    sti += 1
                                    for ch in range(chunks):
                                        bk = bank()
                                        for c in range(DC):
                                            S.op("pe", lambda e, bk=bk, wt=wt, c=c, ch=ch, ht=ht: e.matmul(
                                                bk.t[:], lhsT=wt.t[:, c, ch * 128:(ch + 1) * 128], rhs=ht.t[:, c, :], start=(c == 0), stop=(c == DC - 1)),
                                                reads=[wt.b, ht.b], writes=[bk.b])
                                        if wt_i < 6:
                                            S.op("dve", lambda e, bk=bk, st=st, ch=ch, tok=tok: e.tensor_tensor(
                                                out=st.t[:, ch, 2:514], in0=bk.t[:], in1=mrow32.t[:], op=ALU.mult), reads=[bk.b, mrow32.b], writes=[st.b])
                                        elif wt_i < 8:
                                            S.op("act", lambda e, bk=bk, stb=stb, ch=ch: e.activation(out=stb.t[:, ch, :], in_=bk.t[:], func=AF.Silu),
                                                 reads=[bk.b], writes=[stb.b])
                                        else:
                                            nw = qnw_c if wt_i < 10 else knw_c
                                            sh_ = sqh[ch % 2]
                                            rsx = rs_h[ch % 2]
                                            qb_ = qn_bf[ch % 2]
                                            S.op("act", lambda e, bk=bk, sh_=sh_: e.activation(out=sh_.t[:], in_=bk.t[:], func=AF.Square),
                                                 reads=[bk.b], writes=[sh_.b])
                                            bk2 = bank()
                                            S.op("pe", lambda e, bk2=bk2, sh_=sh_: e.matmul(bk2.t[:], lhsT=ones_bf, rhs=sh_.t[:], start=True, stop=True),
                                                 reads=[sh_.b, cb.b], writes=[bk2.b])
                                            S.op("act", lambda e, bk2=bk2, rsx=rsx: e.activation(out=rsx.t[:], in_=bk2.t[:], func=AF.Sqrt, bias=epsc.t[:],
                                                                                                 scale=1.0 / 128), reads=[bk2.b, epsc.b], writes=[rsx.b])
                                            S.op("dve", lambda e, rsx=rsx: e.reciprocal(out=rsx.t[:], in_=rsx.t[:]), reads=[rsx.b], writes=[rsx.b])
                                            qf = tmp(cm)
                                            S.op("dve", lambda e, bk=bk, qf=qf, nw=nw, rsx=rsx: e.scalar_tensor_tensor(
                                                out=qf.t[:], in0=bk.t[:], scalar=nw.t[:, 0:1], in1=rsx.t[:], op0=ALU.mult, op1=ALU.mult),
                                                reads=[bk.b, nw.b, rsx.b], writes=[qf.b])
                                            S.op("act", lambda e, qf=qf, qb_=qb_: e.activation(out=qb_.t[:], in_=qf.t[:], func=AF.Copy), reads=[qf.b], writes=[qb_.b])
                                            bk3 = bank()
                                            S.op("pe", lambda e, bk3=bk3, qb_=qb_: e.matmul(bk3.t[:], lhsT=rl_bf, rhs=qb_.t[:], start=True, stop=True),
                                                 reads=[qb_.b, cb.b], writes=[bk3.b])
                                            r1 = tmp(cm)
                                            S.op("dve", lambda e, bk3=bk3, r1=r1: e.tensor_tensor(out=r1.t[:], in0=bk3.t[:], in1=sinT.t[:], op=ALU.mult),
                                                 reads=[bk3.b, sinT.b], writes=[r1.b])
                                            S.op("dve", lambda e, qf=qf: e.tensor_tensor(out=qf.t[:], in0=qf.t[:], in1=cosT.t[:], op=ALU.mult),
                                                 reads=[qf.b, cosT.b], writes=[qf.b])
                                            if wt_i < 10:
                                                S.op("dve", lambda e, qf=qf, r1=r1, stb=stb, ch=ch: e.tensor_tensor(out=stb.t[:, ch, :], in0=qf.t[:], in1=r1.t[:], op=ALU.add),
                                                     reads=[qf.b, r1.b], writes=[stb.b])
                                            else:
                                                S.op("dve", lambda e, qf=qf, r1=r1, ch=ch, tok=tok: e.tensor_tensor(out=kT_att.t[:, ch, tok], in0=qf.t[:], in1=r1.t[:], op=ALU.add),
                                                     reads=[qf.b, r1.b], writes=[kT_att.b])
                                    if wt_i < 6:
                                        lo = 0 if b == 0 else 2
                                        hi = 516 if b == NB - 1 else 514
                                        S.store("sp", pq_v[:, wt_i * 4:(wt_i + 1) * 4, b * 512 + lo:b * 512 + hi], st, st.t[:, :, lo:hi], writes=[bpq[wt_i][b]])
                                    elif wt_i < 8:
                                        S.store("sp", sz_v[:, (wt_i - 6) * 4:(wt_i - 5) * 4, tok], stb, stb.t[:], writes=[bsz[wt_i - 6][b]])
                                    elif wt_i < 10:
                                        S.store("sp", aq_v[:, (wt_i - 8) * 4:(wt_i - 7) * 4, tok], stb, stb.t[:], writes=[baq[wt_i - 8][b]])
                                    else:
                                        for tt in range(4):
                                            bk = bank()
                                            for c in range(DC):
                                                S.op("pe", lambda e, bk=bk, wt=wt, c=c, tt=tt, ht=ht: e.matmul(
                                                    bk.t[:, 0:256], lhsT=ht.t[:, c, tt * 128:(tt + 1) * 128], rhs=wt.t[:, c, 256:512], start=(c == 0), stop=(c == DC - 1)),
                                                    reads=[wt.b, ht.b], writes=[bk.b])
                                            S.op("act", lambda e, bk=bk, tt=tt, b=b: e.activation(out=v_att.t[:, b * 4 + tt, :], in_=bk.t[:, 0:256], func=AF.Copy),
                                                 reads=[bk.b], writes=[v_att.b])
                                ckpt("PA.wtiles")
                                wt = load_w(cm, w_in[l, :, 4096:4128], cols=32)
                                bk = bank()
                                for tt in range(4):
                                    for c in range(DC):
                                        S.op("pe", lambda e, bk=bk, wt=wt, c=c, tt=tt, ht=ht: e.matmul(
                                            bk.t[:, tt * 32:(tt + 1) * 32], lhsT=ht.t[:, c, tt * 128:(tt + 1) * 128], rhs=wt.t[:, c, 0:32], start=(c == 0), stop=(c == DC - 1)),
                                            reads=[wt.b, ht.b], writes=[bk.b])
                                S.op("dve", lambda e, bk=bk, b=b: e.tensor_copy(out=gb_all.t[:, b * 4:(b + 1) * 4, :], in_=q4(bk.t[:, 0:128])),
                                     reads=[bk.b], writes=[gb_all.b])

                            ckpt("PA.ba")
                            S.op("act", lambda e: e.activation(out=beta_all.t[:], in_=gb_all.t[:, :, 0:16], func=AF.Sigmoid), reads=[gb_all.b], writes=[beta_all.b])
                            S.op("dve", lambda e: e.tensor_tensor(out=beta_all.t[:], in0=beta_all.t[:], in1=mcol.t[:].unsqueeze(2).to_broadcast([128, NT, 16]), op=ALU.mult),
                                 reads=[beta_all.b, mcol.b], writes=[beta_all.b])
                            S.op("dve", lambda e: e.tensor_tensor(out=g_all.t[:], in0=gb_all.t[:, :, 16:32], in1=dtb_r.t[:].unsqueeze(1).to_broadcast([128, NT, 16]), op=ALU.add),
                                 reads=[gb_all.b, dtb_r.b], writes=[g_all.b])
                            S.op("dve", lambda e: e.tensor_scalar(out=g_all.t[:], in0=g_all.t[:], scalar1=60.0, scalar2=None, op0=ALU.min), reads=[g_all.b], writes=[g_all.b])
                            S.op("act", lambda e: e.activation(out=g_all.t[:], in_=g_all.t[:], func=AF.Exp), reads=[g_all.b], writes=[g_all.b])
                            S.op("act", lambda e: e.activation(out=g_all.t[:], in_=g_all.t[:], func=AF.Ln, bias=onec.t[:]), reads=[g_all.b, onec.b], writes=[g_all.b])
                            S.op("dve", lambda e: e.tensor_tensor(out=g_all.t[:], in0=g_all.t[:], in1=nexpalog.t[:].unsqueeze(1).to_broadcast([128, NT, 16]), op=ALU.mult),
                                 reads=[g_all.b, nexpalog.b], writes=[g_all.b])
                            ckpt("PA.gates")
                            for nm, dst in (("cum", "gc"), ("rest", "rest"), ("ones", "gl")):
                                bk = bank()
                                for t in range(NT):
                                    for d_ in range(2):
                                        cname = "ones" if nm == "ones" else nm + ("_f" if d_ == 0 else "_b")
                                        S.op("pe", lambda e, bk=bk, t=t, d_=d_, cname=cname: e.matmul(
                                            bk.t[:, t * 16 + d_ * 8:t * 16 + d_ * 8 + 8], lhsT=C32(cname), rhs=g_all.t[:, t, d_ * 8:(d_ + 1) * 8], start=True, stop=True),
                                            reads=[g_all.b, cf.b], writes=[bk.b])
                                src = bk.t[:, 0:NT * 16].rearrange("p (a c) -> p a c", c=16)
                                if dst == "gc":
                                    S.op("dve", lambda e, src=src: e.tensor_copy(out=gc_all.t[:], in_=src), reads=[bk.b], writes=[gc_all.b])
                                    S.op("act", lambda e, src=src: e.activation(out=negegc_all.t[:], in_=src, func=AF.Exp), reads=[bk.b], writes=[negegc_all.b])
                                    S.op("dve", lambda e: e.tensor_scalar(out=negegc_all.t[:], in0=negegc_all.t[:], scalar1=-1.0, scalar2=None, op0=ALU.mult),
                                         reads=[negegc_all.b], writes=[negegc_all.b])
                                elif dst == "rest":
                                    S.op("act", lambda e, src=src: e.activation(out=erest_all.t[:], in_=src, func=AF.Exp), reads=[bk.b], writes=[erest_all.b])
                                else:
                                    S.op("act", lambda e, src=src: e.activation(out=egl_all.t[:], in_=src, func=AF.Exp), reads=[bk.b], writes=[egl_all.b])
                            if l == 0:
                                dump("beta", beta_all, beta_all.t[:], [128, NT, 16])
                                dump("g", g_all, g_all.t[:], [128, NT, 16])
                                dump("gc", gc_all, gc_all.t[:], [128, NT, 16])
                                dump("kT_att", kT_att, kT_att.t[:], [128, KVH, T], BF16)
                                dump("v_att", v_att, v_att.t[:], [128, NT, 256], BF16)
                            S.flush()

                        with ExitStack() as sc:
                            sb = mk_sb(sc)
                            qblk = [sb([128, 4, 512], BF16, "qblk") for _ in range(2)]
                            pT = [sb([128, 512], BF16, "pT") for _ in range(4)]
                            accsum = [sb([128, 512], F32, "accsum") for _ in range(2)]
                            rsum = [sb([128, 512], F32, "rsum") for _ in range(2)]
                            yb = [sb([128, 4, 512], BF16, "yb") for _ in range(2)]
                            pti = 0
                            pacc = 0
                            for g in range(KVH):
                                for qb in range(NB):
                                    tok = slice(qb * 512, (qb + 1) * 512)
                                    qk_ = qblk[(g * NB + qb) % 2]
                                    ybt = yb[(g * NB + qb) % 2]
                                    S.load("sp", qk_, qk_.t[:], aq_v[:, g * 4:(g + 1) * 4, tok], reads=[baq[g][qb]])
                                    for hq in range(4):
                                        bO, bS = banks[4 + 2 * (pacc % 2)], banks[5 + 2 * (pacc % 2)]
                                        accs = accsum[pacc % 2]
                                        pacc += 1

                                        def qk(kt, g=g, qk_=qk_, hq=hq):
                                            bs_ = banks[kt % 4]
                                            S.op("pe", lambda e: e.matmul(
                                                bs_.t[:], lhsT=kT_att.t[:, g, kt * 128:(kt + 1) * 128], rhs=qk_.t[:, hq, :], start=True, stop=True),
                                                reads=[kT_att.b, qk_.b], writes=[bs_.b])
                                        qk(0)
                                        if NT > 1:
                                            qk(1)
                                        for kt in range(NT):
                                            if kt + 2 < NT:
                                                qk(kt + 2)
                                            bs_ = banks[kt % 4]
                                            p_ = pT[pti % 4]
                                            pti += 1
                                            S.op("act", lambda e, bs_=bs_, p_=p_, kt=kt: e.activation(out=p_.t[:], in_=bs_.t[:], func=AF.Exp, bias=kbias.t[:, kt:kt + 1],
                                                                                                     scale=128.0 ** -0.5), reads=[bs_.b, kbias.b], writes=[p_.b])
                                            S.op("pe", lambda e, bO=bO, g=g, kt=kt, p_=p_: e.matmul(bO.t[:], lhsT=v_att.t[:, kt, g * 128:(g + 1) * 128], rhs=p_.t[:],
                                                                                                     start=(kt == 0), stop=(kt == NT - 1)), reads=[v_att.b, p_.b], writes=[bO.b])
                                            if kt == 0:
                                                S.op("dve", lambda e, accs=accs, p_=p_: e.tensor_copy(out=accs.t[:], in_=p_.t[:]), reads=[p_.b], writes=[accs.b])
                                            else:
                                                S.op("dve", lambda e, accs=accs, p_=p_: e.tensor_tensor(out=accs.t[:], in0=accs.t[:], in1=p_.t[:], op=ALU.add),
                                                     reads=[p_.b, accs.b], writes=[accs.b])
                                        S.op("pe", lambda e, bS=bS, accs=accs: e.matmul(bS.t[:], lhsT=C32("ones"), rhs=accs.t[:], start=True, stop=True),
                                             reads=[cf.b, accs.b], writes=[bS.b])
                                        rs = rsum[hq % 2]
                                        S.op("dve", lambda e, bS=bS, rs=rs: e.reciprocal(out=rs.t[:], in_=bS.t[:]), reads=[bS.b], writes=[rs.b])
                                        S.op("dve", lambda e, bO=bO, rs=rs, ybt=ybt, hq=hq: e.tensor_tensor(out=ybt.t[:, hq, :], in0=bO.t[:], in1=rs.t[:], op=ALU.mult),
                                             reads=[bO.b, rs.b], writes=[ybt.b])
                                    S.store("sp", mix_v[:, 8 + g * 4:8 + (g + 1) * 4, tok], ybt, ybt.t[:], writes=[bmix_a[g][qb]])
                            S.flush()

                    with ExitStack() as sc:
                        sb = mk_sb(sc)
                        praw = [[sb([128, 516], F32, "praw") for _ in range(2)] for _ in range(3)]
                        acc_c = [sb([128, 512], F32, "acc_c") for _ in range(2)]
                        qT_s = sb([128, T], BF16, "qT_s")
                        kT_s = sb([128, T], BF16, "kT_s")
                        vT_b = [sb([128, 512], BF16, "vT_b") for _ in range(2)]
                        k_tok = sb([128, NT, 128], BF16, "k_tok")
                        v_tok = sb([128, NT, 128], BF16, "v_tok")
                        o_all = sb([128, NT, 128], F32, "o_all")
                        szT = [sb([128, 512], BF16, "szT") for _ in range(2)]
                        yT = [sb([128, 512], BF16, "yT") for _ in range(2)]
                        sqh = [sb([128, 512], BF16, "sqh") for _ in range(2)]
                        rs_h = [sb([128, 512], F32, "rs_h") for _ in range(2)]
                        dg2 = [sb([128, 4, 128], F32, "dg") for _ in range(2)]
                        DTt2 = [sb([128, 4, 128], F32, "DT") for _ in range(2)]
                        egr2 = [sb([128, 4, 128], F32, "egr") for _ in range(2)]
                        Gb2 = [sb([128, 4, 128], F32, "Gb") for _ in range(2)]
                        DTx2 = [[sb([128, 4, 128], F32, "DTx%d" % i) for i in range(3)] for _ in range(2)]

                        def grp(name):
                            return [sb([128, 4, 128], BF16, name) for _ in range(2)]
                        MTx = [grp("MTx%d" % i) for i in range(3)]
                        Mx = [grp("Mx%d" % i) for i in range(3)]
                        Pk = [grp("Pk%d" % i) for i in range(2)]
                        PTk = [grp("PTk%d" % i) for i in range(2)]
                        Rr = grp("Rr")
                        RTt = grp("RTt")
                        Zz = grp("Zz")
                        RING = 3
                        NTt = [[sb([128, 4, 128], BF16, "NT") for _ in range(RING)] for _ in range(2)]
                        ATt = [[sb([128, 4, 128], BF16, "AT") for _ in range(RING)] for _ in range(2)]
                        qgT = [[sb([128, 4, 128], BF16, "qgT") for _ in range(RING)] for _ in range(2)]
                        kd = [[sb([128, 4, 128], BF16, "kd") for _ in range(RING)] for _ in range(2)]
                        S32 = [sb([128, 128], F32, "S32_%d" % i) for i in range(2)]
                        Sbf = [sb([128, 128], BF16, "Sbf_%d" % i) for i in range(2)]
                        rr_ = [sb([128, 128], BF16, "r_%d" % i) for i in range(2)]
                        vn_ = [sb([128, 128], BF16, "vn_%d" % i) for i in range(2)]
                        ssq = sb([128, NT], F32, "ssq")
                        junk = sb([128, 128], F32, "junk")
                        on_bf = [sb([128, 128], BF16, "on_bf") for _ in range(2)]
                        idb = ident_bf.unsqueeze(1).to_broadcast([128, 4, 128])
                        k2c = [0]

                        def bcast_u(ap3):
                            return ap3.to_broadcast([128, 4, 128])

                        def mm4(outb, lT, rh):
                            for u in range(4):
                                S.op("pe", lambda e, outb=outb, lT=lT, rh=rh, u=u: e.matmul(
                                    outb.t[:, u * 128:(u + 1) * 128], lhsT=lT.t[:, u, :], rhs=rh.t[:, u, :], start=True, stop=True),
                                    reads=[lT.b, rh.b], writes=[outb.b])

                        def mkbank(lo):
                            st_ = [0]

                            def f():
                                b_ = banks[lo + st_[0] % 2]
                                st_[0] += 1
                                return b_
                            return f

                        def precompute(h, d_, g4, slot):
                            sfx = "_f" if d_ == 0 else "_b"
                            col = d_ * 8 + h
                            k2 = d_
                            pbank = mkbank(2 * d_)
                            dg, DTt, egr, Gb, DTx = dg2[d_], DTt2[d_], egr2[d_], Gb2[d_], DTx2[d_]
                            tsl = slice(g4 * 4, g4 * 4 + 4)
                            S.op("pool", lambda e: e.tensor_tensor(
                                out=dg.t[:], in0=C32("ident").unsqueeze(1).to_broadcast([128, 4, 128]),
                                in1=bcast_u(gc_all.t[:, tsl, col:col + 1]), op=ALU.mult), reads=[gc_all.b, cf.b], writes=[dg.b])
                            yield
                            bC, bD = pbank(), pbank()
                            for u in range(4):
                                us = slice(u * 128, (u + 1) * 128)
                                S.op("pe", lambda e, u=u, us=us: e.matmul(bC.t[:, us], lhsT=C32("ones"), rhs=dg.t[:, u, :], start=True, stop=False),
                                     reads=[dg.b, cf.b], writes=[bC.b])
                                S.op("pe", lambda e, u=u, us=us: e.matmul(bC.t[:, us], lhsT=dg.t[:, u, :], rhs=C32("negones"), start=False, stop=False),
                                     reads=[dg.b, cf.b], writes=[bC.b])
                                S.op("pe", lambda e, us=us: e.matmul(bC.t[:, us], lhsT=C32("ident"), rhs=C32("nm" + sfx), start=False, stop=True),
                                     reads=[cf.b], writes=[bC.b])
                                S.op("pe", lambda e, u=u, us=us: e.matmul(bD.t[:, us], lhsT=C32("ones"), rhs=dg.t[:, u, :], start=True, stop=True),
                                     reads=[dg.b, cf.b], writes=[bD.b])
                            yield
                            S.op("act", lambda e: e.activation(out=v4(DTt), in_=bC.t[:], func=AF.Exp), reads=[bC.b], writes=[DTt.b])
                            S.op("act", lambda e: e.activation(out=v4(egr), in_=bD.t[:], func=AF.Exp), reads=[bD.b], writes=[egr.b])
                            yield
                            bG, bKQ = pbank(), pbank()
                            for u in range(4):
                                t = g4 * 4 + u
                                ksl = kT_s.t[:, t * 128:(t + 1) * 128]
                                qsl = qT_s.t[:, t * 128:(t + 1) * 128]
                                us = slice(u * 128, (u + 1) * 128)
                                S.op("pe", lambda e, ksl=ksl, us=us: e.matmul(bG.t[:, us], lhsT=ksl, rhs=ksl, start=True, stop=True),
                                     reads=[kT_s.b], writes=[bG.b])
                                S.op("pe", lambda e, ksl=ksl, qsl=qsl, us=us: e.matmul(bKQ.t[:, us], lhsT=ksl, rhs=qsl, start=True, stop=True),
                                     reads=[kT_s.b, qT_s.b], writes=[bKQ.b])
                            yield
                            S.op("dve", lambda e: e.tensor_tensor(
                                out=Gb.t[:], in0=q4(bG.t[:]), in1=bcast_u(beta_all.t[:, tsl, col:col + 1]), op=ALU.mult),
                                reads=[bG.b, beta_all.b], writes=[Gb.b])
                            KD = kd[d_][slot]
                            S.op("pool", lambda e: e.tensor_tensor(
                                out=KD.t[:], in0=k_tok.t[:, tsl, :], in1=bcast_u(erest_all.t[:, tsl, col:col + 1]), op=ALU.mult),
                                reads=[k_tok.b, erest_all.b], writes=[KD.b])
                            yield
                            AT = ATt[d_][slot]
                            S.op("dve", lambda e: e.tensor_tensor(out=v4(AT), in0=bKQ.t[:], in1=v4(DTt), op=ALU.mult),
                                 reads=[bKQ.b, DTt.b], writes=[AT.b])
                            QG = qgT[d_][slot]
                            S.op("dve", lambda e: e.tensor_tensor(out=v4(QG), in0=qT_s.t[:, g4 * 512:(g4 + 1) * 512], in1=v4(egr), op=ALU.mult),
                                 reads=[qT_s.b, egr.b], writes=[QG.b])
                            for xi, cmn in enumerate(("cma", "cmb", "cmc")):
                                S.op("pool", lambda e, xi=xi, cmn=cmn: e.tensor_tensor(
                                    out=DTx[xi].t[:], in0=DTt.t[:], in1=C32(cmn + sfx).unsqueeze(1).to_broadcast([128, 4, 128]), op=ALU.mult),
                                    reads=[DTt.b, cf.b], writes=[DTx[xi].b])
                            yield
                            for xi in range(3):
                                eng = "dve" if xi == 0 else "pool"
                                S.op(eng, lambda e, xi=xi: e.tensor_tensor(out=MTx[xi][k2].t[:], in0=Gb.t[:], in1=DTx[xi].t[:], op=ALU.mult),
                                     reads=[Gb.b, DTx[xi].b], writes=[MTx[xi][k2].b])
                            yield
                            tb = []
                            bkA, bkB = pbank(), pbank()
                            for xi in range(3):
                                bk = bkA if xi < 2 else bkB
                                off = 512 if xi == 1 else 0
                                bkv = bk.t[:].bitcast(BF16)
                                tb.append((bk, bkv, off))
                                for u in range(4):
                                    S.op("pe", lambda e, bkv=bkv, u=u, xi=xi, off=off: e.transpose(
                                        out=bkv[:, off + u * 128:off + (u + 1) * 128], in_=MTx[xi][k2].t[:, u, :], identity=ident_bf),
                                        reads=[MTx[xi][k2].b, cb.b], writes=[bk.b])
                            yield
                            for xi in range(3):
                                bk, bkv, off = tb[xi]
                                S.op("act", lambda e, bkv=bkv, xi=xi, off=off: e.activation(out=v4(Mx[xi][k2]), in_=bkv[:, off:off + 512], func=AF.Copy),
                                     reads=[bk.b], writes=[Mx[xi][k2].b])
                            yield
                            R_, RT_ = Rr[k2], RTt[k2]
                            S.op("pool", lambda e: e.tensor_tensor(out=R_.t[:], in0=Mx[0][k2].t[:], in1=idb, op=ALU.add),
                                 reads=[Mx[0][k2].b, cb.b], writes=[R_.b])
                            S.op("pool", lambda e: e.tensor_tensor(out=RT_.t[:], in0=MTx[0][k2].t[:], in1=idb, op=ALU.add),
                                 reads=[MTx[0][k2].b, cb.b], writes=[RT_.b])
                            Pc, PTc = Mx[0][k2], MTx[0][k2]
                            for kk in range(4):
                                Pn, PTn = Pk[kk % 2][k2], PTk[kk % 2][k2]
                                b1 = pbank()
                                mm4(b1, PTc, Pc)
                                if kk < 3:
                                    b2 = pbank()
                                    mm4(b2, Pc, PTc)
                                yield
                                S.op("act", lambda e, b1=b1, Pn=Pn: e.activation(out=v4(Pn), in_=b1.t[:], func=AF.Copy), reads=[b1.b], writes=[Pn.b])
                                if kk < 3:
                                    S.op("act", lambda e, b2=b2, PTn=PTn: e.activation(out=v4(PTn), in_=b2.t[:], func=AF.Copy), reads=[b2.b], writes=[PTn.b])
                                yield
                                b3, b4 = pbank(), pbank()
                                mm4(b3, RT_, Pn)
                                mm4(b4, Pn, RT_)
                                yield
                                S.op("dve", lambda e, b3=b3: e.tensor_tensor(out=v4(R_), in0=b3.t[:], in1=v4(R_), op=ALU.add),
                                     reads=[b3.b, R_.b], writes=[R_.b])
                                S.op("dve", lambda e, b4=b4: e.tensor_tensor(out=v4(RT_), in0=b4.t[:], in1=v4(RT_), op=ALU.add),
                                     reads=[b4.b, RT_.b], writes=[RT_.b])
                                yield
                                Pc, PTc = Pn, PTn
                            b1 = pbank()
                            mm4(b1, Mx[1][k2], RT_)
                            b2 = pbank()
                            mm4(b2, MTx[1][k2], R_)
                            yield
                            Z1, Z2 = Zz[k2], Pk[0][k2]
                            S.op("act", lambda e: e.activation(out=v4(Z1), in_=b1.t[:], func=AF.Copy), reads=[b1.b], writes=[Z1.b])
                            S.op("act", lambda e: e.activation(out=v4(Z2), in_=b2.t[:], func=AF.Copy), reads=[b2.b], writes=[Z2.b])
                            yield
                            b3, b4 = pbank(), pbank()
                            mm4(b3, R_, Z1)
                            mm4(b4, RT_, Z2)
                            yield
                            S.op("dve", lambda e: e.tensor_tensor(out=v4(RT_), in0=v4(RT_), in1=b3.t[:], op=ALU.subtract),
                                 reads=[b3.b, RT_.b], writes=[RT_.b])
                            S.op("dve", lambda e: e.tensor_tensor(out=v4(R_), in0=v4(R_), in1=b4.t[:], op=ALU.subtract),
                                 reads=[b4.b, R_.b], writes=[R_.b])
                            yield
                            b5 = pbank()
                            mm4(b5, Mx[2][k2], RT_)
                            yield
                            S.op("act", lambda e: e.activation(out=v4(Z1), in_=b5.t[:], func=AF.Copy), reads=[b5.b], writes=[Z1.b])
                            yield
                            b6 = pbank()
                            mm4(b6, R_, Z1)
                            yield
                            NTg = NTt[d_][slot]
                            S.op("dve", lambda e: e.tensor_tensor(out=v4(NTg), in0=v4(RT_), in1=b6.t[:], op=ALU.subtract),
                                 reads=[b6.b, RT_.b], writes=[NTg.b])

                        def rec_group(h, d_, tiles, slot, o_written):
                            col = d_ * 8 + h
                            rbank = mkbank(4 + 2 * d_)
                            for t in tiles:
                                u = t % 4
                                ksl = kT_s.t[:, t * 128:(t + 1) * 128]
                                bk1 = rbank()
                                S.op("pe", lambda e, ksl=ksl, bk1=bk1: e.matmul(bk1.t[:, 0:128], lhsT=ksl, rhs=Sbf[d_].t[:], start=True, stop=True),
                                     reads=[kT_s.b, Sbf[d_].b], writes=[bk1.b])
                                yield
                                S.op("dve", lambda e, bk1=bk1, t=t: e.scalar_tensor_tensor(
                                    out=rr_[d_].t[:], in0=bk1.t[:, 0:128], scalar=negegc_all.t[:, t, col:col + 1], in1=v_tok.t[:, t, :], op0=ALU.mult, op1=ALU.add),
                                    reads=[bk1.b, negegc_all.b, v_tok.b], writes=[rr_[d_].b])
                                yield
                                bk2 = rbank()
                                S.op("pe", lambda e, bk2=bk2, u=u: e.matmul(bk2.t[:, 0:128], lhsT=NTt[d_][slot].t[:, u, :], rhs=rr_[d_].t[:], start=True, stop=True),
                                     reads=[NTt[d_][slot].b, rr_[d_].b], writes=[bk2.b])
                                yield
                                S.op("act", lambda e, bk2=bk2, t=t: e.activation(out=vn_[d_].t[:], in_=bk2.t[:, 0:128], func=AF.Identity, scale=beta_all.t[:, t, col:col + 1]),
                                     reads=[bk2.b, beta_all.b], writes=[vn_[d_].b])
                                yield
                                bk4 = rbank()
                                S.op("pe", lambda e, bk4=bk4, u=u: e.matmul(bk4.t[:, 0:128], lhsT=kd[d_][slot].t[:, u, :], rhs=vn_[d_].t[:], start=True, stop=True),
                                     reads=[kd[d_][slot].b, vn_[d_].b], writes=[bk4.b])
                                bk3 = bk4
                                S.op("pe", lambda e, bk3=bk3, u=u: e.matmul(bk3.t[:, 128:256], lhsT=qgT[d_][slot].t[:, u, :], rhs=Sbf[d_].t[:], start=True, stop=False),
                                     reads=[qgT[d_][slot].b, Sbf[d_].b], writes=[bk3.b])
                                S.op("pe", lambda e, bk3=bk3, u=u: e.matmul(bk3.t[:, 128:256], lhsT=ATt[d_][slot].t[:, u, :], rhs=vn_[d_].t[:], start=False, stop=True),
                                     reads=[ATt[d_][slot].b, vn_[d_].b], writes=[bk3.b])
                                yield
                                S.op("dve", lambda e, bk4=bk4, t=t: e.scalar_tensor_tensor(
                                    out=S32[d_].t[:], in0=S32[d_].t[:], scalar=egl_all.t[:, t, col:col + 1], in1=bk4.t[:, 0:128], op0=ALU.mult, op1=ALU.add),
                                    reads=[bk4.b, egl_all.b, S32[d_].b], writes=[S32[d_].b])
                                yield
                                S.op("act", lambda e: e.activation(out=Sbf[d_].t[:], in_=S32[d_].t[:], func=AF.Copy), reads=[S32[d_].b], writes=[Sbf[d_].b])
                                if t not in o_written:
                                    o_written.add(t)
                                    S.op("dve", lambda e, bk3=bk3, t=t: e.tensor_copy(out=o_all.t[:, t, :], in_=bk3.t[:, 128:256]), reads=[bk3.b], writes=[o_all.b])
                                else:
                                    S.op("dve", lambda e, bk3=bk3, t=t: e.tensor_tensor(out=o_all.t[:, t, :], in0=bk3.t[:, 128:256], in1=o_all.t[:, t, :], op=ALU.add),
                                         reads=[bk3.b, o_all.b], writes=[o_all.b])
                                yield

                        def lockstep(gens):
                            gens = list(gens)
                            while gens:
                                nxt = []
                                for g_ in gens:
                                    try:
                                        next(g_)
                                        nxt.append(g_)
                                    except StopIteration:
                                        pass
                                gens = nxt

                        for h in range(GH):
                            li = 0
                            for i3 in range(3):
                                chn = i3 * 8 + h
                                wt_i = chn // 4
                                cw = convw.t[:, chn, :]
                                for b in range(NB):
                                    tok = slice(b * 512, (b + 1) * 512)
                                    pr = praw[i3][b % 2]
                                    rd = [bpq[wt_i][b], bpq_halo]
                                    if b > 0:
                                        rd.append(bpq[wt_i][b - 1])
                                    if b < NB - 1:
                                        rd.append(bpq[wt_i][b + 1])
                                    S.load("sp", pr, pr.t[:], pqkv[chn * 128:(chn + 1) * 128, b * 512:b * 512 + 516], reads=rd)
                                    ac = acc_c[li % 2]
                                    li += 1
                                    S.op("dve", lambda e, pr=pr, cw=cw, ac=ac: e.tensor_scalar(
                                        out=ac.t[:], in0=pr.t[:, 0:512], scalar1=cw[:, 0:1], scalar2=None, op0=ALU.mult),
                                        reads=[pr.b, convw.b], writes=[ac.b])
                                    for j in range(1, 5):
                                        S.op("dve", lambda e, pr=pr, cw=cw, ac=ac, j=j: e.scalar_tensor_tensor(
                                            out=ac.t[:], in0=pr.t[:, j:j + 512], scalar=cw[:, j:j + 1], in1=ac.t[:],
                                            op0=ALU.mult, op1=ALU.add), reads=[pr.b, convw.b, ac.b], writes=[ac.b])
                                    if i3 == 2:
                                        vb = vT_b[b % 2]
                                        S.op("act", lambda e, ac=ac, vb=vb: e.activation(out=vb.t[:], in_=ac.t[:], func=AF.Silu), reads=[ac.b], writes=[vb.b])
                                        bk = bank()
                                        bkv = bk.t[:].bitcast(BF16)
                                        for u in range(4):
                                            S.op("pe", lambda e, bkv=bkv, u=u, vb=vb: e.transpose(out=bkv[:, u * 128:(u + 1) * 128], in_=vb.t[:, u * 128:(u + 1) * 128], identity=ident_bf),
                                                 reads=[vb.b, cb.b], writes=[bk.b])
                                        S.op("act", lambda e, bkv=bkv, b=b: e.activation(out=v_tok.t[:, b * 4:(b + 1) * 4, :], in_=q4(bkv[:, 0:512]), func=AF.Copy),
                                             reads=[bk.b], writes=[v_tok.b])
                                    else:
                                        dstT = qT_s if i3 == 0 else kT_s
                                        S.op("act", lambda e, ac=ac: e.activation(out=ac.t[:], in_=ac.t[:], func=AF.Silu), reads=[ac.b], writes=[ac.b])
                                        sh_ = sqh[b % 2]
                                        rsx = rs_h[b % 2]
                                        S.op("act", lambda e, ac=ac, sh_=sh_: e.activation(out=sh_.t[:], in_=ac.t[:], func=AF.Square), reads=[ac.b], writes=[sh_.b])
                                        bk2 = bank()
                                        S.op("pe", lambda e, bk2=bk2, sh_=sh_: e.matmul(bk2.t[:], lhsT=ones_bf, rhs=sh_.t[:], start=True, stop=True),
                                             reads=[sh_.b, cb.b], writes=[bk2.b])
                                        S.op("act", lambda e, bk2=bk2, rsx=rsx: e.activation(out=rsx.t[:], in_=bk2.t[:], func=AF.Sqrt, bias=epsc.t[:], scale=1.0),
                                             reads=[bk2.b, epsc.b], writes=[rsx.b])
                                        S.op("dve", lambda e, rsx=rsx: e.reciprocal(out=rsx.t[:], in_=rsx.t[:]), reads=[rsx.b], writes=[rsx.b])
                                        scq = (128.0 ** -0.5) if i3 == 0 else 1.0
                                        S.op("dve", lambda e, tok=tok, rsx=rsx, ac=ac, scq=scq, dstT=dstT: e.scalar_tensor_tensor(
                                            out=dstT.t[:, tok], in0=ac.t[:], scalar=scq, in1=rsx.t[:], op0=ALU.mult, op1=ALU.mult),
                                            reads=[ac.b, rsx.b], writes=[dstT.b])
                                        if i3 == 1:
                                            bk = bank()
                                            bkv = bk.t[:].bitcast(BF16)
                                            for u in range(4):
                                                t = b * 4 + u
                                                S.op("pe", lambda e, bkv=bkv, u=u, t=t: e.transpose(out=bkv[:, u * 128:(u + 1) * 128], in_=kT_s.t[:, t * 128:(t + 1) * 128], identity=ident_bf),
                                                     reads=[kT_s.b, cb.b], writes=[bk.b])
                                            S.op("act", lambda e, bkv=bkv, b=b: e.activation(out=k_tok.t[:, b * 4:(b + 1) * 4, :], in_=q4(bkv[:, 0:512]), func=AF.Copy),
                                                 reads=[bk.b], writes=[k_tok.b])
                            if h == 0 and l == 0:
                                dump("qT", qT_s, qT_s.t[:], [128, T], BF16)
                                dump("kT", kT_s, kT_s.t[:], [128, T], BF16)
                                dump("v_tok", v_tok, v_tok.t[:], [128, NT, 128], BF16)

                            for d_ in range(2):
                                S.op("pool", lambda e, d_=d_: e.memset(S32[d_].t[:], 0.0), writes=[S32[d_].b])
                                S.op("pool", lambda e, d_=d_: e.memset(Sbf[d_].t[:], 0.0), writes=[Sbf[d_].b])
                            o_written = set()
                            lockstep([precompute(h, 0, 0, 0), precompute(h, 1, NG - 1, 0)])
                            if h == 0 and l == 0:
                                dump("NT_f", NTt[0][0], NTt[0][0].t[:], [128, 4, 128], BF16)
                                dump("AT_f", ATt[0][0], ATt[0][0].t[:], [128, 4, 128], BF16)
                                dump("NT_b", NTt[1][0], NTt[1][0].t[:], [128, 4, 128], BF16)
                            for gi in range(NG):
                                gens = []
                                if gi + 1 < NG:
                                    gens.append(precompute(h, 0, gi + 1, (gi + 1) % RING))
                                    gens.append(precompute(h, 1, NG - 2 - gi, (gi + 1) % RING))
                                gens.append(rec_group(h, 0, [gi * 4 + s_ for s_ in range(4)], gi % RING, o_written))
                                gens.append(rec_group(h, 1, [(NG - 1 - gi) * 4 + 3 - s_ for s_ in range(4)], gi % RING, o_written))
                                lockstep(gens)
                            if h == 0 and l == 0:
                                dump("o_all", o_all, o_all.t[:], [128, NT, 128])
                            for t in range(NT):
                                S.op("act", lambda e, t=t: e.activation(out=junk.t[:], in_=o_all.t[:, t, :], func=AF.Square, accum_out=ssq.t[:, t:t + 1]),
                                     reads=[o_all.b], writes=[junk.b, ssq.b])
                            S.op("act", lambda e: e.activation(out=ssq.t[:], in_=ssq.t[:], func=AF.Sqrt, bias=epsc.t[:], scale=1.0 / 128), reads=[ssq.b, epsc.b], writes=[ssq.b])
                            S.op("dve", lambda e: e.reciprocal(out=ssq.t[:], in_=ssq.t[:]), reads=[ssq.b], writes=[ssq.b])
                            for g4 in range(NG):
                                sz = szT[g4 % 2]
                                yt = yT[g4 % 2]
                                S.load("sp", sz, sz.t[:], szs[h * 128:(h + 1) * 128, g4 * 512:(g4 + 1) * 512], reads=[bsz[h // 4][g4]])
                                bk = bank()
                                bkv = bk.t[:].bitcast(BF16)
                                for u in range(4):
                                    t = g4 * 4 + u
                                    ob = on_bf[t % 2]
                                    S.op("dve", lambda e, t=t, ob=ob: e.tensor_scalar(out=ob.t[:], in0=o_all.t[:, t, :], scalar1=ssq.t[:, t:t + 1], scalar2=None, op0=ALU.mult),
                                         reads=[o_all.b, ssq.b], writes=[ob.b])
                                    S.op("pe", lambda e, bkv=bkv, u=u, ob=ob: e.transpose(out=bkv[:, u * 128:(u + 1) * 128], in_=ob.t[:], identity=ident_bf),
                                         reads=[ob.b, cb.b], writes=[bk.b])
                                S.op("dve", lambda e, bkv=bkv, sz=sz, yt=yt: e.scalar_tensor_tensor(
                                    out=yt.t[:], in0=bkv[:, 0:512], scalar=gnw_c.t[:, 0:1], in1=sz.t[:], op0=ALU.mult, op1=ALU.mult),
                                    reads=[bk.b, gnw_c.b, sz.b], writes=[yt.b])
                                S.store("sp", mixs[h * 128:(h + 1) * 128, g4 * 512:(g4 + 1) * 512], yt, yt.t[:], writes=[bmix_g[h]])
                        S.flush()

                with ExitStack() as sc:
                    sb = mk_sb(sc)
                    cm = make_common(sb, 3)
                    xt = sb([128, DC, 512], F32, "xT")
                    hTs = [sb([128, DC, 512], BF16, "hT")] * 2
                    actT = sb([128, FC, 512], BF16, "actT")
                    last = (l == depth - 1)
                    if last:
                        xtok = [sb([128, D], F32, "xtok") for _ in range(2)]
                    for b in range(NB):
                        tok = slice(b * 512, (b + 1) * 512)
                        mt = hTs[0]
                        h2 = hTs[1]
                        S.load("sp", xt, xt.t[:], xres_v[:, :, tok], reads=[bxres[b]])
                        S.load("sp", mt, mt.t[:], mix_v[:, :, tok], reads=bmix_g + [bmix_a[0][b], bmix_a[1][b]])
                        if b == 0 and l == 0:
                            dump("mixT", mt, mt.t[:], [128, DC, 512], BF16)
                        for wi in range(4):
                            wt = load_w(cm, w_out[l, :, wi * 512:(wi + 1) * 512])
                            for ch in range(4):
                                j = wi * 4 + ch
                                bk = bank()
                                for c in range(DC):
                                    S.op("pe", lambda e, bk=bk, wt=wt, c=c, ch=ch, mt=mt: e.matmul(
                                        bk.t[:], lhsT=wt.t[:, c, ch * 128:(ch + 1) * 128], rhs=mt.t[:, c, :], start=(c == 0), stop=(c == DC - 1)),
                                        reads=[wt.b, mt.b], writes=[bk.b])
                                S.op("dve", lambda e, bk=bk, j=j: e.scalar_tensor_tensor(
                                    out=xt.t[:, j, :], in0=bk.t[:], scalar=gt1[:, j:j + 1], in1=xt.t[:, j, :], op0=ALU.mult, op1=ALU.add),
                                    reads=[bk.b, xt.b, modT[l].b], writes=[xt.b])
                        norm_block(cm, xt, A2[l].t, sh2, h2)
                        for wi in range(16):
                            wt = load_w(cm, w_up[l, :, wi * 512:(wi + 1) * 512])
                            for ch in range(4):
                                f = wi * 4 + ch
                                bk = bank()
                                for c in range(DC):
                                    S.op("pe", lambda e, bk=bk, wt=wt, c=c, ch=ch: e.matmul(
                                        bk.t[:], lhsT=wt.t[:, c, ch * 128:(ch + 1) * 128], rhs=h2.t[:, c, :], start=(c == 0), stop=(c == DC - 1)),
                                        reads=[wt.b, h2.b], writes=[bk.b])
                                r1 = tmp(cm)
                                S.op("act", lambda e, bk=bk, r1=r1: e.activation(out=r1.t[:], in_=bk.t[:], func=AF.Relu), reads=[bk.b], writes=[r1.b])
                                S.op("dve", lambda e, r1=r1, f=f: e.tensor_tensor(out=actT.t[:, f, :], in0=r1.t[:], in1=r1.t[:], op=ALU.mult),
                                     reads=[r1.b], writes=[actT.b])
                        for jg in range(4):
                            bks = [bank() for _ in range(4)]
                            for fq in range(4):
                                wt = load_w(cm, w_down[l, fq * 2048:(fq + 1) * 2048, jg * 512:(jg + 1) * 512])
                                for ch in range(4):
                                    for fc in range(16):
                                        f = fq * 16 + fc
                                        S.op("pe", lambda e, bk=bks[ch], wt=wt, fc=fc, ch=ch, f=f: e.matmul(
                                            bk.t[:], lhsT=wt.t[:, fc, ch * 128:(ch + 1) * 128], rhs=actT.t[:, f, :], start=(f == 0), stop=(f == FC - 1)),
                                            reads=[wt.b, actT.b], writes=[bks[ch].b])
                            for ch in range(4):
                                j = jg * 4 + ch
                                S.op("dve", lambda e, bk=bks[ch], j=j: e.scalar_tensor_tensor(
                                    out=xt.t[:, j, :], in0=bk.t[:], scalar=gt2[:, j:j + 1], in1=xt.t[:, j, :], op0=ALU.mult, op1=ALU.add),
                                    reads=[bks[ch].b, xt.b, modT[l].b], writes=[xt.b])
                        if not last:
                            S.store("sp", xres_v[:, :, tok], xt, xt.t[:], writes=[bxres[b]])
                        else:
                            for tt in range(4):
                                ti = b * 4 + tt
                                xk = xtok[ti % 2]
                                for cg in range(4):
                                    bk = bank()
                                    for cc in range(4):
                                        c = cg * 4 + cc
                                        S.op("pe", lambda e, bk=bk, c=c, cc=cc, tt=tt: e.transpose(
                                            out=bk.t[:, cc * 128:(cc + 1) * 128], in_=xt.t[:, c, tt * 128:(tt + 1) * 128], identity=C32("ident")),
                                            reads=[xt.b, cf.b], writes=[bk.b])
                                    if cg % 2 == 0:
                                        S.op("act", lambda e, bk=bk, xk=xk, cg=cg: e.activation(out=xk.t[:, cg * 512:(cg + 1) * 512], in_=bk.t[:], func=AF.Copy),
                                             reads=[bk.b], writes=[xk.b])
                                    else:
                                        S.op("dve", lambda e, bk=bk, xk=xk, cg=cg: e.tensor_copy(out=xk.t[:, cg * 512:(cg + 1) * 512], in_=bk.t[:]),
                                             reads=[bk.b], writes=[xk.b])
                                S.store("sp", y_out[ti * 128:(ti + 1) * 128, :], xk, xk.t[:])
                    S.flush()
    except _Stop:
        pass
    return nc, dbg_out


_PROG_CACHE = {}


def _core_inputs(x_seq, c_vec, T, shared):
    tv = x_seq.shape[0]
    xp = np.zeros((T, D), np.float32)
    xp[:tv] = x_seq
    mask = np.zeros((T,), np.float32)
    mask[:tv] = 1.0
    m = dict(shared)
    m["x"] = xp
    m["c"] = np.ascontiguousarray(c_vec, dtype=np.float32)
    m["mask_row"] = mask.reshape(1, T).copy()
    m["mask_col"] = np.ascontiguousarray(mask.reshape(T // 128, 128).T)
    return m


def run_trunk(seqs, cvecs, weights, T, depth, dbg=(), n_cores=None):
    key = (T, depth, tuple(dbg))
    if key not in _PROG_CACHE:
        _PROG_CACHE[key] = build_program(T, depth, dbg)
    nc, dbg_out = _PROG_CACHE[key]
    names32, cf_np, cb_np, cosT, sinT = make_consts(T)
    shared = {
        "ada_w": weights["ada_w"], "ada_b": weights["ada_b"], "norm1_w": weights["norm1_w"], "norm2_w": weights["norm2_w"],
        "w_in": weights["w_in"], "conv_w": weights["conv_w"],
        "a_log": np.ascontiguousarray(weights["a_log"].reshape(depth, 16)),
        "dt_bias": np.ascontiguousarray(weights["dt_bias"].reshape(depth, 16)),
        "gdn_norm_w": weights["gdn_norm_w"], "q_norm_w": weights["q_norm_w"], "k_norm_w": weights["k_norm_w"],
        "w_out": weights["w_out"], "w_up": weights["w_up"], "w_down": weights["w_down"],
        "cf": cf_np, "cb": cb_np, "cosT": cosT, "sinT": sinT,
    }
    shared = {k: np.ascontiguousarray(np.asarray(v, dtype=np.float32)) for k, v in shared.items()}
    in_maps = [_core_inputs(s, c, T, shared) for s, c in zip(seqs, cvecs)]
    res = run_bass_kernel_spmd(nc, in_maps, core_ids=list(range(len(in_maps))))
    return res


def kernel(x_prompt, x_sample, c_prompt, c_sample, ada_w, ada_b, norm1_w, norm2_w, w_in, conv_w, a_log, dt_bias,
           gdn_norm_w, q_norm_w, k_norm_w, w_out, w_up, w_down):
    x_prompt = np.asarray(x_prompt, np.float32)
    x_sample = np.asarray(x_sample, np.float32)
    c_prompt = np.asarray(c_prompt, np.float32)
    c_sample = np.asarray(c_sample, np.float32)
    depth = int(np.asarray(ada_w).shape[0])
    T = x_prompt.shape[1]
    weights = dict(ada_w=ada_w, ada_b=ada_b, norm1_w=norm1_w, norm2_w=norm2_w, w_in=w_in, conv_w=conv_w, a_log=np.asarray(a_log),
                   dt_bias=np.asarray(dt_bias), gdn_norm_w=gdn_norm_w, q_norm_w=q_norm_w, k_norm_w=k_norm_w, w_out=w_out, w_up=w_up,
                   w_down=w_down)
    seqs = [x_prompt[i] for i in range(4)] + [x_sample[i] for i in range(4)]
    cvecs = [c_prompt[i] for i in range(4)] + [c_sample[i] for i in range(4)]
    res = run_trunk(seqs, cvecs, weights, T, depth)
    ys = [r["y"] for r in res.results]
    y_prompt = np.stack([ys[i] for i in range(4)], axis=0).astype(np.float32)
    ts = x_sample.shape[1]
    y_sample = np.stack([ys[4 + i][:ts] for i in range(4)], axis=0).astype(np.float32)
    return (y_prompt, y_sample)
```

```python
import math
from contextlib import ExitStack

import numpy as np
import concourse.bass as bass
import concourse.mybir as mybir
from concourse.bass_utils import run_bass_kernel_spmd

F32 = mybir.dt.float32
BF16 = mybir.dt.bfloat16
ALU = mybir.AluOpType
AF = mybir.ActivationFunctionType
AX = mybir.AxisListType

D = 2048
DC = 16
HD = 128
GH = 8
AH = 8
KVH = 2
DFF = 8192
FC = 64
IN_COLS = 5664
EPS = 1e-6
NEG = -30000.0


class _Stop(Exception):
    pass


STOP_AFTER = [0]
CKPT = [{"PA.params", "PA.mrow_zpad", "PA.wtiles", "PA.ba", "PA.gates"}]


class Buf:
    __slots__ = ("name", "w", "r")

    def __init__(self, name=""):
        self.name = name
        self.w = None
        self.r = []


class Sched:
    ENG = ("pe", "act", "dve", "pool", "sp")

    def __init__(self, nc, es):
        self.nc = nc
        self.es = es
        self.ops = {k: [] for k in self.ENG}
        self.esem = {k: es.enter_context(nc.semaphore("s_" + k)) for k in self.ENG}
        self.ebase = {k: 0 for k in self.ENG}
        self.dsem = {}
        self.dcount = {}
        self.ddirty = set()
        self.gen = 0
        self.nops = 0

    def _deps(self, reads, writes):
        g = self.gen
        deps = []
        for b in reads:
            if b.w is not None and b.w[3] == g:
                deps.append(b.w)
        for b in writes:
            if b.w is not None and b.w[3] == g:
                deps.append(b.w)
            for t in b.r:
                if t[3] == g:
                    deps.append(t)
        return deps

    def _mark(self, tok, reads, writes):
        src = (tok[0], tok[1])
        for b in reads:
            r = b.r
            for i in range(len(r)):
                if (r[i][0], r[i][1]) == src:
                    r[i] = tok
                    break
            else:
                r.append(tok)
        for b in writes:
            b.w = tok
            b.r = []

    def op(self, eng, fn, reads=(), writes=()):
        deps = self._deps(reads, writes)
        lst = self.ops[eng]
        tok = ("E", eng, len(lst), self.gen)
        lst.append([fn, deps, None, False])
        self._mark(tok, reads, writes)
        return tok

    def dma(self, q, out, in_, reads=(), writes=(), key=None, **kw):
        deps = self._deps(reads, writes)
        if key not in self.dsem:
            self.dsem[key] = self.es.enter_context(self.nc.semaphore("d_%d" % len(self.dsem)))
            self.dcount[key] = 0
        self.dcount[key] += 16
        self.ddirty.add(key)
        tok = ("D", key, self.dcount[key], self.gen)
        sem = self.dsem[key]

        def fn(e, out=out, in_=in_, kw=kw):
            return e.dma_start(out=out, in_=in_, **kw)
        self.ops[q].append([fn, deps, (sem, 16), False])
        self._mark(tok, reads, writes)
        return tok

    def load(self, q, tl, dst_ap, src_ap, reads=(), **kw):
        return self.dma(q, dst_ap, src_ap, reads=reads, writes=[tl.b], key="ld_" + tl.b.name, **kw)

    def store(self, q, dst_ap, tl, src_ap, writes=(), **kw):
        return self.dma(q, dst_ap, src_ap, reads=[tl.b], writes=writes, key="st_" + tl.b.name, **kw)

    def flush(self):
        for eng, lst in self.ops.items():
            for rec in lst:
                for d in rec[1]:
                    if d[0] == "E" and not (d[1] == eng and eng == "pe"):
                        self.ops[d[1]][d[2]][3] = True
        cum = {}
        for eng, lst in self.ops.items():
            c = self.ebase[eng]
            arr = []
            for rec in lst:
                if rec[3]:
                    c += 1
                arr.append(c)
            cum[eng] = arr
        nc = self.nc
        with nc.Block() as block:
            def run(eng, e):
                seen = {}
                for rec in self.ops[eng]:
                    need = {}
                    for d in rec[1]:
                        if d[0] == "E":
                            if d[1] == eng and eng == "pe":
                                continue
                            s = ("E", d[1])
                            v = cum[d[1]][d[2]]
                        else:
                            s = ("D", d[1])
                            v = d[2]
                        if seen.get(s, 0) >= v:
                            continue
                        if need.get(s, 0) < v:
                            need[s] = v
                    for s, v in need.items():
                        sem = self.esem[s[1]] if s[0] == "E" else self.dsem[s[1]]
                        e.wait_ge(sem, v)
                        seen[s] = v
                    ins = rec[0](e)
                    if rec[2] is not None:
                        ins.then_inc(rec[2][0], rec[2][1])
                    elif rec[3]:
                        ins.then_inc(self.esem[eng], 1)
                if eng == "sp":
                    for key in sorted(self.ddirty):
                        e.wait_ge(self.dsem[key], self.dcount[key])

            block.tensor(lambda e: run("pe", e))
            block.scalar(lambda e: run("act", e))
            block.vector(lambda e: run("dve", e))
            block.gpsimd(lambda e: run("pool", e))
            block.sync(lambda e: run("sp", e))
        for eng in self.ENG:
            self.nops += len(self.ops[eng])
            self.ebase[eng] = cum[eng][-1] if cum[eng] else self.ebase[eng]
            self.ops[eng] = []
        self.ddirty = set()
        self.gen += 1
        if STOP_AFTER[0] and self.gen == STOP_AFTER[0]:
            raise _Stop()


class Tl:
    __slots__ = ("t", "b")

    def __init__(self, t, name=""):
        self.t = t
        self.b = Buf(name)


def make_consts(T):
    i = np.arange(128)
    J, I = np.meshgrid(i, i, indexing="ij")
    c32 = {}
    c32["ident"] = (J == I)
    c32["ones"] = np.ones((128, 128))
    c32["negones"] = -np.ones((128, 128))
    c32["cum_f"] = (J <= I)
    c32["cum_b"] = (J >= I)
    c32["rest_f"] = (J > I)
    c32["rest_b"] = (J < I)
    c32["nm_f"] = np.where(I >= J, 0.0, NEG)
    c32["nm_b"] = np.where(I <= J, 0.0, NEG)
    bd32 = (J // 32 == I // 32)
    bd64 = (J // 64 == I // 64)
    su = (I > J)
    sl = (I < J)
    for nm, st in (("f", su), ("b", sl)):
        c32["cma_" + nm] = -(st & bd32).astype(np.float64)
        c32["cmb_" + nm] = (st & bd64 & ~bd32)
        c32["cmc_" + nm] = (st & ~bd64)
    names32 = list(c32.keys())
    cf = np.concatenate([np.asarray(c32[k], np.float32) for k in names32], axis=1)
    Rl = np.zeros((128, 128), np.float32)
    for d in range(128):
        q = d // 32
        if q % 2 == 0:
            Rl[d + 32, d] = -1.0
        else:
            Rl[d - 32, d] = 1.0
    cb = np.concatenate([np.eye(128, dtype=np.float32), np.ones((128, 128), np.float32), Rl], axis=1)
    rows = T // 64
    row = np.repeat(np.arange(rows, dtype=np.float32), 64)
    col = np.tile(np.arange(64, dtype=np.float32), rows)
    inv = (10000.0 ** (-np.arange(0, 64, 2, dtype=np.float32) / 64)).astype(np.float32)
    ar = row[:, None] * inv[None, :]
    ac = col[:, None] * inv[None, :]
    ang = np.concatenate([ar, ar, ac, ac], axis=-1)
    cosT = np.ascontiguousarray(np.cos(ang).T.astype(np.float32))
    sinT = np.ascontiguousarray(np.sin(ang).T.astype(np.float32))
    return names32, cf, cb, cosT, sinT


def build_program(T, depth, dbg=()):
    assert T % 512 == 0
    NT = T // 128
    NB = T // 512
    NG = NT // 4
    names32, cf_np, cb_np, _, _ = make_consts(T)
    NCF = cf_np.shape[1]
    nc = bass.Bass("TRN2", target_bir_lowering=False)

    def din(name, shape, dt=F32):
        return nc.dram_tensor(name, list(shape), dt, kind="ExternalInput").ap()

    def dscr(name, shape, dt=F32):
        return nc.dram_tensor(name, list(shape), dt, kind="Internal").ap()

    x_in = din("x", [T, D])
    c_in = din("c", [D])
    ada_w = din("ada_w", [depth, D, 6 * D])
    ada_b = din("ada_b", [depth, 6 * D])
    n1w = din("norm1_w", [depth, D])
    n2w = din("norm2_w", [depth, D])
    w_in = din("w_in", [depth, D, IN_COLS])
    conv_w = din("conv_w", [depth, 5, 3072])
    a_log = din("a_log", [depth, 16])
    dt_bias = din("dt_bias", [depth, 16])
    gnw = din("gdn_norm_w", [depth, 128])
    qnw = din("q_norm_w", [depth, 128])
    knw = din("k_norm_w", [depth, 128])
    w_out = din("w_out", [depth, D, D])
    w_up = din("w_up", [depth, D, DFF])
    w_down = din("w_down", [depth, DFF, D])
    cf_in = din("cf", [128, NCF])
    cb_in = din("cb", [128, 384])
    cos_in = din("cosT", [128, T])
    sin_in = din("sinT", [128, T])
    mrow_in = din("mask_row", [1, T])
    mcol_in = din("mask_col", [128, NT])
    y_out = nc.dram_tensor("y", [T, D], F32, kind="ExternalOutput").ap()
    dbg_out = {}

    xres = dscr("xres", [D, T])
    pqkv = dscr("pqkv", [3072, T + 4])
    szs = dscr("szs", [1024, T], BF16)
    aqs = dscr("aqs", [1024, T], BF16)
    mixs = dscr("mixs", [D, T], BF16)
    modrow = dscr("modrow", [depth, 6 * D])
    xres_v = xres.rearrange("(c p) t -> p c t", p=128)
    pq_v = pqkv.rearrange("(c p) t -> p c t", p=128)
    sz_v = szs.rearrange("(c p) t -> p c t", p=128)
    aq_v = aqs.rearrange("(c p) t -> p c t", p=128)
    mix_v = mixs.rearrange("(c p) t -> p c t", p=128)
    bxres = [Buf("xres%d" % b) for b in range(NB)]
    bpq = [[Buf("pq") for b in range(NB)] for _ in range(6)]
    bpq_halo = Buf("pqh")
    bsz = [[Buf("sz") for b in range(NB)] for _ in range(2)]
    baq = [[Buf("aq") for b in range(NB)] for _ in range(2)]
    bmix_g = [Buf("mixg") for _ in range(GH)]
    bmix_a = [[Buf("mixa") for b in range(NB)] for _ in range(KVH)]
    bmod = Buf("modrow")

    es = ExitStack()
    try:
        with es:
            S = Sched(nc, es)
            cnt = [0]

            def ckpt(tag):
                if CKPT[0] is True or (CKPT[0] and tag in CKPT[0]):
                    S.flush()

            def mk_sb(scope):
                def sb(shape, dt=F32, name="t"):
                    cnt[0] += 1
                    nm = "%s_%d" % (name, cnt[0])
                    return Tl(scope.enter_context(nc.sbuf_tensor(nm, list(shape), dt)), nm)
                return sb
            sb0 = mk_sb(es)

            banks = [Tl(es.enter_context(nc.psum_tensor("bank%d" % i, [128, 512], F32)), "bank%d" % i) for i in range(8)]
            bank_i = [0]

            def bank():
                b = banks[bank_i[0] % 8]
                bank_i[0] += 1
                return b

            def v4(tl):
                return tl.t[:].rearrange("p a c -> p (a c)")

            def q4(ap):
                return ap.rearrange("p (a c) -> p a c", a=4)

            cf = sb0([128, NCF], F32, "cf")
            cb = sb0([128, 384], BF16, "cb")
            epsc = sb0([128, 1], F32, "eps")
            onec = sb0([128, 1], F32, "one")
            mcol = sb0([128, NT], F32, "mcol")
            S.load("sp", cf, cf.t[:], cf_in)
            S.load("sp", mcol, mcol.t[:], mcol_in)
            with ExitStack() as sc:
                sb = mk_sb(sc)
                cbf = sb([128, 384], F32, "cbf")
                S.load("sp", cbf, cbf.t[:], cb_in)
                S.op("dve", lambda e: e.tensor_copy(out=cb.t[:], in_=cbf.t[:]), reads=[cbf.b], writes=[cb.b])
                S.op("pool", lambda e: e.memset(epsc.t[:], EPS), writes=[epsc.b])
                S.op("pool", lambda e: e.memset(onec.t[:], 1.0), writes=[onec.b])
                S.flush()

            def C32(name):
                k = names32.index(name)
                return cf.t[:, k * 128:(k + 1) * 128]
            ident_bf = cb.t[:, 0:128]
            ones_bf = cb.t[:, 128:256]
            rl_bf = cb.t[:, 256:384]

            def dump(name, src_tl, src_ap, shape, dt=F32):
                if name in dbg and name not in dbg_out:
                    o = nc.dram_tensor("dbg_" + name, list(shape), dt, kind="ExternalOutput").ap()
                    dbg_out[name] = o
                    S.dma("sp", o, src_ap, reads=[src_tl.b], key="dbg_" + name)


            def load_cols(sbf, dst_tl, dst_ap, src_1d, n, reads=()):
                tmpT = sbf([n, 128], F32, "lc")
                S.load("sp", tmpT, tmpT.t[:], src_1d.rearrange("(c p) -> c p", p=128), reads=reads)
                bk = bank()
                S.op("pe", lambda e: e.transpose(out=bk.t[:, 0:n], in_=tmpT.t[:], identity=cf.t[0:n, 0:n]),
                     reads=[tmpT.b, cf.b], writes=[bk.b])
                S.op("dve", lambda e: e.tensor_copy(out=dst_ap, in_=bk.t[:, 0:n]), reads=[bk.b], writes=[dst_tl.b])

            def bload(tl, src_row):
                S.load("sp", tl, tl.t[:].unsqueeze(1), src_row.partition_broadcast(128))

            modT = [sb0([128, 6 * DC], F32, "modT") for _ in range(depth)]
            A1 = [sb0([128, DC], F32, "A1") for _ in range(depth)]
            A2 = [sb0([128, DC], F32, "A2") for _ in range(depth)]

            with ExitStack() as sc:
                sb = mk_sb(sc)
                xtok = [sb([128, D], F32, "xtok") for _ in range(2)]
                xTd = [sb([128, DC, 512], F32, "xT") for _ in range(2)]
                for b in range(NB):
                    xt = xTd[b % 2]
                    for tt in range(4):
                        ti = b * 4 + tt
                        xk = xtok[ti % 2]
                        S.load("sp", xk, xk.t[:], x_in[ti * 128:(ti + 1) * 128, :])
                        for cg in range(4):
                            bk = bank()
                            for cc in range(4):
                                c = cg * 4 + cc
                                S.op("pe", lambda e, bk=bk, xk=xk, c=c, cc=cc: e.transpose(
                                    out=bk.t[:, cc * 128:(cc + 1) * 128], in_=xk.t[:, c * 128:(c + 1) * 128], identity=C32("ident")),
                                    reads=[xk.b, cf.b], writes=[bk.b])
                            outv = xt.t[:, cg * 4:(cg + 1) * 4, tt * 128:(tt + 1) * 128]
                            inv = q4(bk.t[:])
                            if cg % 2 == 0:
                                S.op("act", lambda e, o=outv, i=inv: e.activation(out=o, in_=i, func=AF.Copy), reads=[bk.b], writes=[xt.b])
                            else:
                                S.op("dve", lambda e, o=outv, i=inv: e.tensor_copy(out=o, in_=i), reads=[bk.b], writes=[xt.b])
                    S.store("sp", xres_v[:, :, b * 512:(b + 1) * 512], xt, xt.t[:], writes=[bxres[b]])
                S.flush()

            with ExitStack() as sc:
                sb = mk_sb(sc)
                cT = sb([128, DC], F32, "cT")
                sc_ = sb([128, DC], F32, "silu_c")
                adaw_t = [sb([128, 2048], F32, "adaw") for _ in range(3)]
                modr = sb([1, 6 * D], F32, "modr")
                adab = sb([1, 6 * D], F32, "adab")
                n1T = [sb([128, DC], F32, "n1T") for _ in range(depth)]
                n2T = [sb([128, DC], F32, "n2T") for _ in range(depth)]
                load_cols(sb, cT, cT.t[:], c_in, DC)
                S.op("act", lambda e: e.activation(out=sc_.t[:], in_=cT.t[:], func=AF.Silu), reads=[cT.b], writes=[sc_.b])
                ai = 0
                for l in range(depth):
                    S.load("sp", adab, adab.t[:], ada_b[l:l + 1, :])
                    for g4 in range(6):
                        bks = [bank() for _ in range(4)]
                        for kc in range(DC):
                            wt = adaw_t[ai % 3]
                            ai += 1
                            S.load("sp", wt, wt.t[:], ada_w[l, kc * 128:(kc + 1) * 128, g4 * 2048:(g4 + 1) * 2048])
                            for q in range(4):
                                S.op("pe", lambda e, bk=bks[q], wt=wt, kc=kc, q=q: e.matmul(
                                    bk.t[0:1, :], lhsT=sc_.t[:, kc:kc + 1], rhs=wt.t[:, q * 512:(q + 1) * 512],
                                    start=(kc == 0), stop=(kc == DC - 1)), reads=[sc_.b, wt.b], writes=[bks[q].b])
                        for q in range(4):
                            col = g4 * 2048 + q * 512
                            S.op("dve", lambda e, bk=bks[q], col=col: e.tensor_tensor(
                                out=modr.t[0:1, col:col + 512], in0=bk.t[0:1, :], in1=adab.t[0:1, col:col + 512], op=ALU.add),
                                reads=[bks[q].b, adab.b], writes=[modr.b])
                    S.store("sp", modrow[l:l + 1, :], modr, modr.t[:], writes=[bmod])
                    load_cols(sb, modT[l], modT[l].t[:], modrow[l], 6 * DC, reads=[bmod])
                    load_cols(sb, n1T[l], n1T[l].t[:], n1w[l], DC)
                    load_cols(sb, n2T[l], n2T[l].t[:], n2w[l], DC)
                    S.op("dve", lambda e, l=l: e.scalar_tensor_tensor(out=A1[l].t[:], in0=modT[l].t[:, DC:2 * DC], scalar=1.0,
                                                                      in1=n1T[l].t[:], op0=ALU.add, op1=ALU.mult),
                         reads=[modT[l].b, n1T[l].b], writes=[A1[l].b])
                    S.op("dve", lambda e, l=l: e.scalar_tensor_tensor(out=A2[l].t[:], in0=modT[l].t[:, 4 * DC:5 * DC], scalar=1.0,
                                                                      in1=n2T[l].t[:], op0=ALU.add, op1=ALU.mult),
                         reads=[modT[l].b, n2T[l].b], writes=[A2[l].b])
                if "mod" in dbg:
                    dump("mod", modT[0], modT[0].t[:], [128, 6 * DC])
                S.flush()

            def make_common(sb, nw):
                cm = {}
                cm["sq"] = sb([128, 8, 512], BF16, "sq")
                cm["rstd"] = sb([128, 512], F32, "rstd")
                cm["tmpf"] = [sb([128, 512], F32, "tmpf") for _ in range(3)]
                cm["tmpi"] = 0
                cm["wbuf"] = [sb([128, DC, 512], BF16, "wbuf") for _ in range(nw)]
                cm["wbi"] = 0
                return cm

            def tmp(cm):
                t = cm["tmpf"][cm["tmpi"] % 3]
                cm["tmpi"] += 1
                return t

            def load_w(cm, src_ap, kparts=DC, cols=512):
                n = len(cm["wbuf"])
                wt = cm["wbuf"][cm["wbi"] % n]
                cm["wbi"] += 1
                S.load("pool", wt, wt.t[:, 0:kparts, 0:cols], src_ap.rearrange("(c p) m -> p c m", p=128))
                return wt

            def norm_block(cm, xt, Acol, Bcol, ht):
                sq, rstd = cm["sq"], cm["rstd"]
                bk = bank()
                for half in range(2):
                    S.op("act", lambda e, half=half: e.activation(out=sq.t[:], in_=xt.t[:, half * 8:(half + 1) * 8, :], func=AF.Square),
                         reads=[xt.b], writes=[sq.b])
                    for c8 in range(8):
                        c = half * 8 + c8
                        S.op("pe", lambda e, c=c, c8=c8, bk=bk: e.matmul(bk.t[:], lhsT=ones_bf, rhs=sq.t[:, c8, :], start=(c == 0), stop=(c == DC - 1)),
                             reads=[sq.b, cb.b], writes=[bk.b])
                S.op("act", lambda e, bk=bk: e.activation(out=rstd.t[:], in_=bk.t[:], func=AF.Sqrt, bias=epsc.t[:], scale=1.0 / D),
                     reads=[bk.b, epsc.b], writes=[rstd.b])
                S.op("dve", lambda e: e.reciprocal(out=rstd.t[:], in_=rstd.t[:]), reads=[rstd.b], writes=[rstd.b])
                for c in range(DC):
                    tp = tmp(cm)
                    S.op("dve", lambda e, c=c, tp=tp: e.scalar_tensor_tensor(out=tp.t[:], in0=xt.t[:, c, :], scalar=Acol[:, c:c + 1],
                                                                             in1=rstd.t[:], op0=ALU.mult, op1=ALU.mult),
                         reads=[xt.b, rstd.b], writes=[tp.b])
                    S.op("act", lambda e, c=c, tp=tp: e.activation(out=ht.t[:, c, :], in_=tp.t[:], func=AF.Identity, bias=Bcol[:, c:c + 1]),
                         reads=[tp.b], writes=[ht.b])

            for l in range(depth):
                mod = modT[l].t
                sh1 = mod[:, 0:DC]
                gt1 = mod[:, 2 * DC:3 * DC]
                sh2 = mod[:, 3 * DC:4 * DC]
                gt2 = mod[:, 5 * DC:6 * DC]
                ly = ExitStack()
                with ly:
                    sbl = mk_sb(ly)
                    gb_all = sbl([128, NT, 32], F32, "gb_all")
                    beta_all = sbl([128, NT, 16], F32, "beta_all")
                    g_all = sbl([128, NT, 16], F32, "g_all")
                    gc_all = sbl([128, NT, 16], F32, "gc_all")
                    negegc_all = sbl([128, NT, 16], F32, "negegc")
                    erest_all = sbl([128, NT, 16], F32, "erest")
                    egl_all = sbl([128, NT, 16], F32, "egl")
                    alog_r = sbl([128, 16], F32, "alog")
                    dtb_r = sbl([128, 16], F32, "dtb")
                    nexpalog = sbl([128, 16], F32, "nexpalog")
                    qnw_c = sbl([128, 1], F32, "qnw_c")
                    knw_c = sbl([128, 1], F32, "knw_c")
                    gnw_c = sbl([128, 1], F32, "gnw_c")
                    qnw_r = sbl([128, 128], F32, "qnw_r")
                    knw_r = sbl([128, 128], F32, "knw_r")
                    kbias = sbl([128, NT], F32, "kbias")
                    mq = sbl([128, 1], F32, "mq")
                    mk = sbl([128, 1], F32, "mk")
                    convw = sbl([128, 24, 5], F32, "convw")
                    att = ExitStack()
                    with att:
                        sba = mk_sb(att)
                        kT_att = sba([128, KVH, T], BF16, "kT_att")
                        v_att = sba([128, NT, 256], BF16, "v_att")
                        with ExitStack() as sc:
                            sb = mk_sb(sc)
                            cm = make_common(sb, 2)
                            with nc.allow_non_contiguous_dma(reason="tiny param loads"):
                                bload(alog_r, a_log[l:l + 1, :])
                                bload(dtb_r, dt_bias[l:l + 1, :])
                                S.load("sp", qnw_c, qnw_c.t[:], qnw[l].rearrange("(p o) -> p o", o=1))
                                S.load("sp", knw_c, knw_c.t[:], knw[l].rearrange("(p o) -> p o", o=1))
                                S.load("sp", gnw_c, gnw_c.t[:], gnw[l].rearrange("(p o) -> p o", o=1))
                                bload(qnw_r, qnw[l:l + 1, :])
                                bload(knw_r, knw[l:l + 1, :])
                            cw5 = sb([5, 3072], F32, "cw5")
                            S.load("sp", cw5, cw5.t[:], conv_w[l])
                            bkc = bank()
                            for c in range(24):
                                S.op("pe", lambda e, c=c: e.transpose(out=bkc.t[:, c * 5:(c + 1) * 5], in_=cw5.t[0:5, c * 128:(c + 1) * 128], identity=cf.t[0:5, 0:5]),
                                     reads=[cw5.b, cf.b], writes=[bkc.b])
                            S.op("dve", lambda e: e.tensor_copy(out=convw.t[:], in_=bkc.t[:, 0:120].rearrange("p (c k) -> p c k", k=5)), reads=[bkc.b], writes=[convw.b])
                            S.op("act", lambda e: e.activation(out=nexpalog.t[:], in_=alog_r.t[:], func=AF.Exp), reads=[alog_r.b], writes=[nexpalog.b])
                            S.op("dve", lambda e: e.tensor_scalar(out=nexpalog.t[:], in0=nexpalog.t[:], scalar1=-1.0, scalar2=None, op0=ALU.mult),
                                 reads=[nexpalog.b], writes=[nexpalog.b])
                            S.op("dve", lambda e: e.tensor_reduce(out=mq.t[:], in_=qnw_r.t[:], axis=AX.X, op=ALU.max, apply_absolute_value=True),
                                 reads=[qnw_r.b], writes=[mq.b])
                            S.op("dve", lambda e: e.tensor_reduce(out=mk.t[:], in_=knw_r.t[:], axis=AX.X, op=ALU.max, apply_absolute_value=True),
                                 reads=[knw_r.b], writes=[mk.b])
                            S.op("dve", lambda e: e.scalar_tensor_tensor(out=mq.t[:], in0=mq.t[:], scalar=-math.sqrt(128.0), in1=mk.t[:],
                                                                         op0=ALU.mult, op1=ALU.mult), reads=[mq.b, mk.b], writes=[mq.b])
                            S.op("dve", lambda e: e.tensor_scalar(out=kbias.t[:], in0=mcol.t[:], scalar1=-NEG, scalar2=NEG, op0=ALU.mult, op1=ALU.add),
                                 reads=[mcol.b], writes=[kbias.b])
                            S.op("dve", lambda e: e.tensor_scalar(out=kbias.t[:], in0=kbias.t[:], scalar1=mq.t[:, 0:1], scalar2=None, op0=ALU.add),
                                 reads=[kbias.b, mq.b], writes=[kbias.b])

                            ckpt("PA.params")
                            xt = sb([128, DC, 512], F32, "xT")
                            hTs = [sb([128, DC, 512], BF16, "hT")] * 2
                            mrow32 = sb([128, 512], F32, "mrow32")
                            cosT = sb([128, 512], F32, "cosT")
                            sinT = sb([128, 512], F32, "sinT")
                            ckpt("PA.mrow_zpad")
                            stage = [sb([128, 4, 516], F32, "stage")] * 2
                            for st_ in stage[:1]:
                                S.op("pool", lambda e, st_=st_: e.memset(st_.t[:], 0.0), writes=[st_.b])
                            stage_bf = [sb([128, 4, 512], BF16, "stage_bf")] * 2
                            qn_bf = [sb([128, 512], BF16, "qn_bf") for _ in range(2)]
                            sqh = [sb([128, 512], BF16, "sqh") for _ in range(2)]
                            rs_h = [sb([128, 512], F32, "rs_h") for _ in range(2)]
                            sti = 0
                            for b in range(NB):
                                tok = slice(b * 512, (b + 1) * 512)
                                ht = hTs[b % 2]
                                if b == 0:
                                    S.load("sp", xt, xt.t[:], xres_v[:, :, tok], reads=[bxres[b]])
                                bload(mrow32, mrow_in[:, tok])
                                S.load("sp", cosT, cosT.t[:], cos_in[:, tok])
                                S.load("sp", sinT, sinT.t[:], sin_in[:, tok])
                                norm_block(cm, xt, A1[l].t, sh1, ht)
                                if b + 1 < NB:
                                    S.load("sp", xt, xt.t[:], xres_v[:, :, (b + 1) * 512:(b + 2) * 512], reads=[bxres[b + 1]])
                                ckpt("PA.norm")
                                if b == 0 and l == 0:
                                    dump("hT", ht, ht.t[:], [128, DC, 512], BF16)
                                for wt_i in range(11):
                                    col0 = wt_i * 512 if wt_i < 8 else 4128 + (wt_i - 8) * 512
                                    wt = load_w(cm, w_in[l, :, col0:col0 + 512])
                                    chunks = 2 if wt_i == 10 else 4
                                    st = stage[sti % 2]
                                    stb = stage_bf[sti % 2]
                                    sti += 1
                                    for ch in range(chunks):
                                        bk = bank()
                                        for c in range(DC):
                                            S.op("pe", lambda e, bk=bk, wt=wt, c=c, ch=ch, ht=ht: e.matmul(
                                                bk.t[:], lhsT=wt.t[:, c, ch * 128:(ch + 1) * 128], rhs=ht.t[:, c, :], start=(c == 0), stop=(c == DC - 1)),
                                                reads=[wt.b, ht.b], writes=[bk.b])
                                        if wt_i < 6:
                                            S.op("dve", lambda e, bk=bk, st=st, ch=ch, tok=tok: e.tensor_tensor(
                                                out=st.t[:, ch, 2:514], in0=bk.t[:], in1=mrow32.t[:], op=ALU.mult), reads=[bk.b, mrow32.b], writes=[st.b])
                                        elif wt_i < 8:
                                            S.op("act", lambda e, bk=bk, stb=stb, ch=ch: e.activation(out=stb.t[:, ch, :], in_=bk.t[:], func=AF.Silu),
                                                 reads=[bk.b], writes=[stb.b])
                                        else:
                                            nw = qnw_c if wt_i < 10 else knw_c
                                            sh_ = sqh[ch % 2]
                                            rsx = rs_h[ch % 2]
                                            qb_ = qn_bf[ch % 2]
                                            S.op("act", lambda e, bk=bk, sh_=sh_: e.activation(out=sh_.t[:], in_=bk.t[:], func=AF.Square),
                                                 reads=[bk.b], writes=[sh_.b])
                                            bk2 = bank()
                                            S.op("pe", lambda e, bk2=bk2, sh_=sh_: e.matmul(bk2.t[:], lhsT=ones_bf, rhs=sh_.t[:], start=True, stop=True),
                                                 reads=[sh_.b, cb.b], writes=[bk2.b])
                                            S.op("act", lambda e, bk2=bk2, rsx=rsx: e.activation(out=rsx.t[:], in_=bk2.t[:], func=AF.Sqrt, bias=epsc.t[:],
                                                                                                 scale=1.0 / 128), reads=[bk2.b, epsc.b], writes=[rsx.b])
                                            S.op("dve", lambda e, rsx=rsx: e.reciprocal(out=rsx.t[:], in_=rsx.t[:]), reads=[rsx.b], writes=[rsx.b])
                                            qf = tmp(cm)
                                            S.op("dve", lambda e, bk=bk, qf=qf, nw=nw, rsx=rsx: e.scalar_tensor_tensor(
                                                out=qf.t[:], in0=bk.t[:], scalar=nw.t[:, 0:1], in1=rsx.t[:], op0=ALU.mult, op1=ALU.mult),
                                                reads=[bk.b, nw.b, rsx.b], writes=[qf.b])
                                            S.op("act", lambda e, qf=qf, qb_=qb_: e.activation(out=qb_.t[:], in_=qf.t[:], func=AF.Copy), reads=[qf.b], writes=[qb_.b])
                                            bk3 = bank()
                                            S.op("pe", lambda e, bk3=bk3, qb_=qb_: e.matmul(bk3.t[:], lhsT=rl_bf, rhs=qb_.t[:], start=True, stop=True),
                                                 reads=[qb_.b, cb.b], writes=[bk3.b])
                                            r1 = tmp(cm)
                                            S.op("dve", lambda e, bk3=bk3, r1=r1: e.tensor_tensor(out=r1.t[:], in0=bk3.t[:], in1=sinT.t[:], op=ALU.mult),
                                                 reads=[bk3.b, sinT.b], writes=[r1.b])
                                            S.op("dve", lambda e, qf=qf: e.tensor_tensor(out=qf.t[:], in0=qf.t[:], in1=cosT.t[:], op=ALU.mult),
                                                 reads=[qf.b, cosT.b], writes=[qf.b])
                                            if wt_i < 10:
                                                S.op("dve", lambda e, qf=qf, r1=r1, stb=stb, ch=ch: e.tensor_tensor(out=stb.t[:, ch, :], in0=qf.t[:], in1=r1.t[:], op=ALU.add),
                                                     reads=[qf.b, r1.b], writes=[stb.b])
                                            else:
                                                S.op("dve", lambda e, qf=qf, r1=r1, ch=ch, tok=tok: e.tensor_tensor(out=kT_att.t[:, ch, tok], in0=qf.t[:], in1=r1.t[:], op=ALU.add),
                                                     reads=[qf.b, r1.b], writes=[kT_att.b])
                                    if wt_i < 6:
                                        lo = 0 if b == 0 else 2
                                        hi = 516 if b == NB - 1 else 514
                                        S.store("sp", pq_v[:, wt_i * 4:(wt_i + 1) * 4, b * 512 + lo:b * 512 + hi], st, st.t[:, :, lo:hi], writes=[bpq[wt_i][b]])
                                    elif wt_i < 8:
                                        S.store("sp", sz_v[:, (wt_i - 6) * 4:(wt_i - 5) * 4, tok], stb, stb.t[:], writes=[bsz[wt_i - 6][b]])
                                    elif wt_i < 10:
                                        S.store("sp", aq_v[:, (wt_i - 8) * 4:(wt_i - 7) * 4, tok], stb, stb.t[:], writes=[baq[wt_i - 8][b]])
                                    else:
                                        for tt in range(4):
                                            bk = bank()
                                            for c in range(DC):
                                                S.op("pe", lambda e, bk=bk, wt=wt, c=c, tt=tt, ht=ht: e.matmul(
                                                    bk.t[:, 0:256], lhsT=ht.t[:, c, tt * 128:(tt + 1) * 128], rhs=wt.t[:, c, 256:512], start=(c == 0), stop=(c == DC - 1)),
                                                    reads=[wt.b, ht.b], writes=[bk.b])
                                            S.op("act", lambda e, bk=bk, tt=tt, b=b: e.activation(out=v_att.t[:, b * 4 + tt, :], in_=bk.t[:, 0:256], func=AF.Copy),
                                                 reads=[bk.b], writes=[v_att.b])
                                ckpt("PA.wtiles")
                                wt = load_w(cm, w_in[l, :, 4096:4128], cols=32)
                                bk = bank()
                                for tt in range(4):
                                    for c in range(DC):
                                        S.op("pe", lambda e, bk=bk, wt=wt, c=c, tt=tt, ht=ht: e.matmul(
                                            bk.t[:, tt * 32:(tt + 1) * 32], lhsT=ht.t[:, c, tt * 128:(tt + 1) * 128], rhs=wt.t[:, c, 0:32], start=(c == 0), stop=(c == DC - 1)),
                                            reads=[wt.b, ht.b], writes=[bk.b])
                                S.op("dve", lambda e, bk=bk, b=b: e.tensor_copy(out=gb_all.t[:, b * 4:(b + 1) * 4, :], in_=q4(bk.t[:, 0:128])),
                                     reads=[bk.b], writes=[gb_all.b])

                            ckpt("PA.ba")
                            S.op("act", lambda e: e.activation(out=beta_all.t[:], in_=gb_all.t[:, :, 0:16], func=AF.Sigmoid), reads=[gb_all.b], writes=[beta_all.b])
                            S.op("dve", lambda e: e.tensor_tensor(out=beta_all.t[:], in0=beta_all.t[:], in1=mcol.t[:].unsqueeze(2).to_broadcast([128, NT, 16]), op=ALU.mult),
                                 reads=[beta_all.b, mcol.b], writes=[beta_all.b])
                            S.op("dve", lambda e: e.tensor_tensor(out=g_all.t[:], in0=gb_all.t[:, :, 16:32], in1=dtb_r.t[:].unsqueeze(1).to_broadcast([128, NT, 16]), op=ALU.add),
                                 reads=[gb_all.b, dtb_r.b], writes=[g_all.b])
                            S.op("dve", lambda e: e.tensor_scalar(out=g_all.t[:], in0=g_all.t[:], scalar1=60.0, scalar2=None, op0=ALU.min), reads=[g_all.b], writes=[g_all.b])
                            S.op("act", lambda e: e.activation(out=g_all.t[:], in_=g_all.t[:], func=AF.Exp), reads=[g_all.b], writes=[g_all.b])
                            S.op("act", lambda e: e.activation(out=g_all.t[:], in_=g_all.t[:], func=AF.Ln, bias=onec.t[:]), reads=[g_all.b, onec.b], writes=[g_all.b])
                            S.op("dve", lambda e: e.tensor_tensor(out=g_all.t[:], in0=g_all.t[:], in1=nexpalog.t[:].unsqueeze(1).to_broadcast([128, NT, 16]), op=ALU.mult),
                                 reads=[g_all.b, nexpalog.b], writes=[g_all.b])
                            ckpt("PA.gates")
                            for nm, dst in (("cum", "gc"), ("rest", "rest"), ("ones", "gl")):
                                bk = bank()
                                for t in range(NT):
                                    for d_ in range(2):
                                        cname = "ones" if nm == "ones" else nm + ("_f" if d_ == 0 else "_b")
                                        S.op("pe", lambda e, bk=bk, t=t, d_=d_, cname=cname: e.matmul(
                                            bk.t[:, t * 16 + d_ * 8:t * 16 + d_ * 8 + 8], lhsT=C32(cname), rhs=g_all.t[:, t, d_ * 8:(d_ + 1) * 8], start=True, stop=True),
                                            reads=[g_all.b, cf.b], writes=[bk.b])
                                src = bk.t[:, 0:NT * 16].rearrange("p (a c) -> p a c", c=16)
                                if dst == "gc":
                                    S.op("dve", lambda e, src=src: e.tensor_copy(out=gc_all.t[:], in_=src), reads=[bk.b], writes=[gc_all.b])
                                    S.op("act", lambda e, src=src: e.activation(out=negegc_all.t[:], in_=src, func=AF.Exp), reads=[bk.b], writes=[negegc_all.b])
                                    S.op("dve", lambda e: e.tensor_scalar(out=negegc_all.t[:], in0=negegc_all.t[:], scalar1=-1.0, scalar2=None, op0=ALU.mult),
                                         reads=[negegc_all.b], writes=[negegc_all.b])
                                elif dst == "rest":
                                    S.op("act", lambda e, src=src: e.activation(out=erest_all.t[:], in_=src, func=AF.Exp), reads=[bk.b], writes=[erest_all.b])
                                else:
                                    S.op("act", lambda e, src=src: e.activation(out=egl_all.t[:], in_=src, func=AF.Exp), reads=[bk.b], writes=[egl_all.b])
                            if l == 0:
                                dump("beta", beta_all, beta_all.t[:], [128, NT, 16])
                                dump("g", g_all, g_all.t[:], [128, NT, 16])
                                dump("gc", gc_all, gc_all.t[:], [128, NT, 16])
                                dump("kT_att", kT_att, kT_att.t[:], [128, KVH, T], BF16)
                                dump("v_att", v_att, v_att.t[:], [128, NT, 256], BF16)
                            S.flush()

                        with ExitStack() as sc:
                            sb = mk_sb(sc)
                            qblk = [sb([128, 4, 512], BF16, "qblk") for _ in range(2)]
                            pT = [sb([128, 512], BF16, "pT") for _ in range(4)]
                            accsum = [sb([128, 512], F32, "accsum") for _ in range(2)]
                            rsum = [sb([128, 512], F32, "rsum") for _ in range(2)]
                            yb = [sb([128, 4, 512], BF16, "yb") for _ in range(2)]
                            pti = 0
                            pacc = 0
                            for g in range(KVH):
                                for qb in range(NB):
                                    tok = slice(qb * 512, (qb + 1) * 512)
                                    qk_ = qblk[(g * NB + qb) % 2]
                                    ybt = yb[(g * NB + qb) % 2]
                                    S.load("sp", qk_, qk_.t[:], aq_v[:, g * 4:(g + 1) * 4, tok], reads=[baq[g][qb]])
                                    for hq in range(4):
                                        bO, bS = banks[4 + 2 * (pacc % 2)], banks[5 + 2 * (pacc % 2)]
                                        accs = accsum[pacc % 2]
                                        pacc += 1

                                        def qk(kt, g=g, qk_=qk_, hq=hq):
                                            bs_ = banks[kt % 4]
                                            S.op("pe", lambda e: e.matmul(
                                                bs_.t[:], lhsT=kT_att.t[:, g, kt * 128:(kt + 1) * 128], rhs=qk_.t[:, hq, :], start=True, stop=True),
                                                reads=[kT_att.b, qk_.b], writes=[bs_.b])
                                        qk(0)
                                        if NT > 1:
                                            qk(1)
                                        for kt in range(NT):
                                            if kt + 2 < NT:
                                                qk(kt + 2)
                                            bs_ = banks[kt % 4]
                                            p_ = pT[pti % 4]
                                            pti += 1
                                            S.op("act", lambda e, bs_=bs_, p_=p_, kt=kt: e.activation(out=p_.t[:], in_=bs_.t[:], func=AF.Exp, bias=kbias.t[:, kt:kt + 1],
                                                                                                     scale=128.0 ** -0.5), reads=[bs_.b, kbias.b], writes=[p_.b])
                                            S.op("pe", lambda e, bO=bO, g=g, kt=kt, p_=p_: e.matmul(bO.t[:], lhsT=v_att.t[:, kt, g * 128:(g + 1) * 128], rhs=p_.t[:],
                                                                                                     start=(kt == 0), stop=(kt == NT - 1)), reads=[v_att.b, p_.b], writes=[bO.b])
                                            if kt == 0:
                                                S.op("dve", lambda e, accs=accs, p_=p_: e.tensor_copy(out=accs.t[:], in_=p_.t[:]), reads=[p_.b], writes=[accs.b])
                                            else:
                                                S.op("dve", lambda e, accs=accs, p_=p_: e.tensor_tensor(out=accs.t[:], in0=accs.t[:], in1=p_.t[:], op=ALU.add),
                                                     reads=[p_.b, accs.b], writes=[accs.b])
                                        S.op("pe", lambda e, bS=bS, accs=accs: e.matmul(bS.t[:], lhsT=C32("ones"), rhs=accs.t[:], start=True, stop=True),
                                             reads=[cf.b, accs.b], writes=[bS.b])
                                        rs = rsum[hq % 2]
                                        S.op("dve", lambda e, bS=bS, rs=rs: e.reciprocal(out=rs.t[:], in_=bS.t[:]), reads=[bS.b], writes=[rs.b])
                                        S.op("dve", lambda e, bO=bO, rs=rs, ybt=ybt, hq=hq: e.tensor_tensor(out=ybt.t[:, hq, :], in0=bO.t[:], in1=rs.t[:], op=ALU.mult),
                                             reads=[bO.b, rs.b], writes=[ybt.b])
                                    S.store("sp", mix_v[:, 8 + g * 4:8 + (g + 1) * 4, tok], ybt, ybt.t[:], writes=[bmix_a[g][qb]])
                            S.flush()

                    with ExitStack() as sc:
                        sb = mk_sb(sc)
                        praw = [[sb([128, 516], F32, "praw") for _ in range(2)] for _ in range(3)]
                        acc_c = [sb([128, 512], F32, "acc_c") for _ in range(2)]
                        qT_s = sb([128, T], BF16, "qT_s")
                        kT_s = sb([128, T], BF16, "kT_s")
                        vT_b = [sb([128, 512], BF16, "vT_b") for _ in range(2)]
                        k_tok = sb([128, NT, 128], BF16, "k_tok")
                        v_tok = sb([128, NT, 128], BF16, "v_tok")
                        o_all = sb([128, NT, 128], F32, "o_all")
                        szT = [sb([128, 512], BF16, "szT") for _ in range(2)]
                        yT = [sb([128, 512], BF16, "yT") for _ in range(2)]
                        sqh = [sb([128, 512], BF16, "sqh") for _ in range(2)]
                        rs_h = [sb([128, 512], F32, "rs_h") for _ in range(2)]
                        dg2 = [sb([128, 4, 128], F32, "dg") for _ in range(2)]
                        DTt2 = [sb([128, 4, 128], F32, "DT") for _ in range(2)]
                        egr2 = [sb([128, 4, 128], F32, "egr") for _ in range(2)]
                        Gb2 = [sb([128, 4, 128], F32, "Gb") for _ in range(2)]
                        DTx2 = [[sb([128, 4, 128], F32, "DTx%d" % i) for i in range(3)] for _ in range(2)]

                        def grp(name):
                            return [sb([128, 4, 128], BF16, name) for _ in range(2)]
                        MTx = [grp("MTx%d" % i) for i in range(3)]
                        Mx = [grp("Mx%d" % i) for i in range(3)]
                        Pk = [grp("Pk%d" % i) for i in range(2)]
                        PTk = [grp("PTk%d" % i) for i in range(2)]
                        Rr = grp("Rr")
                        RTt = grp("RTt")
                        Zz = grp("Zz")
                        RING = 3
                        NTt = [[sb([128, 4, 128], BF16, "NT") for _ in range(RING)] for _ in range(2)]
                        ATt = [[sb([128, 4, 128], BF16, "AT") for _ in range(RING)] for _ in range(2)]
                        qgT = [[sb([128, 4, 128], BF16, "qgT") for _ in range(RING)] for _ in range(2)]
                        kd = [[sb([128, 4, 128], BF16, "kd") for _ in range(RING)] for _ in range(2)]
                        S32 = [sb([128, 128], F32, "S32_%d" % i) for i in range(2)]
                        Sbf = [sb([128, 128], BF16, "Sbf_%d" % i) for i in range(2)]
                        rr_ = [sb([128, 128], BF16, "r_%d" % i) for i in range(2)]
                        vn_ = [sb([128, 128], BF16, "vn_%d" % i) for i in range(2)]
                        ssq = sb([128, NT], F32, "ssq")
                        junk = sb([128, 128], F32, "junk")
                        on_bf = [sb([128, 128], BF16, "on_bf") for _ in range(2)]
                        idb = ident_bf.unsqueeze(1).to_broadcast([128, 4, 128])
                        k2c = [0]

                        def bcast_u(ap3):
                            return ap3.to_broadcast([128, 4, 128])

                        def mm4(outb, lT, rh):
                            for u in range(4):
                                S.op("pe", lambda e, outb=outb, lT=lT, rh=rh, u=u: e.matmul(
                                    outb.t[:, u * 128:(u + 1) * 128], lhsT=lT.t[:, u, :], rhs=rh.t[:, u, :], start=True, stop=True),
                                    reads=[lT.b, rh.b], writes=[outb.b])

                        def mkbank(lo):
                            st_ = [0]

                            def f():
                                b_ = banks[lo + st_[0] % 2]
                                st_[0] += 1
                                return b_
                            return f

                        def precompute(h, d_, g4, slot):
                            sfx = "_f" if d_ == 0 else "_b"
                            col = d_ * 8 + h
                            k2 = d_
                            pbank = mkbank(2 * d_)
                            dg, DTt, egr, Gb, DTx = dg2[d_], DTt2[d_], egr2[d_], Gb2[d_], DTx2[d_]
                            tsl = slice(g4 * 4, g4 * 4 + 4)
                            S.op("pool", lambda e: e.tensor_tensor(
                                out=dg.t[:], in0=C32("ident").unsqueeze(1).to_broadcast([128, 4, 128]),
                                in1=bcast_u(gc_all.t[:, tsl, col:col + 1]), op=ALU.mult), reads=[gc_all.b, cf.b], writes=[dg.b])
                            yield
                            bC, bD = pbank(), pbank()
                            for u in range(4):
                                us = slice(u * 128, (u + 1) * 128)
                                S.op("pe", lambda e, u=u, us=us: e.matmul(bC.t[:, us], lhsT=C32("ones"), rhs=dg.t[:, u, :], start=True, stop=False),
                                     reads=[dg.b, cf.b], writes=[bC.b])
                                S.op("pe", lambda e, u=u, us=us: e.matmul(bC.t[:, us], lhsT=dg.t[:, u, :], rhs=C32("negones"), start=False, stop=False),
                                     reads=[dg.b, cf.b], writes=[bC.b])
                                S.op("pe", lambda e, us=us: e.matmul(bC.t[:, us], lhsT=C32("ident"), rhs=C32("nm" + sfx), start=False, stop=True),
                                     reads=[cf.b], writes=[bC.b])
                                S.op("pe", lambda e, u=u, us=us: e.matmul(bD.t[:, us], lhsT=C32("ones"), rhs=dg.t[:, u, :], start=True, stop=True),
                                     reads=[dg.b, cf.b], writes=[bD.b])
                            yield
                            S.op("act", lambda e: e.activation(out=v4(DTt), in_=bC.t[:], func=AF.Exp), reads=[bC.b], writes=[DTt.b])
                            S.op("act", lambda e: e.activation(out=v4(egr), in_=bD.t[:], func=AF.Exp), reads=[bD.b], writes=[egr.b])
                            yield
                            bG, bKQ = pbank(), pbank()
                            for u in range(4):
                                t = g4 * 4 + u
                                ksl = kT_s.t[:, t * 128:(t + 1) * 128]
                                qsl = qT_s.t[:, t * 128:(t + 1) * 128]
                                us = slice(u * 128, (u + 1) * 128)
                                S.op("pe", lambda e, ksl=ksl, us=us: e.matmul(bG.t[:, us], lhsT=ksl, rhs=ksl, start=True, stop=True),
                                     reads=[kT_s.b], writes=[bG.b])
                                S.op("pe", lambda e, ksl=ksl, qsl=qsl, us=us: e.matmul(bKQ.t[:, us], lhsT=ksl, rhs=qsl, start=True, stop=True),
                                     reads=[kT_s.b, qT_s.b], writes=[bKQ.b])
                            yield
                            S.op("dve", lambda e: e.tensor_tensor(
                                out=Gb.t[:], in0=q4(bG.t[:]), in1=bcast_u(beta_all.t[:, tsl, col:col + 1]), op=ALU.mult),
                                reads=[bG.b, beta_all.b], writes=[Gb.b])
                            KD = kd[d_][slot]
                            S.op("pool", lambda e: e.tensor_tensor(
                                out=KD.t[:], in0=k_tok.t[:, tsl, :], in1=bcast_u(erest_all.t[:, tsl, col:col + 1]), op=ALU.mult),
                                reads=[k_tok.b, erest_all.b], writes=[KD.b])
                            yield
                            AT = ATt[d_][slot]
                            S.op("dve", lambda e: e.tensor_tensor(out=v4(AT), in0=bKQ.t[:], in1=v4(DTt), op=ALU.mult),
                                 reads=[bKQ.b, DTt.b], writes=[AT.b])
                            QG = qgT[d_][slot]
                            S.op("dve", lambda e: e.tensor_tensor(out=v4(QG), in0=qT_s.t[:, g4 * 512:(g4 + 1) * 512], in1=v4(egr), op=ALU.mult),
                                 reads=[qT_s.b, egr.b], writes=[QG.b])
                            for xi, cmn in enumerate(("cma", "cmb", "cmc")):
                                S.op("pool", lambda e, xi=xi, cmn=cmn: e.tensor_tensor(
                                    out=DTx[xi].t[:], in0=DTt.t[:], in1=C32(cmn + sfx).unsqueeze(1).to_broadcast([128, 4, 128]), op=ALU.mult),
                                    reads=[DTt.b, cf.b], writes=[DTx[xi].b])
                            yield
                            for xi in range(3):
                                eng = "dve" if xi == 0 else "pool"
                                S.op(eng, lambda e, xi=xi: e.tensor_tensor(out=MTx[xi][k2].t[:], in0=Gb.t[:], in1=DTx[xi].t[:], op=ALU.mult),
                                     reads=[Gb.b, DTx[xi].b], writes=[MTx[xi][k2].b])
                            yield
                            tb = []
                            bkA, bkB = pbank(), pbank()
                            for xi in range(3):
                                bk = bkA if xi < 2 else bkB
                                off = 512 if xi == 1 else 0
                                bkv = bk.t[:].bitcast(BF16)
                                tb.append((bk, bkv, off))
                                for u in range(4):
                                    S.op("pe", lambda e, bkv=bkv, u=u, xi=xi, off=off: e.transpose(
                                        out=bkv[:, off + u * 128:off + (u + 1) * 128], in_=MTx[xi][k2].t[:, u, :], identity=ident_bf),
                                        reads=[MTx[xi][k2].b, cb.b], writes=[bk.b])
                            yield
                            for xi in range(3):
                                bk, bkv, off = tb[xi]
                                S.op("act", lambda e, bkv=bkv, xi=xi, off=off: e.activation(out=v4(Mx[xi][k2]), in_=bkv[:, off:off + 512], func=AF.Copy),
                                     reads=[bk.b], writes=[Mx[xi][k2].b])
                            yield
                            R_, RT_ = Rr[k2], RTt[k2]
                            S.op("pool", lambda e: e.tensor_tensor(out=R_.t[:], in0=Mx[0][k2].t[:], in1=idb, op=ALU.add),
                                 reads=[Mx[0][k2].b, cb.b], writes=[R_.b])
                            S.op("pool", lambda e: e.tensor_tensor(out=RT_.t[:], in0=MTx[0][k2].t[:], in1=idb, op=ALU.add),
                                 reads=[MTx[0][k2].b, cb.b], writes=[RT_.b])
                            Pc, PTc = Mx[0][k2], MTx[0][k2]
                            for kk in range(4):
                                Pn, PTn = Pk[kk % 2][k2], PTk[kk % 2][k2]
                                b1 = pbank()
                                mm4(b1, PTc, Pc)
                                if kk < 3:
                                    b2 = pbank()
                                    mm4(b2, Pc, PTc)
                                yield
                                S.op("act", lambda e, b1=b1, Pn=Pn: e.activation(out=v4(Pn), in_=b1.t[:], func=AF.Copy), reads=[b1.b], writes=[Pn.b])
                                if kk < 3:
                                    S.op("act", lambda e, b2=b2, PTn=PTn: e.activation(out=v4(PTn), in_=b2.t[:], func=AF.Copy), reads=[b2.b], writes=[PTn.b])
                                yield
                                b3, b4 = pbank(), pbank()
                                mm4(b3, RT_, Pn)
                                mm4(b4, Pn, RT_)
                                yield
                                S.op("dve", lambda e, b3=b3: e.tensor_tensor(out=v4(R_), in0=b3.t[:], in1=v4(R_), op=ALU.add),
                                     reads=[b3.b, R_.b], writes=[R_.b])
                                S.op("dve", lambda e, b4=b4: e.tensor_tensor(out=v4(RT_), in0=b4.t[:], in1=v4(RT_), op=ALU.add),
                                     reads=[b4.b, RT_.b], writes=[RT_.b])
                                yield
                                Pc, PTc = Pn, PTn
                            b1 = pbank()
                            mm4(b1, Mx[1][k2], RT_)
                            b2 = pbank()
                            mm4(b2, MTx[1][k2], R_)
                            yield
                            Z1, Z2 = Zz[k2], Pk[0][k2]
                            S.op("act", lambda e: e.activation(out=v4(Z1), in_=b1.t[:], func=AF.Copy), reads=[b1.b], writes=[Z1.b])
                            S.op("act", lambda e: e.activation(out=v4(Z2), in_=b2.t[:], func=AF.Copy), reads=[b2.b], writes=[Z2.b])
                            yield
                            b3, b4 = pbank(), pbank()
                            mm4(b3, R_, Z1)
                            mm4(b4, RT_, Z2)
                            yield
                            S.op("dve", lambda e: e.tensor_tensor(out=v4(RT_), in0=v4(RT_), in1=b3.t[:], op=ALU.subtract),
                                 reads=[b3.b, RT_.b], writes=[RT_.b])
                            S.op("dve", lambda e: e.tensor_tensor(out=v4(R_), in0=v4(R_), in1=b4.t[:], op=ALU.subtract),
                                 reads=[b4.b, R_.b], writes=[R_.b])
                            yield
                            b5 = pbank()
                            mm4(b5, Mx[2][k2], RT_)
                            yield
                            S.op("act", lambda e: e.activation(out=v4(Z1), in_=b5.t[:], func=AF.Copy), reads=[b5.b], writes=[Z1.b])
                            yield
                            b6 = pbank()
                            mm4(b6, R_, Z1)
                            yield
                            NTg = NTt[d_][slot]
                            S.op("dve", lambda e: e.tensor_tensor(out=v4(NTg), in0=v4(RT_), in1=b6.t[:], op=ALU.subtract),
                                 reads=[b6.b, RT_.b], writes=[NTg.b])

                        def rec_group(h, d_, tiles, slot, o_written):
                            col = d_ * 8 + h
                            rbank = mkbank(4 + 2 * d_)
                            for t in tiles:
                                u = t % 4
                                ksl = kT_s.t[:, t * 128:(t + 1) * 128]
                                bk1 = rbank()
                                S.op("pe", lambda e, ksl=ksl, bk1=bk1: e.matmul(bk1.t[:, 0:128], lhsT=ksl, rhs=Sbf[d_].t[:], start=True, stop=True),
                                     reads=[kT_s.b, Sbf[d_].b], writes=[bk1.b])
                                yield
                                S.op("dve", lambda e, bk1=bk1, t=t: e.scalar_tensor_tensor(
                                    out=rr_[d_].t[:], in0=bk1.t[:, 0:128], scalar=negegc_all.t[:, t, col:col + 1], in1=v_tok.t[:, t, :], op0=ALU.mult, op1=ALU.add),
                                    reads=[bk1.b, negegc_all.b, v_tok.b], writes=[rr_[d_].b])
                                yield
                                bk2 = rbank()
                                S.op("pe", lambda e, bk2=bk2, u=u: e.matmul(bk2.t[:, 0:128], lhsT=NTt[d_][slot].t[:, u, :], rhs=rr_[d_].t[:], start=True, stop=True),
                                     reads=[NTt[d_][slot].b, rr_[d_].b], writes=[bk2.b])
                                yield
                                S.op("act", lambda e, bk2=bk2, t=t: e.activation(out=vn_[d_].t[:], in_=bk2.t[:, 0:128], func=AF.Identity, scale=beta_all.t[:, t, col:col + 1]),
                                     reads=[bk2.b, beta_all.b], writes=[vn_[d_].b])
                                yield
                                bk4 = rbank()
                                S.op("pe", lambda e, bk4=bk4, u=u: e.matmul(bk4.t[:, 0:128], lhsT=kd[d_][slot].t[:, u, :], rhs=vn_[d_].t[:], start=True, stop=True),
                                     reads=[kd[d_][slot].b, vn_[d_].b], writes=[bk4.b])
                                bk3 = bk4
                                S.op("pe", lambda e, bk3=bk3, u=u: e.matmul(bk3.t[:, 128:256], lhsT=qgT[d_][slot].t[:, u, :], rhs=Sbf[d_].t[:], start=True, stop=False),
                                     reads=[qgT[d_][slot].b, Sbf[d_].b], writes=[bk3.b])
                                S.op("pe", lambda e, bk3=bk3, u=u: e.matmul(bk3.t[:, 128:256], lhsT=ATt[d_][slot].t[:, u, :], rhs=vn_[d_].t[:], start=False, stop=True),
                                     reads=[ATt[d_][slot].b, vn_[d_].b], writes=[bk3.b])
                                yield
                                S.op("dve", lambda e, bk4=bk4, t=t: e.scalar_tensor_tensor(
                                    out=S32[d_].t[:], in0=S32[d_].t[:], scalar=egl_all.t[:, t, col:col + 1], in1=bk4.t[:, 0:128], op0=ALU.mult, op1=ALU.add),
                                    reads=[bk4.b, egl_all.b, S32[d_].b], writes=[S32[d_].b])
                                yield
                                S.op("act", lambda e: e.activation(out=Sbf[d_].t[:], in_=S32[d_].t[:], func=AF.Copy), reads=[S32[d_].b], writes=[Sbf[d_].b])
                                if t not in o_written:
                                    o_written.add(t)
                                    S.op("dve", lambda e, bk3=bk3, t=t: e.tensor_copy(out=o_all.t[:, t, :], in_=bk3.t[:, 128:256]), reads=[bk3.b], writes=[o_all.b])
                                else:
                                    S.op("dve", lambda e, bk3=bk3, t=t: e.tensor_tensor(out=o_all.t[:, t, :], in0=bk3.t[:, 128:256], in1=o_all.t[:, t, :], op=ALU.add),
                                         reads=[bk3.b, o_all.b], writes=[o_all.b])
                                yield

                        def lockstep(gens):
                            gens = list(gens)
                            while gens:
                                nxt = []
                                for g_ in gens:
                                    try:
                                        next(g_)
                                        nxt.append(g_)
                                    except StopIteration:
                                        pass
                                gens = nxt

                        for h in range(GH):
                            li = 0
                            for i3 in range(3):
                                chn = i3 * 8 + h
                                wt_i = chn // 4
                                cw = convw.t[:, chn, :]
                                for b in range(NB):
                                    tok = slice(b * 512, (b + 1) * 512)
                                    pr = praw[i3][b % 2]
                                    rd = [bpq[wt_i][b], bpq_halo]
                                    if b > 0:
                                        rd.append(bpq[wt_i][b - 1])
                                    if b < NB - 1:
                                        rd.append(bpq[wt_i][b + 1])
                                    S.load("sp", pr, pr.t[:], pqkv[chn * 128:(chn + 1) * 128, b * 512:b * 512 + 516], reads=rd)
                                    ac = acc_c[li % 2]
                                    li += 1
                                    S.op("dve", lambda e, pr=pr, cw=cw, ac=ac: e.tensor_scalar(
                                        out=ac.t[:], in0=pr.t[:, 0:512], scalar1=cw[:, 0:1], scalar2=None, op0=ALU.mult),
                                        reads=[pr.b, convw.b], writes=[ac.b])
                                    for j in range(1, 5):
                                        S.op("dve", lambda e, pr=pr, cw=cw, ac=ac, j=j: e.scalar_tensor_tensor(
                                            out=ac.t[:], in0=pr.t[:, j:j + 512], scalar=cw[:, j:j + 1], in1=ac.t[:],
                                            op0=ALU.mult, op1=ALU.add), reads=[pr.b, convw.b, ac.b], writes=[ac.b])
                                    if i3 == 2:
                                        vb = vT_b[b % 2]
                                        S.op("act", lambda e, ac=ac, vb=vb: e.activation(out=vb.t[:], in_=ac.t[:], func=AF.Silu), reads=[ac.b], writes=[vb.b])
                                        bk = bank()
                                        bkv = bk.t[:].bitcast(BF16)
                                        for u in range(4):
                                            S.op("pe", lambda e, bkv=bkv, u=u, vb=vb: e.transpose(out=bkv[:, u * 128:(u + 1) * 128], in_=vb.t[:, u * 128:(u + 1) * 128], identity=ident_bf),
                                                 reads=[vb.b, cb.b], writes=[bk.b])
                                        S.op("act", lambda e, bkv=bkv, b=b: e.activation(out=v_tok.t[:, b * 4:(b + 1) * 4, :], in_=q4(bkv[:, 0:512]), func=AF.Copy),
                                             reads=[bk.b], writes=[v_tok.b])
                                    else:
                                        dstT = qT_s if i3 == 0 else kT_s
                                        S.op("act", lambda e, ac=ac: e.activation(out=ac.t[:], in_=ac.t[:], func=AF.Silu), reads=[ac.b], writes=[ac.b])
                                        sh_ = sqh[b % 2]
                                        rsx = rs_h[b % 2]
                                        S.op("act", lambda e, ac=ac, sh_=sh_: e.activation(out=sh_.t[:], in_=ac.t[:], func=AF.Square), reads=[ac.b], writes=[sh_.b])
                                        bk2 = bank()
                                        S.op("pe", lambda e, bk2=bk2, sh_=sh_: e.matmul(bk2.t[:], lhsT=ones_bf, rhs=sh_.t[:], start=True, stop=True),
                                             reads=[sh_.b, cb.b], writes=[bk2.b])
                                        S.op("act", lambda e, bk2=bk2, rsx=rsx: e.activation(out=rsx.t[:], in_=bk2.t[:], func=AF.Sqrt, bias=epsc.t[:], scale=1.0),
                                             reads=[bk2.b, epsc.b], writes=[rsx.b])
                                        S.op("dve", lambda e, rsx=rsx: e.reciprocal(out=rsx.t[:], in_=rsx.t[:]), reads=[rsx.b], writes=[rsx.b])
                                        scq = (128.0 ** -0.5) if i3 == 0 else 1.0
                                        S.op("dve", lambda e, tok=tok, rsx=rsx, ac=ac, scq=scq, dstT=dstT: e.scalar_tensor_tensor(
                                            out=dstT.t[:, tok], in0=ac.t[:], scalar=scq, in1=rsx.t[:], op0=ALU.mult, op1=ALU.mult),
                                            reads=[ac.b, rsx.b], writes=[dstT.b])
                                        if i3 == 1:
                                            bk = bank()
                                            bkv = bk.t[:].bitcast(BF16)
                                            for u in range(4):
                                                t = b * 4 + u
                                                S.op("pe", lambda e, bkv=bkv, u=u, t=t: e.transpose(out=bkv[:, u * 128:(u + 1) * 128], in_=kT_s.t[:, t * 128:(t + 1) * 128], identity=ident_bf),
                                                     reads=[kT_s.b, cb.b], writes=[bk.b])
                                            S.op("act", lambda e, bkv=bkv, b=b: e.activation(out=k_tok.t[:, b * 4:(b + 1) * 4, :], in_=q4(bkv[:, 0:512]), func=AF.Copy),
                                                 reads=[bk.b], writes=[k_tok.b])
                            if h == 0 and l == 0:
                                dump("qT", qT_s, qT_s.t[:], [128, T], BF16)
                                dump("kT", kT_s, kT_s.t[:], [128, T], BF16)
                                dump("v_tok", v_tok, v_tok.t[:], [128, NT, 128], BF16)

                            for d_ in range(2):
                                S.op("pool", lambda e, d_=d_: e.memset(S32[d_].t[:], 0.0), writes=[S32[d_].b])
                                S.op("pool", lambda e, d_=d_: e.memset(Sbf[d_].t[:], 0.0), writes=[Sbf[d_].b])
                            o_written = set()
                            lockstep([precompute(h, 0, 0, 0), precompute(h, 1, NG - 1, 0)])
                            if h == 0 and l == 0:
                                dump("NT_f", NTt[0][0], NTt[0][0].t[:], [128, 4, 128], BF16)
                                dump("AT_f", ATt[0][0], ATt[0][0].t[:], [128, 4, 128], BF16)
                                dump("NT_b", NTt[1][0], NTt[1][0].t[:], [128, 4, 128], BF16)
                            for gi in range(NG):
                                gens = []
                                if gi + 1 < NG:
                                    gens.append(precompute(h, 0, gi + 1, (gi + 1) % RING))
                                    gens.append(precompute(h, 1, NG - 2 - gi, (gi + 1) % RING))
                                gens.append(rec_group(h, 0, [gi * 4 + s_ for s_ in range(4)], gi % RING, o_written))
                                gens.append(rec_group(h, 1, [(NG - 1 - gi) * 4 + 3 - s_ for s_ in range(4)], gi % RING, o_written))
                                lockstep(gens)
                            if h == 0 and l == 0:
                                dump("o_all", o_all, o_all.t[:], [128, NT, 128])
                            for t in range(NT):
                                S.op("act", lambda e, t=t: e.activation(out=junk.t[:], in_=o_all.t[:, t, :], func=AF.Square, accum_out=ssq.t[:, t:t + 1]),
                                     reads=[o_all.b], writes=[junk.b, ssq.b])
                            S.op("act", lambda e: e.activation(out=ssq.t[:], in_=ssq.t[:], func=AF.Sqrt, bias=epsc.t[:], scale=1.0 / 128), reads=[ssq.b, epsc.b], writes=[ssq.b])
                            S.op("dve", lambda e: e.reciprocal(out=ssq.t[:], in_=ssq.t[:]), reads=[ssq.b], writes=[ssq.b])
                            for g4 in range(NG):
                                sz = szT[g4 % 2]
                                yt = yT[g4 % 2]
                                S.load("sp", sz, sz.t[:], szs[h * 128:(h + 1) * 128, g4 * 512:(g4 + 1) * 512], reads=[bsz[h // 4][g4]])
                                bk = bank()
                                bkv = bk.t[:].bitcast(BF16)
                                for u in range(4):
                                    t = g4 * 4 + u
                                    ob = on_bf[t % 2]
                                    S.op("dve", lambda e, t=t, ob=ob: e.tensor_scalar(out=ob.t[:], in0=o_all.t[:, t, :], scalar1=ssq.t[:, t:t + 1], scalar2=None, op0=ALU.mult),
                                         reads=[o_all.b, ssq.b], writes=[ob.b])
                                    S.op("pe", lambda e, bkv=bkv, u=u, ob=ob: e.transpose(out=bkv[:, u * 128:(u + 1) * 128], in_=ob.t[:], identity=ident_bf),
                                         reads=[ob.b, cb.b], writes=[bk.b])
                                S.op("dve", lambda e, bkv=bkv, sz=sz, yt=yt: e.scalar_tensor_tensor(
                                    out=yt.t[:], in0=bkv[:, 0:512], scalar=gnw_c.t[:, 0:1], in1=sz.t[:], op0=ALU.mult, op1=ALU.mult),
                                    reads=[bk.b, gnw_c.b, sz.b], writes=[yt.b])
                                S.store("sp", mixs[h * 128:(h + 1) * 128, g4 * 512:(g4 + 1) * 512], yt, yt.t[:], writes=[bmix_g[h]])
                        S.flush()

                with ExitStack() as sc:
                    sb = mk_sb(sc)
                    cm = make_common(sb, 3)
                    xt = sb([128, DC, 512], F32, "xT")
                    hTs = [sb([128, DC, 512], BF16, "hT")] * 2
                    actT = sb([128, FC, 512], BF16, "actT")
                    last = (l == depth - 1)
                    if last:
                        xtok = [sb([128, D], F32, "xtok") for _ in range(2)]
                    for b in range(NB):
                        tok = slice(b * 512, (b + 1) * 512)
                        mt = hTs[0]
                        h2 = hTs[1]
                        S.load("sp", xt, xt.t[:], xres_v[:, :, tok], reads=[bxres[b]])
                        S.load("sp", mt, mt.t[:], mix_v[:, :, tok], reads=bmix_g + [bmix_a[0][b], bmix_a[1][b]])
                        if b == 0 and l == 0:
                            dump("mixT", mt, mt.t[:], [128, DC, 512], BF16)
                        for wi in range(4):
                            wt = load_w(cm, w_out[l, :, wi * 512:(wi + 1) * 512])
                            for ch in range(4):
                                j = wi * 4 + ch
                                bk = bank()
                                for c in range(DC):
                                    S.op("pe", lambda e, bk=bk, wt=wt, c=c, ch=ch, mt=mt: e.matmul(
                                        bk.t[:], lhsT=wt.t[:, c, ch * 128:(ch + 1) * 128], rhs=mt.t[:, c, :], start=(c == 0), stop=(c == DC - 1)),
                                        reads=[wt.b, mt.b], writes=[bk.b])
                                S.op("dve", lambda e, bk=bk, j=j: e.scalar_tensor_tensor(
                                    out=xt.t[:, j, :], in0=bk.t[:], scalar=gt1[:, j:j + 1], in1=xt.t[:, j, :], op0=ALU.mult, op1=ALU.add),
                                    reads=[bk.b, xt.b, modT[l].b], writes=[xt.b])
                        norm_block(cm, xt, A2[l].t, sh2, h2)
                        for wi in range(16):
                            wt = load_w(cm, w_up[l, :, wi * 512:(wi + 1) * 512])
                            for ch in range(4):
                                f = wi * 4 + ch
                                bk = bank()
                                for c in range(DC):
                                    S.op("pe", lambda e, bk=bk, wt=wt, c=c, ch=ch: e.matmul(
                                        bk.t[:], lhsT=wt.t[:, c, ch * 128:(ch + 1) * 128], rhs=h2.t[:, c, :], start=(c == 0), stop=(c == DC - 1)),
                                        reads=[wt.b, h2.b], writes=[bk.b])
                                r1 = tmp(cm)
                                S.op("act", lambda e, bk=bk, r1=r1: e.activation(out=r1.t[:], in_=bk.t[:], func=AF.Relu), reads=[bk.b], writes=[r1.b])
                                S.op("dve", lambda e, r1=r1, f=f: e.tensor_tensor(out=actT.t[:, f, :], in0=r1.t[:], in1=r1.t[:], op=ALU.mult),
                                     reads=[r1.b], writes=[actT.b])
                        for jg in range(4):
                            bks = [bank() for _ in range(4)]
                            for fq in range(4):
                                wt = load_w(cm, w_down[l, fq * 2048:(fq + 1) * 2048, jg * 512:(jg + 1) * 512])
                                for ch in range(4):
                                    for fc in range(16):
                                        f = fq * 16 + fc
                                        S.op("pe", lambda e, bk=bks[ch], wt=wt, fc=fc, ch=ch, f=f: e.matmul(
                                            bk.t[:], lhsT=wt.t[:, fc, ch * 128:(ch + 1) * 128], rhs=actT.t[:, f, :], start=(f == 0), stop=(f == FC - 1)),
                                            reads=[wt.b, actT.b], writes=[bks[ch].b])
                            for ch in range(4):
                                j = jg * 4 + ch
                                S.op("dve", lambda e, bk=bks[ch], j=j: e.scalar_tensor_tensor(
                                    out=xt.t[:, j, :], in0=bk.t[:], scalar=gt2[:, j:j + 1], in1=xt.t[:, j, :], op0=ALU.mult, op1=ALU.add),
                                    reads=[bks[ch].b, xt.b, modT[l].b], writes=[xt.b])
                        if not last:
                            S.store("sp", xres_v[:, :, tok], xt, xt.t[:], writes=[bxres[b]])
                        else:
                            for tt in range(4):
                                ti = b * 4 + tt
                                xk = xtok[ti % 2]
                                for cg in range(4):
                                    bk = bank()
                                    for cc in range(4):
                                        c = cg * 4 + cc
                                        S.op("pe", lambda e, bk=bk, c=c, cc=cc, tt=tt: e.transpose(
                                            out=bk.t[:, cc * 128:(cc + 1) * 128], in_=xt.t[:, c, tt * 128:(tt + 1) * 128], identity=C32("ident")),
                                            reads=[xt.b, cf.b], writes=[bk.b])
                                    if cg % 2 == 0:
                                        S.op("act", lambda e, bk=bk, xk=xk, cg=cg: e.activation(out=xk.t[:, cg * 512:(cg + 1) * 512], in_=bk.t[:], func=AF.Copy),
                                             reads=[bk.b], writes=[xk.b])
                                    else:
                                        S.op("dve", lambda e, bk=bk, xk=xk, cg=cg: e.tensor_copy(out=xk.t[:, cg * 512:(cg + 1) * 512], in_=bk.t[:]),
                                             reads=[bk.b], writes=[xk.b])
                                S.store("sp", y_out[ti * 128:(ti + 1) * 128, :], xk, xk.t[:])
                    S.flush()
    except _Stop:
        pass
    return nc, dbg_out


_PROG_CACHE = {}


def _core_inputs(x_seq, c_vec, T, shared):
    tv = x_seq.shape[0]
    xp = np.zeros((T, D), np.float32)
    xp[:tv] = x_seq
    mask = np.zeros((T,), np.float32)
    mask[:tv] = 1.0
    m = dict(shared)
    m["x"] = xp
    m["c"] = np.ascontiguousarray(c_vec, dtype=np.float32)
    m["mask_row"] = mask.reshape(1, T).copy()
    m["mask_col"] = np.ascontiguousarray(mask.reshape(T // 128, 128).T)
    return m


def run_trunk(seqs, cvecs, weights, T, depth, dbg=(), n_cores=None):
    key = (T, depth, tuple(dbg))
    if key not in _PROG_CACHE:
        _PROG_CACHE[key] = build_program(T, depth, dbg)
    nc, dbg_out = _PROG_CACHE[key]
    names32, cf_np, cb_np, cosT, sinT = make_consts(T)
    shared = {
        "ada_w": weights["ada_w"], "ada_b": weights["ada_b"], "norm1_w": weights["norm1_w"], "norm2_w": weights["norm2_w"],
        "w_in": weights["w_in"], "conv_w": weights["conv_w"],
        "a_log": np.ascontiguousarray(weights["a_log"].reshape(depth, 16)),
        "dt_bias": np.ascontiguousarray(weights["dt_bias"].reshape(depth, 16)),
        "gdn_norm_w": weights["gdn_norm_w"], "q_norm_w": weights["q_norm_w"], "k_norm_w": weights["k_norm_w"],
        "w_out": weights["w_out"], "w_up": weights["w_up"], "w_down": weights["w_down"],
        "cf": cf_np, "cb": cb_np, "cosT": cosT, "sinT": sinT,
    }
    shared = {k: np.ascontiguousarray(np.asarray(v, dtype=np.float32)) for k, v in shared.items()}
    in_maps = [_core_inputs(s, c, T, shared) for s, c in zip(seqs, cvecs)]
    res = run_bass_kernel_spmd(nc, in_maps, core_ids=list(range(len(in_maps))))
    return res


def kernel(x_prompt, x_sample, c_prompt, c_sample, ada_w, ada_b, norm1_w, norm2_w, w_in, conv_w, a_log, dt_bias,
           gdn_norm_w, q_norm_w, k_norm_w, w_out, w_up, w_down):
    x_prompt = np.asarray(x_prompt, np.float32)
    x_sample = np.asarray(x_sample, np.float32)
    c_prompt = np.asarray(c_prompt, np.float32)
    c_sample = np.asarray(c_sample, np.float32)
    depth = int(np.asarray(ada_w).shape[0])
    T = x_prompt.shape[1]
    weights = dict(ada_w=ada_w, ada_b=ada_b, norm1_w=norm1_w, norm2_w=norm2_w, w_in=w_in, conv_w=conv_w, a_log=np.asarray(a_log),
                   dt_bias=np.asarray(dt_bias), gdn_norm_w=gdn_norm_w, q_norm_w=q_norm_w, k_norm_w=k_norm_w, w_out=w_out, w_up=w_up,
                   w_down=w_down)
    seqs = [x_prompt[i] for i in range(4)] + [x_sample[i] for i in range(4)]
    cvecs = [c_prompt[i] for i in range(4)] + [c_sample[i] for i in range(4)]
    res = run_trunk(seqs, cvecs, weights, T, depth)
    ys = [r["y"] for r in res.results]
    y_prompt = np.stack([ys[i] for i in range(4)], axis=0).astype(np.float32)
    ts = x_sample.shape[1]
    y_sample = np.stack([ys[4 + i][:ts] for i in range(4)], axis=0).astype(np.float32)
    return (y_prompt, y_sample)
```
